# Optimizing a Trainium2 kernel written in Bass

```python
import jax, jax.numpy as jnp
from jax import lax
import numpy as np

D_MODEL = 1024
BATCH = 8
SEQ = 8192
DEPTH = 2

GRID_W = 64
CTX_LEN = 256
N_MIXERS = 2
N_A_LAYERS = (DEPTH + 1) // 2
N_B_LAYERS = DEPTH // 2
CHUNK = 128
GM_HALF = 3 * D_MODEL
GM_GROUPS = 8
GM_GROUP_DIM = GM_HALF // GM_GROUPS
HG_HEADS = 8
HG_KEY_DIM = D_MODEL // HG_HEADS
HG_VAL_DIM = D_MODEL // HG_HEADS
HG_KEY = HG_HEADS * HG_KEY_DIM
HG_VAL = HG_HEADS * HG_VAL_DIM
HG_PROJ = 3 * HG_KEY + 2 * HG_VAL
SCAN_CHUNK = 64
D_FF = ((8 * D_MODEL // 3 + 255) // 256) * 256
NORM_EPS = 1e-6
POS_BASE = 10000.0

kernel_name = 'hybrid_gmlp_hgrn2_prefix_dit'


def rms_norm(x, w):
    xf = x.astype(jnp.float32)
    xf = xf * lax.rsqrt(jnp.mean(xf * xf, axis=-1, keepdims=True) + NORM_EPS)
    return (xf * w.astype(jnp.float32)).astype(x.dtype)


def layer_norm(x, g, b):
    xf = x.astype(jnp.float32)
    mu = jnp.mean(xf, axis=-1, keepdims=True)
    xc = xf - mu
    var = jnp.mean(xc * xc, axis=-1, keepdims=True)
    return (xc * lax.rsqrt(var + NORM_EPS) * g.astype(jnp.float32) + b.astype(jnp.float32)).astype(x.dtype)


def ada_modulate(x, w, shift, scale):
    return rms_norm(x, w) * (1.0 + scale) + shift


def sincos(pos, dim):
    half = dim // 2
    omega = 1.0 / (POS_BASE ** (jnp.arange(half, dtype=jnp.float32) / half))
    ang = pos.astype(jnp.float32)[:, None] * omega[None, :]
    return jnp.concatenate([jnp.sin(ang), jnp.cos(ang)], axis=-1)


def grid_pos_code(n):
    rows = n // GRID_W
    half = D_MODEL // 2
    row_code = sincos(jnp.arange(rows), half)
    col_code = sincos(jnp.arange(GRID_W), half)
    code = jnp.concatenate([
        jnp.broadcast_to(row_code[:, None, :], (rows, GRID_W, half)),
        jnp.broadcast_to(col_code[None, :, :], (rows, GRID_W, half))], axis=-1)
    return code.reshape(rows * GRID_W, D_MODEL)


def chunk_gmlp(h, w_in, b_in, ln_g, ln_b, w_s, b_s, w_out):
    bsz, n, _ = h.shape
    z = jax.nn.gelu(h @ w_in + b_in, approximate=False)
    u, v = jnp.split(z, 2, axis=-1)
    v = layer_norm(v, ln_g, ln_b)
    v = v.reshape(bsz, n // CHUNK, CHUNK, GM_GROUPS, GM_GROUP_DIM)
    v = jnp.einsum('gpq,bcqgd->bcpgd', w_s, v) + b_s.T[:, :, None]
    v = v.reshape(bsz, n, GM_HALF)
    return (u * v) @ w_out


def gla_chunked(q, k, v, log_f, s0):
    bsz, n, h, _ = q.shape
    dv = v.shape[-1]
    nc = n // SCAN_CHUNK

    def blocks(t):
        return t.astype(jnp.float32).reshape(bsz, nc, SCAN_CHUNK, h, t.shape[-1])

    q, k, v, log_f = blocks(q), blocks(k), blocks(v), blocks(log_f)
    b = jnp.cumsum(log_f, axis=2)
    b_last = b[:, :, -1:]
    q_dec = q * jnp.exp(b)
    k_inv = k * jnp.exp(-b)
    k_dec = k * jnp.exp(b_last - b)
    lower = jnp.tril(jnp.ones((SCAN_CHUNK, SCAN_CHUNK), dtype=bool))
    att = jnp.einsum('bcthk,bcshk->bchts', q_dec, k_inv)
    att = jnp.where(lower, att, 0.0)
    o_intra = jnp.einsum('bchts,bcshv->bcthv', att, v)

    def step(state, xs):
        q_c, k_c, v_c, d_c = xs
        o_c = jnp.einsum('bthk,bhkv->bthv', q_c, state)
        state = d_c[..., None] * state + jnp.einsum('bshk,bshv->bhkv', k_c, v_c)
        return state, o_c

    xs = (jnp.moveaxis(q_dec, 1, 0), jnp.moveaxis(k_dec, 1, 0), jnp.moveaxis(v, 1, 0),
          jnp.moveaxis(jnp.exp(b_last[:, :, 0]), 1, 0))
    s_final, o_inter = lax.scan(step, s0, xs)
    o = o_intra + jnp.moveaxis(o_inter, 0, 1)
    return o.reshape(bsz, n, h, dv), s_final


def scan_final_state(k, v, log_f):
    b = jnp.cumsum(log_f.astype(jnp.float32), axis=1)
    k_dec = k.astype(jnp.float32) * jnp.exp(b[:, -1:] - b)
    return jnp.einsum('bshk,bshv->bhkv', k_dec, v.astype(jnp.float32))


def hg_forget(z, lb):
    zf = z.astype(jnp.float32)
    log_f = jnp.logaddexp(jnp.log(lb), jnp.log1p(-lb) + jax.nn.log_sigmoid(zf))
    key = (1.0 - lb) * jax.nn.sigmoid(-zf)
    return key, log_f


def hgrn2_mixer(h_lat, h_ctx, w_in, lower_bound, norm_w, w_out, ctx_out):
    bsz = h_lat.shape[0]
    lbs = lower_bound.astype(jnp.float32).reshape(2, HG_HEADS, HG_KEY_DIM)
    cuts = [HG_KEY, 2 * HG_KEY, 3 * HG_KEY, 3 * HG_KEY + HG_VAL]

    def heads(t, d):
        return t.reshape(t.shape[0], t.shape[1], HG_HEADS, d)

    def rev(t):
        return jnp.flip(t, axis=1)

    def readout(o, g):
        o = rms_norm(o, norm_w.reshape(HG_HEADS, HG_VAL_DIM))
        o = o.reshape(o.shape[0], o.shape[1], HG_VAL).astype(g.dtype) * jax.nn.silu(g)
        return o @ w_out

    q_x, ff_x, fb_x, i_x, g_x = jnp.split(h_lat @ w_in, cuts, axis=-1)
    q_x = heads(jax.nn.silu(q_x), HG_KEY_DIM)
    i_x = heads(i_x, HG_VAL_DIM)
    kf_x, lf_x = hg_forget(heads(ff_x, HG_KEY_DIM), lbs[0])
    kb_x, lbk_x = hg_forget(heads(fb_x, HG_KEY_DIM), lbs[1])

    if ctx_out:
        q_c, ff_c, fb_c, i_c, g_c = jnp.split(h_ctx @ w_in, cuts, axis=-1)
        q_c = heads(jax.nn.silu(q_c), HG_KEY_DIM)
        i_c = heads(i_c, HG_VAL_DIM)
        kf_c, lf_c = hg_forget(heads(ff_c, HG_KEY_DIM), lbs[0])
        kb_c, lbk_c = hg_forget(heads(fb_c, HG_KEY_DIM), lbs[1])
        zero = jnp.zeros((bsz, HG_HEADS, HG_KEY_DIM, HG_VAL_DIM), jnp.float32)
        o_cf, s_f = gla_chunked(q_c, kf_c, i_c, lf_c, zero)
        o_cb, s_b = gla_chunked(rev(q_c), rev(kb_c), rev(i_c), rev(lbk_c), zero)
        y_ctx = readout(o_cf + rev(o_cb), g_c)
    else:
        ff_c, fb_c, i_c = jnp.split(h_ctx @ w_in[:, HG_KEY:3 * HG_KEY + HG_VAL], [HG_KEY, 2 * HG_KEY], axis=-1)
        i_c = heads(i_c, HG_VAL_DIM)
        kf_c, lf_c = hg_forget(heads(ff_c, HG_KEY_DIM), lbs[0])
        kb_c, lbk_c = hg_forget(heads(fb_c, HG_KEY_DIM), lbs[1])
        s_f = scan_final_state(kf_c, i_c, lf_c)
        s_b = scan_final_state(rev(kb_c), rev(i_c), rev(lbk_c))
        y_ctx = None

    o_f, _ = gla_chunked(q_x, kf_x, i_x, lf_x, s_f)
    o_b, _ = gla_chunked(rev(q_x), rev(kb_x), rev(i_x), rev(lbk_x), s_b)
    y_lat = readout(o_f + rev(o_b), g_x)
    return y_lat, y_ctx


def swiglu(h, w_in, w_out):
    a, b = jnp.split(h @ w_in, 2, axis=-1)
    return (jax.nn.silu(a) * b) @ w_out


def setup_inputs(seed: int = 0) -> dict:
    key = jax.random.key(seed)
    ks = iter(jax.random.split(key, 22))
    D = D_MODEL

    def nrm(shape, s):
        return jax.random.normal(next(ks), shape, jnp.float32) * s

    return {
        'x': nrm((BATCH, SEQ, D), 1.0),
        'c': nrm((BATCH, D), 1.0),
        'ctx': nrm((BATCH, CTX_LEN, D), 1.0),
        'c_ctx': nrm((D,), 1.0),
        'ada_w': nrm((DEPTH, D, 6 * D), D ** -0.5),
        'ada_b': nrm((DEPTH, 6 * D), 0.01),
        'norm_mix_w': 1.0 + nrm((DEPTH, D), 0.02),
        'norm_ffn_w': 1.0 + nrm((DEPTH, D), 0.02),
        'gm_w_in': nrm((N_A_LAYERS, D, 2 * GM_HALF), D ** -0.5),
        'gm_b_in': nrm((N_A_LAYERS, 2 * GM_HALF), 0.01),
        'gm_ln_g': 1.0 + nrm((N_A_LAYERS, GM_HALF), 0.02),
        'gm_ln_b': nrm((N_A_LAYERS, GM_HALF), 0.01),
        'gm_w_s': nrm((N_A_LAYERS, GM_GROUPS, CHUNK, CHUNK), CHUNK ** -0.5),
        'gm_b_s': 1.0 + nrm((N_A_LAYERS, GM_GROUPS, CHUNK), 0.02),
        'gm_w_out': nrm((N_A_LAYERS, GM_HALF, D), GM_HALF ** -0.5),
        'hg_w_in': nrm((N_B_LAYERS, D, HG_PROJ), D ** -0.5),
        'hg_lb': nrm((DEPTH, 2, HG_KEY), 0.1),
        'hg_norm_w': 1.0 + nrm((N_B_LAYERS, HG_VAL), 0.02),
        'hg_w_out': nrm((N_B_LAYERS, HG_VAL, D), HG_VAL ** -0.5),
        'ffn_w_in': nrm((DEPTH, D, 2 * D_FF), D ** -0.5),
        'ffn_w_out': nrm((DEPTH, D_FF, D), D_FF ** -0.5),
        'final_norm_w': 1.0 + nrm((D,), 0.02),
    }


def reference(x, c, ctx, c_ctx, ada_w, ada_b, norm_mix_w, norm_ffn_w, gm_w_in, gm_b_in, gm_ln_g, gm_ln_b,
              gm_w_s, gm_b_s, gm_w_out, hg_w_in, hg_lb, hg_norm_w, hg_w_out, ffn_w_in, ffn_w_out, final_norm_w):
    n = x.shape[1]
    x = x + grid_pos_code(n).astype(x.dtype)
    p = jax.nn.softmax(hg_lb.astype(jnp.float32), axis=0)
    lower_bounds = jnp.cumsum(p, axis=0) - p[0]
    s_lat = jax.nn.silu(c)
    s_ctx = jax.nn.silu(c_ctx)

    for i in range(DEPTH):
        last = i == DEPTH - 1
        use_a = i % N_MIXERS == 0
        j = i // N_MIXERS
        ctx_needed = (not last) or (not use_a)

        mod_lat = (s_lat @ ada_w[i] + ada_b[i])[:, None, :]
        sh_m, sc_m, gt_m, sh_f, sc_f, gt_f = jnp.split(mod_lat, 6, axis=-1)
        h_lat = ada_modulate(x, norm_mix_w[i], sh_m, sc_m)
        if ctx_needed:
            mod_ctx = s_ctx @ ada_w[i] + ada_b[i]
            csh_m, csc_m, cgt_m, csh_f, csc_f, cgt_f = jnp.split(mod_ctx, 6, axis=-1)
            h_ctx = ada_modulate(ctx, norm_mix_w[i], csh_m, csc_m)

        if use_a:
            gm = (gm_w_in[j], gm_b_in[j], gm_ln_g[j], gm_ln_b[j], gm_w_s[j], gm_b_s[j], gm_w_out[j])
            y_lat = chunk_gmlp(h_lat, *gm)
            y_ctx = None if last else chunk_gmlp(h_ctx, *gm)
        else:
            y_lat, y_ctx = hgrn2_mixer(h_lat, h_ctx, hg_w_in[j], lower_bounds[i], hg_norm_w[j],
                                       hg_w_out[j], not last)

        x = x + gt_m * y_lat
        x = x + gt_f * swiglu(ada_modulate(x, norm_ffn_w[i], sh_f, sc_f), ffn_w_in[i], ffn_w_out[i])
        if not last:
            ctx = ctx + cgt_m * y_ctx
            ctx = ctx + cgt_f * swiglu(ada_modulate(ctx, norm_ffn_w[i], csh_f, csc_f), ffn_w_in[i], ffn_w_out[i])

    return rms_norm(x, final_norm_w)
```

```python
import contextlib
import numpy as np
import concourse.bass as bass
import concourse.mybir as mybir
from concourse.alu_op_type import AluOpType as ALU
from concourse.bass_utils import run_bass_kernel_spmd

F32 = mybir.dt.float32
BF16 = mybir.dt.bfloat16
AF = mybir.ActivationFunctionType
ENGS = ["tensor", "vector", "scalar", "gpsimd", "sync"]

D = 1024
KC = 8
CTX = 256
SEQ = 8192
GRID_W = 64
EPS = 1e-6
GMH = 3072
DFF = 2816
NFC = 22
PE_PARTIAL_CHAIN = True
STRICT_ENGS = {"vector", "scalar", "gpsimd", "sync"}


class Buf:
    __slots__ = ("name", "last_w", "readers", "dcount", "sem")

    def __init__(self, name):
        self.name = name
        self.last_w = None
        self.readers = []
        self.dcount = 0
        self.sem = None


class Prog:
    def __init__(self, nc, es):
        self.nc = nc
        self.es = es
        self.ops = {e: [] for e in ENGS}
        self.nemit = {e: 0 for e in ENGS}
        self.seen = {e: {} for e in ENGS}
        self.esem = {e: es.enter_context(nc.semaphore("e_" + e)) for e in ENGS}
        self.ecount = {e: 0 for e in ENGS}
        self.dbufs = []
        self.allbufs = []
        self.prev_chained = True

    def buf(self, name):
        b = Buf(name)
        self.allbufs.append(b)
        return b

    def _need(self, eng, tok):
        key, val = tok
        if self.seen[eng].get(key, -1) >= val:
            return False
        self.seen[eng][key] = val
        return True

    def op(self, eng, fn, reads=(), writes=(), dma=False, chain=True, nowaw=False):
        idx = len(self.ops[eng])
        deps = []
        for b in reads:
            if b.last_w is not None:
                deps.append(b.last_w)
        for b in writes:
            if b.last_w is not None and not (nowaw and b.last_w[0] == ("d", id(b))):
                deps.append(b.last_w)
            deps.extend(b.readers)
        waits = []
        for tok in deps:
            key, val = tok
            if key == ("e", "tensor") and eng == "tensor" and not dma:
                continue
            if self._need(eng, tok):
                waits.append(tok)
        rec = {"fn": fn, "waits": waits, "inc": False, "dma": None, "chain": chain}
        if dma:
            dst = writes[0]
            if dst.sem is None:
                dst.sem = self.es.enter_context(self.nc.semaphore("d%d_%s" % (len(self.dbufs), dst.name)))
                self.dbufs.append(dst)
            dst.dcount += 16
            mytok = (("d", id(dst)), dst.dcount)
            rec["dma"] = dst
        else:
            mytok = (("e", eng), idx)
        self.ops[eng].append(rec)
        for tok in waits:
            if tok[0][0] == "e":
                self.ops[tok[0][1]][tok[1]]["inc"] = True
        for b in reads:
            b.readers = [t for t in b.readers if t[0] != mytok[0]] + [mytok]
        for b in writes:
            b.last_w = mytok
            b.readers = []
        return mytok

    def barrier(self):
        bars = []
        for e in ENGS:
            b = Buf("bar_" + e)
            rd = list(self.dbufs) if e in ("sync", "gpsimd") else []
            self.op(e, lambda en: en.nop(), reads=rd, writes=[b])
            bars.append(b)
        for e in ENGS:
            self.op(e, lambda en: en.nop(), reads=bars)

    def emit(self):
        nc = self.nc
        sigval = getattr(self, "sigval", {})
        self.sigval = sigval
        for e in ENGS:
            for i in range(self.nemit[e], len(self.ops[e])):
                r = self.ops[e][i]
                if r["inc"] and r["dma"] is None:
                    self.ecount[e] += 1
                    sigval[(e, i)] = self.ecount[e]
        dsem = {id(b): b.sem for b in self.dbufs}
        with nc.Block() as block:
            for e in ENGS:
                lo, hi = self.nemit[e], len(self.ops[e])
                if hi == lo:
                    continue

                def run(en, e=e, lo=lo, hi=hi):
                    for i in range(lo, hi):
                        r = self.ops[e][i]
                        for key, val in r["waits"]:
                            if key[0] == "e":
                                en.wait_ge(self.esem[key[1]], sigval[(key[1], val)])
                            else:
                                en.wait_ge(dsem[key[1]], val)
                        strict = (e in STRICT_ENGS) or (e == "tensor" and PE_PARTIAL_CHAIN and
                                                        (r["chain"] or self.prev_chained))
                        if e == "tensor" and r["inc"] and r["dma"] is None:
                            self.prev_chained = r["chain"]
                        if strict and r["inc"] and r["dma"] is None and sigval[(e, i)] > 1:
                            en.wait_ge(self.esem[e], sigval[(e, i)] - 1)
                        ins = r["fn"](en)
                        if r["dma"] is not None:
                            ins.then_inc(r["dma"].sem, 16)
                        elif r["inc"]:
                            ins.then_inc(self.esem[e], 1)

                getattr(block, e)(run)
                self.nemit[e] = hi


def _col(v):
    v = np.asarray(v, np.float32).reshape(-1, 128)
    return np.ascontiguousarray(v.T)


VEC_LAYOUT = {}
_off = 0
for _n, _w in [("cc", 16), ("ada_b0", 48), ("ada_b1", 48), ("nmw0", 8), ("nmw1", 8), ("nfw0", 8), ("nfw1", 8),
               ("bu", 24), ("lng", 24), ("lb0", 16), ("lb1", 16), ("hnw", 8), ("fnw", 8)]:
    VEC_LAYOUT[_n] = (_off, _w)
    _off += _w
NV = _off


def pos_table(n):
    half = D // 2

    def sincos(pos, dim):
        h = dim // 2
        omega = (1.0 / (10000.0 ** (np.arange(h, dtype=np.float32) / np.float32(h)))).astype(np.float32)
        ang = pos.astype(np.float32)[:, None] * omega[None, :]
        return np.concatenate([np.sin(ang), np.cos(ang)], axis=-1).astype(np.float32)

    rows = n // GRID_W
    rc = sincos(np.arange(rows), half)
    cc = sincos(np.arange(GRID_W), half)
    code = np.concatenate([np.broadcast_to(rc[:, None, :], (rows, GRID_W, half)),
                           np.broadcast_to(cc[None, :, :], (rows, GRID_W, half))], axis=-1)
    return np.ascontiguousarray(code.reshape(rows * GRID_W, D).T.astype(np.float32))


def build(S=SEQ, stop_after=None, dbg_out=None):
    nc = bass.Bass("TRN2", target_bir_lowering=False)
    es = contextlib.ExitStack()
    P = Prog(nc, es)

    def din(name, shape, dt=F32):
        return nc.dram_tensor(name, list(shape), dt, kind="ExternalInput").ap()

    def dscr(name, shape, dt):
        kind = "ExternalOutput" if (dbg_out and name in dbg_out) else "Internal"
        return nc.dram_tensor(name, list(shape), dt, kind=kind).ap()

    xT = din("xT", [D, S])
    ctxT = din("ctxT", [D, CTX])
    posT = din("posT", [D, S])
    vecs = din("vecs", [128, NV])
    bvrow = din("bvrow", [1, GMH])
    lnbrow = din("lnbrow", [1, GMH])
    bsrow = din("bsrow", [1, 1024])
    wsT_d = din("wsT", [128, 1024])
    ada_w = din("ada_w", [2, D, 6 * D])
    gm_w_in = din("gm_w_in", [D, 2 * GMH])
    gm_w_out = din("gm_w_out", [GMH, D])
    hg_w_in = din("hg_w_in", [D, 5 * D])
    hg_w_out = din("hg_w_out", [D, D])
    ffn_w_in = din("ffn_w_in", [2, D, 2 * DFF])
    ffn_w_out = din("ffn_w_out", [2, DFF, D])
    outT = nc.dram_tensor("outT", [D, S], F32, kind="ExternalOutput").ap()

    V1 = dscr("V1", [GMH, S + CTX], BF16)
    XA = dscr("XA", [D, S + CTX], F32)
    XB = dscr("XB", [D, S + CTX], F32)
    XC = dscr("XC", [D, S], F32)
    OF = dscr("OF", [D, S], F32)
    QDB = dscr("QDB", [D, S], BF16)
    KXB = dscr("KXB", [D, S], BF16)
    SG = dscr("SG", [D, S], BF16)
    VT = dscr("VT", [S, D], BF16)
    bV1, bXA, bXB, bXC, bOF, bQDB, bKXB, bSG, bVT, bOUT = [P.buf(n) for n in
                                                            ["V1", "XA", "XB", "XC", "OF", "QDB", "KXB", "SG", "VT", "OUT"]]
    bIN = P.buf("inputs")

    def fm(ap3, c0, c1):
        return ap3.rearrange("(k p) t -> p k t", p=128)[:, :, c0:c1]

    ARENA_BYTES = 184 * 1024
    AR = es.enter_context(nc.sbuf_tensor("AR", [128, ARENA_BYTES // 2], BF16))
    aoff = [0]

    class _Phase:
        def close(self):
            pass

    def new_phase():
        aoff[0] = 0
        return _Phase()

    def sb(name, shape, dt, stack=None):
        if stack is None:
            return es.enter_context(nc.sbuf_tensor(name, list(shape), dt))
        esz = 4 if dt == F32 else 2
        nel = 1
        for d_ in shape[1:]:
            nel *= d_
        nb = (nel * esz + 63) // 64 * 64
        assert aoff[0] + nb <= ARENA_BYTES, ("arena overflow", name, aoff[0], nb)
        v = AR[:, aoff[0] // 2:(aoff[0] + nb) // 2]
        aoff[0] += nb
        if dt == F32:
            v = v.bitcast(F32)
        v = v[:, 0:nel]
        if len(shape) == 3:
            v = v.rearrange("p (a b) -> p a b", a=shape[1])
        if shape[0] < 128:
            v = v[0:shape[0]]
        return v

    wcnt = [0]

    def wview(e0, kc, n):
        wcnt[0] += 1
        ap = sb("w%d" % wcnt[0], [128, kc, n], BF16, True)
        return ap, [P.buf("w%d" % wcnt[0])]

    Vv = sb("Vv", [128, NV], F32)
    bVv = P.buf("Vv")
    MV = sb("MV", [128, 2 * 2 * 6 * 8], F32)
    bMV = P.buf("MV")
    OML = sb("OML", [128, 32], F32)
    bOML = P.buf("OML")
    ones_bf = sb("ones_bf", [128, 128], BF16)
    ident_bf = sb("ident_bf", [128, 128], BF16)
    mask01 = sb("mask01", [128, 256], BF16)
    maskF = sb("maskF", [64, 512], BF16)
    maskB = sb("maskB", [64, 512], BF16)
    bCONST = P.buf("const")
    NCHL = S // 64
    NCH = NCHL + CTX // 64
    Df = sb("Df", [128, 8, NCH + 1], F32)
    Db = sb("Db", [128, 8, NCH + 1], F32)
    bDf = P.buf("Df")
    bDb = P.buf("Db")
    Yf = sb("Yf", [128, 8, 128], F32)
    Yb = sb("Yb", [128, 8, 128], F32)
    bYf = [P.buf("Yf%d" % h) for h in range(8)]
    bYb = [P.buf("Yb%d" % h) for h in range(8)]

    psum = [es.enter_context(nc.psum_tensor("ps%d" % i, [128, 512], F32)) for i in range(8)]
    bps = [P.buf("ps%d" % i) for i in range(8)]
    psc = [0]

    def ps_next():
        i = psc[0] % 8
        psc[0] += 1
        return psum[i], bps[i]

    def mvcol(l, who, kind, k):
        c = ((l * 2 + who) * 6 + kind) * 8 + k
        return MV[:, c:c + 1]

    A_M, S_M, G_M, A_F, S_F, G_F = range(6)

    def vcol(name, k=0, n=1):
        o, w = VEC_LAYOUT[name]
        return Vv[:, o + k:o + k + n]

    def wload(dst_ap, dst_bufs, src2d, kc, n):
        step = 2048
        for k in range(kc):
            for c0 in range(0, n, step):
                c1 = min(n, c0 + step)
                P.op("gpsimd", lambda en, k=k, c0=c0, c1=c1: en.dma_start(
                    out=dst_ap[:, k, c0:c1], in_=src2d[k * 128:(k + 1) * 128, c0:c1]),
                    reads=[bIN], writes=dst_bufs, dma=True, nowaw=True)

    ph = new_phase()
    ident_f = sb("ident_f", [128, 128], F32, ph)
    P.op("sync", lambda en: en.dma_start(out=Vv[:], in_=vecs[:, :]), reads=[bIN], writes=[bVv], dma=True)

    for t_, b_, v_ in [(ones_bf, bCONST, 1.0), (mask01, bCONST, 1.0), (Df, bDf, 1.0), (Db, bDb, 1.0)]:
        P.op("vector", lambda en, t_=t_, v_=v_: en.memset(t_[:], v_), writes=[b_])
    P.op("vector", lambda en: en.memset(mask01[:, 0:256:64], 0.0), writes=[bCONST])
    P.op("vector", lambda en: en.memset(Yf[:], 0.0), writes=bYf)
    P.op("vector", lambda en: en.memset(Yb[:], 0.0), writes=bYb)
    bMK = P.buf("masks")
    P.op("gpsimd", lambda en: en.memset(ident_f[:], 1.0), writes=[bMK])
    P.op("gpsimd", lambda en: en.affine_select(out=ident_f[:], in_=ident_f[:], pattern=[[-1, 128]],
                                               compare_op=ALU.is_equal, fill=0.0, base=0, channel_multiplier=1),
         writes=[bMK])
    P.op("gpsimd", lambda en: en.memset(maskF[:], 1.0), writes=[bMK])
    P.op("gpsimd", lambda en: en.memset(maskB[:], 1.0), writes=[bMK])
    mf_ = maskF[:].rearrange("p (h t) -> p h t", h=8)
    mb_ = maskB[:].rearrange("p (h t) -> p h t", h=8)
    P.op("gpsimd", lambda en: en.affine_select(out=mf_, in_=mf_, pattern=[[0, 8], [1, 64]], compare_op=ALU.is_ge,
                                               fill=0.0, base=0, channel_multiplier=-1), writes=[bMK])
    P.op("gpsimd", lambda en: en.affine_select(out=mb_, in_=mb_, pattern=[[0, 8], [-1, 64]], compare_op=ALU.is_ge,
                                               fill=0.0, base=0, channel_multiplier=1), writes=[bMK])
    P.op("vector", lambda en: en.tensor_copy(out=ident_bf[:], in_=ident_f[:]), reads=[bMK], writes=[bCONST])
    s_bf = sb("s_bf", [128, 16], BF16, ph)
    bs_bf = P.buf("s_bf")
    P.op("scalar", lambda en: en.activation(out=s_bf[:], in_=vcol("cc", 0, 16), func=AF.Silu),
         reads=[bVv], writes=[bs_bf])
    lbd = sb("lbd", [128, 16], F32, ph)
    blbd = P.buf("lbd")
    P.op("vector", lambda en: en.tensor_tensor(out=lbd[:], in0=vcol("lb0", 0, 16), in1=vcol("lb1", 0, 16),
                                               op=ALU.subtract), reads=[bVv], writes=[blbd])
    P.op("scalar", lambda en: en.activation(out=OML[:, 0:16], in_=lbd[:], func=AF.Sigmoid),
         reads=[blbd], writes=[bOML])
    P.op("vector", lambda en: en.tensor_scalar(out=OML[:, 16:32], in0=OML[:, 0:16], scalar1=-1.0, scalar2=None,
                                               op0=ALU.mult), reads=[bOML], writes=[bOML])
    modt = sb("modt", [128, 96], F32, ph)
    bmodt = P.buf("modt")
    WA, bWA = wview(0, KC, 6 * D)
    for l in range(2):
        wload(WA, bWA, ada_w[l], KC, 6 * D)
        pst, bpst = ps_next()

        def _mod(en, WA=WA, pst=pst):
            for dc in range(48):
                for k in range(KC):
                    ins = en.matmul(pst[:, dc * 2:dc * 2 + 2], WA[:, k, dc * 128:(dc + 1) * 128],
                                    s_bf[:, 2 * k:2 * k + 2], start=(k == 0), stop=(k == KC - 1))
            return ins

        P.op("tensor", _mod, reads=bWA + [bs_bf], writes=[bpst])
        for who in range(2):
            P.op("vector", lambda en, pst=pst, who=who, l=l: en.tensor_tensor(
                out=modt[:, who * 48:(who + 1) * 48], in0=pst[:, who:96:2], in1=vcol("ada_b%d" % l, 0, 48), op=ALU.add),
                reads=[bpst, bVv], writes=[bmodt])

            def _mv(en, who=who, l=l):
                m = modt[:, who * 48:(who + 1) * 48]
                base = ((l * 2 + who) * 6) * 8
                en.scalar_tensor_tensor(out=MV[:, base + A_M * 8:base + A_M * 8 + 8], in0=m[:, 8:16], scalar=1.0,
                                        in1=vcol("nmw%d" % l, 0, 8), op0=ALU.add, op1=ALU.mult)
                en.tensor_copy(out=MV[:, base + S_M * 8:base + S_M * 8 + 8], in_=m[:, 0:8])
                en.tensor_copy(out=MV[:, base + G_M * 8:base + G_M * 8 + 8], in_=m[:, 16:24])
                en.scalar_tensor_tensor(out=MV[:, base + A_F * 8:base + A_F * 8 + 8], in0=m[:, 32:40], scalar=1.0,
                                        in1=vcol("nfw%d" % l, 0, 8), op0=ALU.add, op1=ALU.mult)
                en.tensor_copy(out=MV[:, base + S_F * 8:base + S_F * 8 + 8], in_=m[:, 24:32])
                return en.tensor_copy(out=MV[:, base + G_F * 8:base + G_F * 8 + 8], in_=m[:, 40:48])

            P.op("vector", _mv, reads=[bmodt, bVv], writes=[bMV])
    P.barrier()
    P.emit()
    ph.close()

    def tiles(T, with_ctx=True, lat=True):
        out = []
        if with_ctx:
            out.append(dict(ctx=True, t0=0, n=CTX, who=1))
        if lat:
            for t0 in range(0, S, T):
                out.append(dict(ctx=False, t0=t0, n=min(T, S - t0), who=0))
        return out

    class Ring:
        def __init__(self, name, shape, dt, n, stack):
            self.t = [sb("%s%d" % (name, i), shape, dt, stack) for i in range(n)]
            self.b = [P.buf("%s%d" % (name, i)) for i in range(n)]
            self.i = 0

        def next(self):
            j = self.i % len(self.t)
            self.i += 1
            return self.t[j], self.b[j]

    def norm_stage(x_t, bx, n, l, who, kinds, out_t, bout, sq_ring, rst_ring, tmp_ring, sq_eng="gpsimd",
                   custom_a=None, out_f32=False, lnexp=False, defer=None):
        sq_t, bsq = sq_ring.next()
        P.op(sq_eng, lambda en: en.tensor_tensor(out=sq_t[:, :, 0:n], in0=x_t[:, :, 0:n], in1=x_t[:, :, 0:n],
                                                 op=ALU.mult), reads=[bx], writes=[bsq])
        if defer is not None:
            defer.append(lambda: norm_post(x_t, bx, n, l, who, kinds, out_t, bout, sq_t, bsq, rst_ring, tmp_ring,
                                           custom_a, out_f32, lnexp))
        else:
            norm_post(x_t, bx, n, l, who, kinds, out_t, bout, sq_t, bsq, rst_ring, tmp_ring, custom_a, out_f32, lnexp)

    def norm_post(x_t, bx, n, l, who, kinds, out_t, bout, sq_t, bsq, rst_ring, tmp_ring, custom_a, out_f32, lnexp):
        pst, bpst = ps_next()

        def _ss(en):
            for k in range(KC):
                ins = en.matmul(pst[:, 0:n], ones_bf[:], sq_t[:, k, 0:n], start=(k == 0), stop=(k == KC - 1))
            return ins

        P.op("tensor", _ss, reads=[bsq, bCONST], writes=[bpst], chain=False)
        rst, brst = rst_ring.next()
        if lnexp:
            P.op("scalar", lambda en: en.activation(out=rst[:, 0:n], in_=pst[:, 0:n], func=AF.Ln, bias=EPS,
                                                    scale=1.0 / D), reads=[bpst], writes=[brst])
            P.op("scalar", lambda en: en.activation(out=rst[:, 0:n], in_=rst[:, 0:n], func=AF.Exp, scale=-0.5),
                 reads=[brst], writes=[brst])
        else:
            P.op("scalar", lambda en: en.activation(out=rst[:, 0:n], in_=pst[:, 0:n], func=AF.Sqrt, bias=EPS,
                                                    scale=1.0 / D), reads=[bpst], writes=[brst])
            P.op("vector", lambda en: en.reciprocal(out=rst[:, 0:n], in_=rst[:, 0:n]), reads=[brst], writes=[brst])
        for k in range(KC):
            a_ap = custom_a(k) if custom_a else mvcol(l, who, kinds[0], k)
            if out_f32:
                P.op("vector", lambda en, k=k, a_ap=a_ap: en.scalar_tensor_tensor(
                    out=out_t[:, k, 0:n], in0=x_t[:, k, 0:n], scalar=a_ap, in1=rst[:, 0:n], op0=ALU.mult,
                    op1=ALU.mult), reads=[bx, brst, bMV, bVv], writes=[bout])
                continue
            tmp, btmp = tmp_ring.next()
            P.op("vector", lambda en, k=k, a_ap=a_ap, tmp=tmp: en.scalar_tensor_tensor(
                out=tmp[:, 0:n], in0=x_t[:, k, 0:n], scalar=a_ap, in1=rst[:, 0:n], op0=ALU.mult, op1=ALU.mult),
                reads=[bx, brst, bMV], writes=[btmp])
            s_ap = mvcol(l, who, kinds[1], k)
            P.op("vector", lambda en, k=k, s_ap=s_ap, tmp=tmp: en.tensor_scalar(
                out=out_t[:, k, 0:n], in0=tmp[:, 0:n], scalar1=s_ap, scalar2=None, op0=ALU.add),
                reads=[btmp, bMV], writes=[bout])

    def src_cols(tl, lat_ap, ctx_ap_or_none, S_off=True):
        if tl["ctx"]:
            return (S, S + CTX)
        return (tl["t0"], tl["t0"] + tl["n"])

    def std_mm(W, bW, kc, col0, rhs_fn, n, reads):
        pst, bpst = ps_next()

        def _mm(en):
            for k in range(kc):
                ins = en.matmul(pst[:, 0:n], W[:, k, col0:col0 + 128], rhs_fn(k), start=(k == 0), stop=(k == kc - 1))
            return ins

        P.op("tensor", _mm, reads=bW + reads, writes=[bpst], chain=False)
        return pst, bpst

    def ffn_phase(l, src, bsrc, src_has_ctx, dst, bdst, final):
        T = 256
        ph = new_phase()
        W1, bW1 = wview(0, KC, 2 * DFF)
        W2, bW2 = wview(KC * 2 * DFF, NFC, D)
        wload(W1, bW1, ffn_w_in[l], KC, 2 * DFF)
        wload(W2, bW2, ffn_w_out[l], NFC, D)
        xr = Ring("fx", [128, KC, T], F32, 2, ph)
        hr = Ring("fh", [128, KC, T], BF16, 2, ph)
        sqr = Ring("fsq", [128, KC, T], BF16, 1, ph)
        rstr = Ring("frst", [128, T], F32, 2, ph)
        tmpr = Ring("ftmp", [128, T], F32, 2, ph)
        mr = Ring("fm", [128, NFC, T], BF16, 1, ph)
        sar = Ring("fsa", [128, T], F32, 3, ph)
        tls = tiles(T, with_ctx=src_has_ctx)
        st = {}

        def stageA(i, defer=None):
            tl = tls[i]
            n = tl["n"]
            x_t, bx = xr.next()
            h_t, bh = hr.next()
            c0, c1 = src_cols(tl, None, None)
            P.op("sync", lambda en: en.dma_start(out=x_t[:, :, 0:n], in_=fm(src, c0, c1)), reads=[bsrc], writes=[bx],
                 dma=True)
            norm_stage(x_t, bx, n, l, tl["who"], (A_F, S_F), h_t, bh, sqr, rstr, tmpr, defer=defer)
            st[i] = (x_t, bx, h_t, bh)

        def stageB(i):
            tl = tls[i]
            n = tl["n"]
            x_t, bx, h_t, bh = st.pop(i)
            m_t, bm = mr.next()
            pend = []
            if i + 1 < len(tls):
                stageA(i + 1, pend)
            for dc in range(NFC):
                pa, bpa = std_mm(W1, bW1, KC, dc * 128, lambda k: h_t[:, k, 0:n], n, [bh])
                pb, bpb = std_mm(W1, bW1, KC, DFF + dc * 128, lambda k: h_t[:, k, 0:n], n, [bh])
                sa, bsa = sar.next()
                P.op("scalar", lambda en, pa=pa, sa=sa: en.activation(out=sa[:, 0:n], in_=pa[:, 0:n], func=AF.Silu),
                     reads=[bpa], writes=[bsa])
                P.op("vector", lambda en, pb=pb, sa=sa, dc=dc: en.tensor_tensor(
                    out=m_t[:, dc, 0:n], in0=pb[:, 0:n], in1=sa[:, 0:n], op=ALU.mult), reads=[bpb, bsa], writes=[bm])
                if dc == 15:
                    for f_ in pend:
                        f_()
            for oc in range(KC):
                py, bpy = std_mm(W2, bW2, NFC, oc * 128, lambda k: m_t[:, k, 0:n], n, [bm])
                g_ap = mvcol(l, tl["who"], G_F, oc)
                P.op("vector", lambda en, py=py, oc=oc, g_ap=g_ap: en.scalar_tensor_tensor(
                    out=x_t[:, oc, 0:n], in0=py[:, 0:n], scalar=g_ap, in1=x_t[:, oc, 0:n], op0=ALU.mult, op1=ALU.add),
                    reads=[bpy, bMV, bx], writes=[bx])
            if final:
                o_t, bo = x_t, bx
                norm_stage(x_t, bx, n, l, 0, None, o_t, bo, sqr, rstr, tmpr,
                           custom_a=lambda k: vcol("fnw", k), out_f32=True)
                P.op("sync", lambda en: en.dma_start(out=fm(dst, tl["t0"], tl["t0"] + n), in_=o_t[:, :, 0:n]),
                     reads=[bo], writes=[bdst], dma=True)
            else:
                c0, c1 = src_cols(tl, None, None)
                P.op("sync", lambda en: en.dma_start(out=fm(dst, c0, c1), in_=x_t[:, :, 0:n]), reads=[bx],
                     writes=[bdst], dma=True)

        stageA(0)
        for i in range(len(tls)):
            stageB(i)
        P.barrier()
        P.emit()
        ph.close()

    pos_eng = ["gpsimd"]

    def load_x0(tl, x_t, bx, pos_ring):
        n = tl["n"]
        if tl["ctx"]:
            P.op("sync", lambda en: en.dma_start(out=x_t[:, :, 0:n], in_=fm(ctxT, 0, CTX)), reads=[bIN], writes=[bx],
                 dma=True)
        else:
            p_t, bp = pos_ring.next()
            t0 = tl["t0"]
            P.op("sync", lambda en: en.dma_start(out=x_t[:, :, 0:n], in_=fm(xT, t0, t0 + n)), reads=[bIN],
                 writes=[bx], dma=True)
            P.op("sync", lambda en: en.dma_start(out=p_t[:, :, 0:n], in_=fm(posT, t0, t0 + n)), reads=[bIN],
                 writes=[bp], dma=True)
            P.op(pos_eng[0], lambda en: en.tensor_tensor(out=x_t[:, :, 0:n], in0=x_t[:, :, 0:n], in1=p_t[:, :, 0:n],
                                                         op=ALU.add), reads=[bx, bp], writes=[bx])

    def phase1a():
        T = 256
        ph = new_phase()
        Wv, bWv = wview(0, KC, GMH)
        wload(Wv, bWv, gm_w_in[:, GMH:2 * GMH], KC, GMH)
        wsb = sb("wsb", [128, 1024], BF16, ph)
        bwsb = P.buf("wsb")
        P.op("gpsimd", lambda en: en.dma_start(out=wsb[:], in_=wsT_d[:, :]), reads=[bIN], writes=[bwsb], dma=True)
        bvb = sb("bvb", [128, GMH], BF16, ph)
        bbvb = P.buf("bvb")
        P.op("vector", lambda en: en.memset(bvb[:], 0.0), writes=[bbvb])
        P.op("gpsimd", lambda en: en.dma_start(out=bvb[0:1, :], in_=bvrow[:, :], max_dma_last_dim=4096), reads=[bIN],
             writes=[bbvb], dma=True)
        Ct = sb("Ct", [128, 24, 128], F32, ph)
        bCt = P.buf("Ct")
        amark = aoff[0]
        Rt = sb("Rt", [2, 1024], F32, ph)
        Lt = sb("Lt", [2, GMH], F32, ph)
        bRt = P.buf("Rt")
        bLt = P.buf("Lt")
        ws32 = sb("ws32", [128, 1024], F32, ph)
        bws32 = P.buf("ws32")
        P.op("sync", lambda en: en.dma_start(out=ws32[:], in_=wsT_d[:, :]), reads=[bIN], writes=[bws32], dma=True)
        ones32 = sb("ones32", [128, 2], F32, ph)
        bo32 = P.buf("ones32")
        P.op("vector", lambda en: en.memset(ones32[:], 1.0), writes=[bo32])
        P.op("vector", lambda en: en.memset(Lt[:], 1.0), writes=[bLt])
        P.op("sync", lambda en: en.dma_start(out=Lt[0:1, :], in_=lnbrow[:, :]), reads=[bIN], writes=[bLt], dma=True)
        P.op("sync", lambda en: en.dma_start(out=Rt[1:2, :], in_=bsrow[:, :]), reads=[bIN], writes=[bRt], dma=True)
        for hb in range(2):
            pst, bpst = ps_next()
            P.op("tensor", lambda en, pst=pst, hb=hb: en.matmul(pst[0:1, :], ones32[:, 0:1],
                                                               ws32[:, hb * 512:(hb + 1) * 512], start=True, stop=True),
                 reads=[bws32, bo32], writes=[bpst])
            P.op("vector", lambda en, pst=pst, hb=hb: en.tensor_copy(out=Rt[0:1, hb * 512:(hb + 1) * 512],
                                                                     in_=pst[0:1, :]), reads=[bpst], writes=[bRt])
        for dc in range(24):
            g = dc // 3
            pst, bpst = ps_next()
            P.op("tensor", lambda en, pst=pst, dc=dc, g=g: en.matmul(
                pst[:, 0:128], Lt[0:2, dc * 128:(dc + 1) * 128], Rt[0:2, g * 128:(g + 1) * 128], start=True, stop=True),
                reads=[bLt, bRt], writes=[bpst])
            P.op("vector", lambda en, pst=pst, dc=dc: en.tensor_copy(out=Ct[:, dc, :], in_=pst[:, 0:128]),
                 reads=[bpst], writes=[bCt])

        P.barrier()
        aoff[0] = amark
        xr = Ring("ax", [128, KC, T], F32, 2, ph)
        posr = Ring("apos", [128, KC, T], F32, 1, ph)
        hr = Ring("ah", [128, KC, T], BF16, 2, ph)
        sqr = Ring("asq", [128, KC, T], BF16, 1, ph)
        rstr = Ring("arst", [128, T], F32, 2, ph)
        tmpr = Ring("atmp", [128, T], F32, 3, ph)
        vgr = Ring("avg", [128, GMH], F32, 2, ph)
        vnr = Ring("avn", [128, GMH], BF16, 2, ph)
        str_ = Ring("ast", [128, 6, 6], F32, 2, ph)
        mvr = Ring("amv", [128, 8], F32, 2, ph)
        vpr = Ring("avp", [128, 24, T], BF16, 2, ph)
        tls = tiles(T)
        st = {}

        pending_sp = []

        def stageA(i, defer=None):
            tl = tls[i]
            x_t, bx = xr.next()
            h_t, bh = hr.next()
            load_x0(tl, x_t, bx, posr)
            norm_stage(x_t, bx, tl["n"], 0, tl["who"], (A_M, S_M), h_t, bh, sqr, rstr, tmpr, defer=defer)
            st[i] = (h_t, bh)

        def stageB(i):
            tl = tls[i]
            n = tl["n"]
            h_t, bh = st.pop(i)
            vp, bvp = vpr.next()
            pend = []
            if i + 1 < len(tls):
                stageA(i + 1, pend)
            nj = n // 128
            for j in range(nj):
                vg, bvg = vgr.next()
                stt, bstt = str_.next()
                for fb in range(6):
                    pst, bpst = ps_next()

                    def _mm(en, pst=pst, fb=fb, j=j):
                        for k in range(KC):
                            en.matmul(pst[:, :], h_t[:, k, j * 128:(j + 1) * 128], Wv[:, k, fb * 512:(fb + 1) * 512],
                                      start=(k == 0), stop=False)
                        return en.matmul(pst[:, :], ones_bf[:, :], bvb[:, fb * 512:(fb + 1) * 512], start=False,
                                         stop=True)

                    P.op("tensor", _mm, reads=bWv + [bh, bbvb, bCONST], writes=[bpst], chain=False)
                    P.op("scalar", lambda en, pst=pst, fb=fb, vg=vg: en.activation(
                        out=vg[:, fb * 512:(fb + 1) * 512], in_=pst[:, :], func=AF.Gelu), reads=[bpst], writes=[bvg])
                    P.op("vector", lambda en, fb=fb, vg=vg, stt=stt: en.bn_stats(
                        out=stt[:, fb, :], in_=vg[:, fb * 512:(fb + 1) * 512]), reads=[bvg], writes=[bstt])
                mv, bmv = mvr.next()

                P.op("vector", lambda en, stt=stt, mv=mv: en.bn_aggr(out=mv[:, 0:2],
                                                                     in_=stt[:].rearrange("p a b -> p (a b)")),
                     reads=[bstt], writes=[bmv])
                P.op("vector", lambda en, mv=mv: en.tensor_scalar(out=mv[:, 2:3], in0=mv[:, 1:2], scalar1=EPS,
                                                                   scalar2=None, op0=ALU.add), reads=[bmv], writes=[bmv])
                P.op("scalar", lambda en, mv=mv: en.activation(out=mv[:, 2:3], in_=mv[:, 2:3], func=AF.Sqrt),
                     reads=[bmv], writes=[bmv])
                P.op("vector", lambda en, mv=mv: en.reciprocal(out=mv[:, 3:4], in_=mv[:, 2:3]), reads=[bmv],
                     writes=[bmv])
                P.op("vector", lambda en, mv=mv: en.tensor_scalar(out=mv[:, 4:5], in0=mv[:, 0:1], scalar1=-1.0,
                                                                   scalar2=mv[:, 3:4], op0=ALU.mult, op1=ALU.mult),
                     reads=[bmv], writes=[bmv])
                vn, bvn = vnr.next()
                P.op("gpsimd", lambda en, vg=vg, vn=vn, mv=mv: en.tensor_scalar(
                    out=vn[:], in0=vg[:], scalar1=mv[:, 3:4], scalar2=mv[:, 4:5], op0=ALU.mult, op1=ALU.add),
                    reads=[bvg, bmv], writes=[bvn])
                while pending_sp:
                    pending_sp.pop(0)()
                pending_sp.append(lambda j=j, vn=vn, bvn=bvn, vp=vp, bvp=bvp, tl=tl, n=n, nj=nj: spatial(
                    j, vn, bvn, vp, bvp, tl, n, j == nj - 1))
                if j == 0:
                    for f_ in pend:
                        f_()

        def spatial(j, vn, bvn, vp, bvp, tl, n, last):
            if True:
                for q4 in range(6):
                    pst, bpst = ps_next()

                    def _sp(en, pst=pst, q4=q4, vn=vn):
                        for r in range(4):
                            dc = q4 * 4 + r
                            ins = en.matmul(pst[:, r * 128:(r + 1) * 128], vn[:, dc * 128:(dc + 1) * 128],
                                            wsb[:, (dc // 3) * 128:(dc // 3 + 1) * 128], start=True, stop=True)
                        return ins

                    P.op("tensor", _sp, reads=[bvn, bwsb], writes=[bpst], chain=False)
                    for r in range(4):
                        dc = q4 * 4 + r
                        P.op("vector", lambda en, pst=pst, r=r, dc=dc, j=j: en.scalar_tensor_tensor(
                            out=vp[:, dc, j * 128:(j + 1) * 128], in0=pst[:, r * 128:(r + 1) * 128],
                            scalar=vcol("lng", dc), in1=Ct[:, dc, :], op0=ALU.mult, op1=ALU.add),
                            reads=[bpst, bVv, bCt], writes=[bvp])
            if last:
                c0, c1 = src_cols(tl, None, None)
                P.op("sync", lambda en: en.dma_start(out=fm(V1, c0, c1), in_=vp[:, :, 0:n]), reads=[bvp], writes=[bV1],
                     dma=True)

        stageA(0)
        for i in range(len(tls)):
            stageB(i)
        while pending_sp:
            pending_sp.pop(0)()
        P.barrier()
        P.emit()
        ph.close()

    def phase1b():
        T = 256
        pos_eng[0] = "vector"
        ph = new_phase()
        Wu, bWu = wview(0, KC, GMH)
        Wo, bWo = wview(KC * GMH, 24, D)
        wload(Wu, bWu, gm_w_in[:, 0:GMH], KC, GMH)
        wload(Wo, bWo, gm_w_out, 24, D)
        xr = Ring("bx", [128, KC, T], F32, 3, ph)
        posr = Ring("bpos", [128, KC, T], F32, 1, ph)
        hr = Ring("bh", [128, KC, T], BF16, 2, ph)
        sqr = Ring("bsq", [128, KC, T], BF16, 1, ph)
        rstr = Ring("brst", [128, T], F32, 2, ph)
        tmpr = Ring("btmp", [128, T], F32, 3, ph)
        vpr = Ring("bvp", [128, 24, T], BF16, 2, ph)
        ur = Ring("bu", [128, T], BF16, 3, ph)
        tls = tiles(T)
        st = {}

        def stageA(i, defer=None):
            tl = tls[i]
            n = tl["n"]
            x_t, bx = xr.next()
            h_t, bh = hr.next()
            vp, bvp = vpr.next()
            load_x0(tl, x_t, bx, posr)
            c0, c1 = src_cols(tl, None, None)
            P.op("sync", lambda en: en.dma_start(out=vp[:, :, 0:n], in_=fm(V1, c0, c1)), reads=[bV1], writes=[bvp],
                 dma=True)
            norm_stage(x_t, bx, n, 0, tl["who"], (A_M, S_M), h_t, bh, sqr, rstr, tmpr, defer=defer)
            st[i] = (x_t, bx, h_t, bh, vp, bvp)

        def stageB(i):
            tl = tls[i]
            n = tl["n"]
            x_t, bx, h_t, bh, vp, bvp = st.pop(i)
            pend = []
            if i + 1 < len(tls):
                stageA(i + 1, pend)
            for dc in range(24):
                pu, bpu = std_mm(Wu, bWu, KC, dc * 128, lambda k: h_t[:, k, 0:n], n, [bh])
                u_t, bu = ur.next()
                P.op("scalar", lambda en, pu=pu, u_t=u_t, dc=dc: en.activation(
                    out=u_t[:, 0:n], in_=pu[:, 0:n], func=AF.Gelu, bias=vcol("bu", dc), scale=1.0),
                    reads=[bpu, bVv], writes=[bu])
                P.op("vector", lambda en, u_t=u_t, dc=dc: en.tensor_tensor(
                    out=vp[:, dc, 0:n], in0=vp[:, dc, 0:n], in1=u_t[:, 0:n], op=ALU.mult), reads=[bu, bvp],
                    writes=[bvp])
                if dc == 17:
                    for f_ in pend:
                        f_()
            for oc in range(KC):
                py, bpy = std_mm(Wo, bWo, 24, oc * 128, lambda k: vp[:, k, 0:n], n, [bvp])
                g_ap = mvcol(0, tl["who"], G_M, oc)
                P.op("vector", lambda en, py=py, oc=oc, g_ap=g_ap: en.scalar_tensor_tensor(
                    out=x_t[:, oc, 0:n], in0=py[:, 0:n], scalar=g_ap, in1=x_t[:, oc, 0:n], op0=ALU.mult, op1=ALU.add),
                    reads=[bpy, bMV, bx], writes=[bx])
            c0, c1 = src_cols(tl, None, None)
            P.op("sync", lambda en: en.dma_start(out=fm(XA, c0, c1), in_=x_t[:, :, 0:n]), reads=[bx], writes=[bXA],
                 dma=True)

        stageA(0)
        for i in range(len(tls)):
            stageB(i)
        P.barrier()
        P.emit()
        ph.close()

    def chain_pre(kx_t, bkx, qd_t, bqd, vt_t, bvt, c, direction, rings, do_out):
        attr, kxtr, xbr, x32r = rings
        c64 = slice(c * 64, (c + 1) * 64)
        mask = maskF if direction == 0 else maskB
        attm = battm = None
        if do_out:
            pa, bpa = ps_next()

            def _att(en):
                for h in range(8):
                    ins = en.matmul(pa[0:64, h * 64:(h + 1) * 64], kx_t[:, h, c64], qd_t[:, h, c64], start=True,
                                    stop=True)
                return ins

            P.op("tensor", _att, reads=[bkx, bqd], writes=[bpa])
            attm, battm = attr.next()
            P.op("vector", lambda en: en.tensor_tensor(out=attm[:, :], in0=pa[0:64, :], in1=mask[:, :], op=ALU.mult),
                 reads=[bpa, bCONST], writes=[battm])
        pt, bpt = ps_next()
        ptb = pt[:].bitcast(BF16)

        def _tr(en):
            for h in range(8):
                ins = en.transpose(ptb[0:64, h * 128:(h + 1) * 128], kx_t[:, h, c64], ident_bf[:])
            return ins

        P.op("tensor", _tr, reads=[bkx, bCONST], writes=[bpt])
        kxt, bkxt = kxtr.next()
        P.op("scalar", lambda en: en.activation(out=kxt[:, :], in_=ptb[0:64, :], func=AF.Copy), reads=[bpt],
             writes=[bkxt])
        pks = []
        for hb in range(2):
            pk, bpk = ps_next()

            def _kv(en, hb=hb, pk=pk):
                for r in range(4):
                    h = hb * 4 + r
                    ins = en.matmul(pk[:, r * 128:(r + 1) * 128], kxt[:, h * 128:(h + 1) * 128],
                                    vt_t[0:64, c, h * 128:(h + 1) * 128], start=True, stop=True)
                return ins

            P.op("tensor", _kv, reads=[bkxt, bvt], writes=[bpk])
            pks.append((pk, bpk))
        return (attm, battm, pks)

    def chain_rec(pre, qd_t, bqd, vt_t, bvt, c, gc, Y, bY, Dt, bD, rings, do_out, of_t=None, bof=None, add_prev=False):
        attr, kxtr, xbr, x32r = rings
        attm, battm, pks = pre
        c64 = slice(c * 64, (c + 1) * 64)
        x32, bx32 = x32r.next()
        Dbc = Dt[:, :, gc:gc + 1].broadcast_to([128, 8, 128])
        P.op("vector", lambda en: en.tensor_tensor(out=x32[:, :, :], in0=Y[:, :, :], in1=Dbc, op=ALU.mult),
             reads=bY + [bD], writes=[bx32])
        for hb in range(2):
            pk, bpk = pks[hb]
            P.op("vector", lambda en, hb=hb, pk=pk: en.tensor_tensor(
                out=Y[:, hb * 4:(hb + 1) * 4, :], in0=x32[:, hb * 4:(hb + 1) * 4, :],
                in1=pk[:].rearrange("p (r v) -> p r v", r=4), op=ALU.add), reads=[bx32, bpk],
                writes=bY[hb * 4:(hb + 1) * 4])
        if do_out:
            xb, bxb = xbr.next()
            P.op("scalar", lambda en: en.activation(out=xb[:, :, :], in_=x32[:, :, :], func=AF.Copy), reads=[bx32],
                 writes=[bxb])
            po, bpo = ps_next()

            def _o(en):
                for h in range(8):
                    en.matmul(po[:, h * 64:(h + 1) * 64], vt_t[0:64, c, h * 128:(h + 1) * 128],
                              attm[:, h * 64:(h + 1) * 64], start=True, stop=False)
                    ins = en.matmul(po[:, h * 64:(h + 1) * 64], xb[:, h, :], qd_t[:, h, c64], start=False, stop=True)
                return ins

            P.op("tensor", _o, reads=[bvt, battm, bxb, bqd], writes=[bpo])
            pov = po[:].rearrange("p (h t) -> p h t", h=8)
            if add_prev:
                P.op("vector", lambda en: en.tensor_tensor(out=of_t[:, :, c64], in0=pov, in1=of_t[:, :, c64],
                                                           op=ALU.add), reads=[bpo, bof], writes=[bof])
            else:
                P.op("scalar", lambda en: en.activation(out=of_t[:, :, c64], in_=pov, func=AF.Copy), reads=[bpo],
                     writes=[bof])

    def run_chain(chunks, kx_t, bkx, qd_t, bqd, vt_t, bvt, direction, Y, bY, Dt, bD, rings, do_out, of_t=None,
                  bof=None, add_prev=False):
        pre = chain_pre(kx_t, bkx, qd_t, bqd, vt_t, bvt, chunks[0][0], direction, rings, do_out)
        for i_, (c, gc) in enumerate(chunks):
            nxt = None
            if i_ + 1 < len(chunks):
                nxt = chain_pre(kx_t, bkx, qd_t, bqd, vt_t, bvt, chunks[i_ + 1][0], direction, rings, do_out)
            chain_rec(pre, qd_t, bqd, vt_t, bvt, c, gc, Y, bY, Dt, bD, rings, do_out, of_t, bof, add_prev)
            pre = nxt

    def phase3():
        T = 256
        NC4 = T // 64
        ph = new_phase()
        W, bW = wview(0, KC, 5 * D)
        wload(W, bW, hg_w_in, KC, 5 * D)
        xr = Ring("cx", [128, KC, T], F32, 1, ph)
        hr = Ring("ch", [128, KC, T], BF16, 1, ph)
        sqr = Ring("csq", [128, KC, T], BF16, 1, ph)
        rstr = Ring("crst", [128, T], F32, 1, ph)
        tmpr = Ring("ctmp", [128, T], F32, 2, ph)
        qr = Ring("cq", [128, 8, T], BF16, 1, ph)
        sgr = Ring("csg", [128, 8, T], BF16, 1, ph)
        vtr = Ring("cvt", [64, NC4, D], BF16, 1, ph)
        qdr = [Ring("cqd%d" % d, [128, 8, T], BF16, 1, ph) for d in range(2)]
        kxr = [Ring("ckx%d" % d, [128, 8, T], BF16, 1, ph) for d in range(2)]
        ofr = Ring("cof", [128, 8, T], F32, 1, ph)
        tAr = Ring("ctA", [128, T], F32, 4, ph)
        tBr = Ring("ctB", [128, T + 1], F32, 4, ph)
        tCr = Ring("ctC", [128, T], F32, 4, ph)
        tDr = Ring("ctD", [128, T], F32, 4, ph)
        bscr = Ring("cbs", [128, T], F32, 4, ph)
        tsr = Ring("cts", [128, NC4], F32, 4, ph)
        rings = (Ring("catt", [64, 512], BF16, 2, ph), Ring("ckxt", [64, 1024], BF16, 2, ph),
                 Ring("cxb", [128, 8, 128], BF16, 2, ph), Ring("cx32", [128, 8, 128], F32, 2, ph))
        for t_, b_ in zip(tBr.t, tBr.b):
            P.op("vector", lambda en, t_=t_: en.memset(t_[:], 0.0), writes=[b_])
        tls = tiles(T)
        ctx_kxb = None
        def p3_tile(i, tl):
            n = tl["n"]
            isctx = tl["ctx"]
            gc0 = 0 if isctx else CTX // 64 + tl["t0"] // 64
            x_t, bx = xr.next()
            h_t, bh = hr.next()
            c0, c1 = src_cols(tl, None, None)
            P.op("sync", lambda en, x_t=x_t, c0=c0, c1=c1, n=n: en.dma_start(out=x_t[:, :, 0:n], in_=fm(XB, c0, c1)),
                 reads=[bXB], writes=[bx], dma=True)
            norm_stage(x_t, bx, n, 1, tl["who"], (A_M, S_M), h_t, bh, sqr, rstr, tmpr, lnexp=True)
            q_t = bq = sg_t = bsg = None
            if not isctx:
                q_t, bq = qr.next()
                sg_t, bsg = sgr.next()
                for h in range(8):
                    pq, bpq = std_mm(W, bW, KC, h * 128, lambda k: h_t[:, k, 0:n], n, [bh])
                    P.op("scalar", lambda en, pq=pq, h=h, q_t=q_t: en.activation(out=q_t[:, h, 0:n], in_=pq[:, 0:n],
                                                                                func=AF.Silu), reads=[bpq], writes=[bq])
                for h in range(8):
                    pq, bpq = std_mm(W, bW, KC, 4 * D + h * 128, lambda k: h_t[:, k, 0:n], n, [bh])
                    P.op("scalar", lambda en, pq=pq, h=h, sg_t=sg_t: en.activation(out=sg_t[:, h, 0:n], in_=pq[:, 0:n],
                                                                                  func=AF.Silu), reads=[bpq],
                         writes=[bsg])
                t0 = tl["t0"]
                P.op("sync", lambda en, sg_t=sg_t, t0=t0, n=n: en.dma_start(out=fm(SG, t0, t0 + n), in_=sg_t[:, :, 0:n]),
                     reads=[bsg], writes=[bSG], dma=True)
            vt_t, bvt = vtr.next()
            for c in range(n // 64):
                for nb in range(2):
                    pst, bpst = ps_next()

                    def _mi(en, pst=pst, c=c, nb=nb):
                        for k in range(KC):
                            ins = en.matmul(pst[0:64, :], h_t[:, k, c * 64:(c + 1) * 64],
                                            W[:, k, 3 * D + nb * 512:3 * D + (nb + 1) * 512], start=(k == 0),
                                            stop=(k == KC - 1))
                        return ins

                    P.op("tensor", _mi, reads=bW + [bh], writes=[bpst])
                    P.op("vector", lambda en, pst=pst, c=c, nb=nb, vt_t=vt_t: en.tensor_copy(
                        out=vt_t[:, c, nb * 512:(nb + 1) * 512], in_=pst[0:64, :]), reads=[bpst], writes=[bvt])
            if not isctx:
                t0 = tl["t0"]
                P.op("sync", lambda en, vt_t=vt_t, t0=t0, n=n: en.dma_start(
                    out=VT.rearrange("(c s) f -> s c f", s=64)[:, t0 // 64:(t0 + n) // 64, :], in_=vt_t[:, 0:n // 64, :]),
                    reads=[bvt], writes=[bVT], dma=True)
            qd = [None, None]
            kx = [None, None]
            for d in range(2):
                kx[d] = kxr[d].next()
                if not isctx:
                    qd[d] = qdr[d].next()
            nch = n // 64
            items = [(d, h) for d in range(2) for h in range(8)]
            WAVE = 4
            for w0 in range(0, 16, WAVE):
                grp = items[w0:w0 + WAVE]
                bufs = {}
                for (d, h) in grp:
                    pz, bpz = std_mm(W, bW, KC, (1 + d) * D + h * 128, lambda k: h_t[:, k, 0:n], n, [bh])
                    tA, btA = tAr.next()
                    tB, btB = tBr.next()
                    tC, btC = tCr.next()
                    tD, btD = tDr.next()
                    bs, bbs = bscr.next()
                    bufs[(d, h)] = (tA, btA, tB, btB, tC, btC, tD, btD, bs, bbs)
                    P.op("scalar", lambda en, pz=pz, tA=tA: en.activation(out=tA[:, 0:n], in_=pz[:, 0:n],
                                                                         func=AF.Exp), reads=[bpz], writes=[btA])
                for (d, h) in grp:
                    tA, btA, tB, btB, tC, btC, tD, btD, bs, bbs = bufs[(d, h)]
                    P.op("scalar", lambda en, tA=tA, tC=tC: en.activation(out=tC[:, 0:n], in_=tA[:, 0:n], func=AF.Ln,
                                                                         bias=1.0, scale=1.0), reads=[btA],
                         writes=[btC])
                    P.op("scalar", lambda en, tA=tA, tC=tC: en.activation(out=tA[:, 0:n], in_=tC[:, 0:n], func=AF.Exp,
                                                                         scale=-1.0), reads=[btC], writes=[btA])
                for (d, h) in grp:
                    tA, btA, tB, btB, tC, btC, tD, btD, bs, bbs = bufs[(d, h)]
                    P.op("scalar", lambda en, tA=tA, tB=tB, d=d, h=h: en.activation(
                        out=tB[:, 1:n + 1], in_=tA[:, 0:n], func=AF.Ln, bias=1.0,
                        scale=OML[:, 16 + d * 8 + h:17 + d * 8 + h]), reads=[btA, bOML], writes=[btB])
                for (d, h) in grp:
                    tA, btA, tB, btB, tC, btC, tD, btD, bs, bbs = bufs[(d, h)]
                    if d == 0:
                        P.op("vector", lambda en, tB=tB, bs=bs: en.tensor_tensor_scan(
                            out=bs[:, 0:n], data0=mask01[:, 0:n], data1=tB[:, 1:n + 1], initial=0.0, op0=ALU.mult,
                            op1=ALU.add), reads=[btB, bCONST], writes=[bbs])
                    else:
                        P.op("vector", lambda en, tB=tB, bs=bs: en.tensor_tensor_scan(
                            out=bs[:, 0:n], data0=tB[:, 0:n], data1=mask01[:, 0:n], initial=0.0, op0=ALU.add,
                            op1=ALU.mult), reads=[btB, bCONST], writes=[bbs])
                for (d, h) in grp:
                    tA, btA, tB, btB, tC, btC, tD, btD, bs, bbs = bufs[(d, h)]
                    sq_, sk_ = (1.0, -1.0) if d == 0 else (-1.0, 1.0)
                    P.op("scalar", lambda en, bs=bs, tC=tC, sq_=sq_: en.activation(out=tC[:, 0:n], in_=bs[:, 0:n],
                                                                                  func=AF.Exp, scale=sq_),
                         reads=[bbs], writes=[btC])
                    P.op("scalar", lambda en, bs=bs, tD=tD, sk_=sk_: en.activation(out=tD[:, 0:n], in_=bs[:, 0:n],
                                                                                  func=AF.Exp, scale=sk_),
                         reads=[bbs], writes=[btD])
                for (d, h) in grp:
                    tA, btA, tB, btB, tC, btC, tD, btD, bs, bbs = bufs[(d, h)]
                    if not isctx:
                        P.op("vector", lambda en, tC=tC, h=h, d=d: en.tensor_tensor(
                            out=qd[d][0][:, h, 0:n], in0=q_t[:, h, 0:n], in1=tC[:, 0:n], op=ALU.mult),
                            reads=[btC, bq], writes=[qd[d][1]])
                    P.op("vector", lambda en, tA=tA, tD=tD, h=h, d=d: en.scalar_tensor_tensor(
                        out=kx[d][0][:, h, 0:n], in0=tA[:, 0:n], scalar=OML[:, d * 8 + h:d * 8 + h + 1], in1=tD[:, 0:n],
                        op0=ALU.mult, op1=ALU.mult), reads=[btA, btD, bOML], writes=[kx[d][1]])
                    if d == 0:
                        P.op("vector", lambda en, tC=tC, h=h: en.tensor_copy(
                            out=Df[:, h, gc0 + 1:gc0 + 1 + nch], in_=tC[:, 63:n:64]), reads=[btC], writes=[bDf])
                    else:
                        ts_, bts = tsr.next()
                        P.op("vector", lambda en, bs=bs, tB=tB, ts_=ts_: en.tensor_tensor(
                            out=ts_[:, 0:nch], in0=bs[:, 63:n:64], in1=tB[:, 64:n + 1:64], op=ALU.add),
                            reads=[bbs, btB], writes=[bts])
                        for c in range(nch):
                            if isctx:
                                gb = CTX // 64 - 1 - c
                            else:
                                gb = CTX // 64 + (NCHL - 1 - (tl["t0"] // 64 + c))
                            P.op("scalar", lambda en, ts_=ts_, c=c, gb=gb, h=h: en.activation(
                                out=Db[:, h, gb:gb + 1], in_=ts_[:, c:c + 1], func=AF.Exp), reads=[bts], writes=[bDb])
            if isctx:
                run_chain([(c, gc0 + c) for c in range(n // 64)], kx[0][0], kx[0][1], None, None, vt_t, bvt, 0, Yf,
                          bYf, Df, bDf, rings, False)
                run_chain([(c, CTX // 64 - 1 - c) for c in reversed(range(n // 64))], kx[1][0], kx[1][1], None, None,
                          vt_t, bvt, 1, Yb, bYb, Db, bDb, rings, False)
            else:
                of_t, bof = ofr.next()
                run_chain([(c, gc0 + c) for c in range(n // 64)], kx[0][0], kx[0][1], qd[0][0], qd[0][1], vt_t, bvt,
                          0, Yf, bYf, Df, bDf, rings, True, of_t, bof)
                t0 = tl["t0"]
                P.op("sync", lambda en, of_t=of_t, t0=t0, n=n: en.dma_start(out=fm(OF, t0, t0 + n), in_=of_t[:, :, 0:n]),
                     reads=[bof], writes=[bOF], dma=True)
                P.op("sync", lambda en, q=qd[1][0], t0=t0, n=n: en.dma_start(out=fm(QDB, t0, t0 + n), in_=q[:, :, 0:n]),
                     reads=[qd[1][1]], writes=[bQDB], dma=True)
                P.op("sync", lambda en, q=kx[1][0], t0=t0, n=n: en.dma_start(out=fm(KXB, t0, t0 + n), in_=q[:, :, 0:n]),
                     reads=[kx[1][1]], writes=[bKXB], dma=True)

        for i, tl in enumerate(tls):
            p3_tile(i, tl)
        P.barrier()
        P.emit()
        ph.close()

    def phase4():
        T = 256
        NC4 = T // 64
        ph = new_phase()
        Wo, bWo = wview(0, KC, D)
        wload(Wo, bWo, hg_w_out, KC, D)
        xr = Ring("dx", [128, KC, T], F32, 2, ph)
        qdr = Ring("dqd", [128, 8, T], BF16, 2, ph)
        kxr = Ring("dkx", [128, 8, T], BF16, 2, ph)
        sgr = Ring("dsg", [128, 8, T], BF16, 2, ph)
        vtr = Ring("dvt", [64, NC4, D], BF16, 2, ph)
        ofr = Ring("dof", [128, 8, T], F32, 2, ph)
        sqr = Ring("dsq", [128, 8, T], BF16, 1, ph)
        rsr = Ring("drs", [128, 8, T], F32, 1, ph)
        tmpr = Ring("dtmp", [128, T], F32, 3, ph)
        mr = Ring("dm", [128, 8, T], BF16, 1, ph)
        rings = (Ring("datt", [64, 512], BF16, 2, ph), Ring("dkxt", [64, 1024], BF16, 2, ph),
                 Ring("dxb", [128, 8, 128], BF16, 2, ph), Ring("dx32", [128, 8, 128], F32, 2, ph))
        tls = list(reversed(tiles(T, with_ctx=False)))
        st = {}

        def stageA(i):
            tl = tls[i]
            n, t0 = tl["n"], tl["t0"]
            x_t, bx = xr.next()
            qd, bqd = qdr.next()
            kx, bkx = kxr.next()
            sg, bsg = sgr.next()
            vt, bvt = vtr.next()
            of, bof = ofr.next()
            P.op("sync", lambda en: en.dma_start(out=qd[:, :, 0:n], in_=fm(QDB, t0, t0 + n)), reads=[bQDB], writes=[bqd],
                 dma=True)
            P.op("sync", lambda en: en.dma_start(out=kx[:, :, 0:n], in_=fm(KXB, t0, t0 + n)), reads=[bKXB], writes=[bkx],
                 dma=True)
            P.op("sync", lambda en: en.dma_start(
                out=vt[:, 0:n // 64, :], in_=VT.rearrange("(c s) f -> s c f", s=64)[:, t0 // 64:(t0 + n) // 64, :]),
                reads=[bVT], writes=[bvt], dma=True)
            P.op("sync", lambda en: en.dma_start(out=of[:, :, 0:n], in_=fm(OF, t0, t0 + n)), reads=[bOF], writes=[bof],
                 dma=True)
            P.op("sync", lambda en: en.dma_start(out=sg[:, :, 0:n], in_=fm(SG, t0, t0 + n)), reads=[bSG], writes=[bsg],
                 dma=True)
            P.op("sync", lambda en: en.dma_start(out=x_t[:, :, 0:n], in_=fm(XB, t0, t0 + n)), reads=[bXB], writes=[bx],
                 dma=True)
            st[i] = (x_t, bx, qd, bqd, kx, bkx, sg, bsg, vt, bvt, of, bof)

        def stageB(i):
            tl = tls[i]
            n, t0 = tl["n"], tl["t0"]
            x_t, bx, qd, bqd, kx, bkx, sg, bsg, vt, bvt, of, bof = st.pop(i)
            run_chain([(c, CTX // 64 + (NCHL - 1 - (t0 // 64 + c))) for c in reversed(range(n // 64))], kx, bkx, qd,
                      bqd, vt, bvt, 1, Yb, bYb, Db, bDb, rings, True, of, bof, add_prev=True)
            if i + 1 < len(tls):
                stageA(i + 1)
            sq_t, bsq = sqr.next()
            P.op("gpsimd", lambda en: en.tensor_tensor(out=sq_t[:, :, 0:n], in0=of[:, :, 0:n], in1=of[:, :, 0:n],
                                                       op=ALU.mult), reads=[bof], writes=[bsq])
            rs, brs = rsr.next()
            for h in range(8):
                pst, bpst = ps_next()
                P.op("tensor", lambda en, pst=pst, h=h: en.matmul(pst[:, 0:n], ones_bf[:], sq_t[:, h, 0:n], start=True,
                                                                 stop=True), reads=[bsq, bCONST], writes=[bpst])
                P.op("scalar", lambda en, pst=pst, h=h: en.activation(out=rs[:, h, 0:n], in_=pst[:, 0:n], func=AF.Sqrt,
                                                                     bias=EPS, scale=1.0 / 128), reads=[bpst],
                     writes=[brs])
            P.op("vector", lambda en: en.reciprocal(out=rs[:, :, 0:n], in_=rs[:, :, 0:n]), reads=[brs], writes=[brs])
            m_t, bm = mr.next()
            for h in range(8):
                tmp, btmp = tmpr.next()
                P.op("vector", lambda en, h=h, tmp=tmp: en.scalar_tensor_tensor(
                    out=tmp[:, 0:n], in0=of[:, h, 0:n], scalar=vcol("hnw", h), in1=rs[:, h, 0:n], op0=ALU.mult,
                    op1=ALU.mult), reads=[bof, brs, bVv], writes=[btmp])
                P.op("gpsimd", lambda en, h=h, tmp=tmp: en.tensor_tensor(
                    out=m_t[:, h, 0:n], in0=tmp[:, 0:n], in1=sg[:, h, 0:n], op=ALU.mult), reads=[btmp, bsg],
                    writes=[bm])
            for oc in range(KC):
                py, bpy = std_mm(Wo, bWo, KC, oc * 128, lambda k: m_t[:, k, 0:n], n, [bm])
                g_ap = mvcol(1, 0, G_M, oc)
                P.op("vector", lambda en, py=py, oc=oc, g_ap=g_ap: en.scalar_tensor_tensor(
                    out=x_t[:, oc, 0:n], in0=py[:, 0:n], scalar=g_ap, in1=x_t[:, oc, 0:n], op0=ALU.mult, op1=ALU.add),
                    reads=[bpy, bMV, bx], writes=[bx])
            P.op("sync", lambda en: en.dma_start(out=fm(XC, t0, t0 + n), in_=x_t[:, :, 0:n]), reads=[bx], writes=[bXC],
                 dma=True)

        stageA(0)
        for i in range(len(tls)):
            stageB(i)
        P.barrier()
        P.emit()
        ph.close()

    phases = [("p1a", phase1a), ("p1b", phase1b), ("p2", lambda: ffn_phase(0, XA, bXA, True, XB, bXB, False)),
              ("p3", phase3), ("p4", phase4), ("p5", lambda: ffn_phase(1, XC, bXC, False, outT, bOUT, True))]
    for name, f in phases:
        f()
        if stop_after == name:
            break
    P.barrier()
    P.emit()
    es.close()
    return nc


def make_in_maps(inputs, S):
    f = lambda a: np.ascontiguousarray(np.asarray(a, dtype=np.float32))
    x = f(inputs["x"])
    B = x.shape[0]
    pos = pos_table(S)
    gm_w_s = f(inputs["gm_w_s"])[0]
    wsT = np.ascontiguousarray(gm_w_s.transpose(2, 0, 1).reshape(128, 1024))
    shared = {
        "posT": pos,
        "bvrow": f(inputs["gm_b_in"])[0, GMH:].reshape(1, GMH).copy(),
        "lnbrow": f(inputs["gm_ln_b"])[0].reshape(1, GMH).copy(),
        "bsrow": f(inputs["gm_b_s"])[0].reshape(1, 1024).copy(),
        "wsT": wsT,
        "ada_w": f(inputs["ada_w"]),
        "gm_w_in": f(inputs["gm_w_in"])[0],
        "gm_w_out": f(inputs["gm_w_out"])[0],
        "hg_w_in": f(inputs["hg_w_in"])[0],
        "hg_w_out": f(inputs["hg_w_out"])[0],
        "ffn_w_in": f(inputs["ffn_w_in"]),
        "ffn_w_out": f(inputs["ffn_w_out"]),
    }
    hg_lb = f(inputs["hg_lb"])
    maps = []
    for b in range(B):
        vec = np.zeros((128, NV), np.float32)

        def put(name, arr):
            o, w = VEC_LAYOUT[name]
            assert arr.shape == (128, w), (name, arr.shape)
            vec[:, o:o + w] = arr

        cc = np.stack([_col(inputs["c"][b]), _col(inputs["c_ctx"])], axis=-1).reshape(128, 16)
        put("cc", cc)
        for l in range(2):
            put("ada_b%d" % l, _col(inputs["ada_b"][l]))
            put("nmw%d" % l, _col(inputs["norm_mix_w"][l]))
            put("nfw%d" % l, _col(inputs["norm_ffn_w"][l]))
            put("lb%d" % l, np.concatenate([_col(hg_lb[l, 0]), _col(hg_lb[l, 1])], axis=1))
        put("bu", _col(f(inputs["gm_b_in"])[0, :GMH]))
        put("lng", _col(inputs["gm_ln_g"][0]))
        put("hnw", _col(inputs["hg_norm_w"][0]))
        put("fnw", _col(inputs["final_norm_w"]))
        m = dict(shared)
        m["xT"] = np.ascontiguousarray(x[b, :S].T)
        m["ctxT"] = np.ascontiguousarray(f(inputs["ctx"])[b].T)
        m["vecs"] = vec
        maps.append(m)
    return maps


def kernel(**inputs):
    S = inputs["x"].shape[1]
    nc = build(S)
    maps = make_in_maps(inputs, S)
    res = run_bass_kernel_spmd(nc, maps, core_ids=list(range(len(maps))))
    out = np.stack([np.ascontiguousarray(r["outT"].T) for r in res.results], axis=0)
    return out.astype(np.float32)
```

```python
import contextlib
import numpy as np
import concourse.bass as bass
import concourse.mybir as mybir
from concourse.alu_op_type import AluOpType as ALU
from concourse.bass_utils import run_bass_kernel_spmd

F32 = mybir.dt.float32
BF16 = mybir.dt.bfloat16
AF = mybir.ActivationFunctionType
ENGS = ["tensor", "vector", "scalar", "gpsimd", "sync"]

D = 1024
KC = 8
CTX = 256
SEQ = 8192
GRID_W = 64
EPS = 1e-6
GMH = 3072
DFF = 2816
NFC = 22
PE_PARTIAL_CHAIN = True
STRICT_ENGS = {"vector", "scalar", "gpsimd", "sync"}


class Buf:
    __slots__ = ("name", "last_w", "readers", "dcount", "sem")

    def __init__(self, name):
        self.name = name
        self.last_w = None
        self.readers = []
        self.dcount = 0
        self.sem = None


class Prog:
    def __init__(self, nc, es):
        self.nc = nc
        self.es = es
        self.ops = {e: [] for e in ENGS}
        self.nemit = {e: 0 for e in ENGS}
        self.seen = {e: {} for e in ENGS}
        self.esem = {e: es.enter_context(nc.semaphore("e_" + e)) for e in ENGS}
        self.ecount = {e: 0 for e in ENGS}
        self.dbufs = []
        self.allbufs = []
        self.prev_chained = True

    def buf(self, name):
        b = Buf(name)
        self.allbufs.append(b)
        return b

    def _need(self, eng, tok):
        key, val = tok
        if self.seen[eng].get(key, -1) >= val:
            return False
        self.seen[eng][key] = val
        return True

    def op(self, eng, fn, reads=(), writes=(), dma=False, chain=True, nowaw=False):
        idx = len(self.ops[eng])
        deps = []
        for b in reads:
            if b.last_w is not None:
                deps.append(b.last_w)
        for b in writes:
            if b.last_w is not None and not (nowaw and b.last_w[0] == ("d", id(b))):
                deps.append(b.last_w)
            deps.extend(b.readers)
        waits = []
        for tok in deps:
            key, val = tok
            if key == ("e", "tensor") and eng == "tensor" and not dma:
                continue
            if self._need(eng, tok):
                waits.append(tok)
        rec = {"fn": fn, "waits": waits, "inc": False, "dma": None, "chain": chain}
        if dma:
            dst = writes[0]
            if dst.sem is None:
                dst.sem = self.es.enter_context(self.nc.semaphore("d%d_%s" % (len(self.dbufs), dst.name)))
                self.dbufs.append(dst)
            dst.dcount += 16
            mytok = (("d", id(dst)), dst.dcount)
            rec["dma"] = dst
        else:
            mytok = (("e", eng), idx)
        self.ops[eng].append(rec)
        for tok in waits:
            if tok[0][0] == "e":
                self.ops[tok[0][1]][tok[1]]["inc"] = True
        for b in reads:
            b.readers = [t for t in b.readers if t[0] != mytok[0]] + [mytok]
        for b in writes:
            b.last_w = mytok
            b.readers = []
        return mytok

    def barrier(self):
        bars = []
        for e in ENGS:
            b = Buf("bar_" + e)
            rd = list(self.dbufs) if e in ("sync", "gpsimd") else []
            self.op(e, lambda en: en.nop(), reads=rd, writes=[b])
            bars.append(b)
        for e in ENGS:
            self.op(e, lambda en: en.nop(), reads=bars)

    def emit(self):
        nc = self.nc
        sigval = getattr(self, "sigval", {})
        self.sigval = sigval
        for e in ENGS:
            for i in range(self.nemit[e], len(self.ops[e])):
                r = self.ops[e][i]
                if r["inc"] and r["dma"] is None:
                    self.ecount[e] += 1
                    sigval[(e, i)] = self.ecount[e]
        dsem = {id(b): b.sem for b in self.dbufs}
        with nc.Block() as block:
            for e in ENGS:
                lo, hi = self.nemit[e], len(self.ops[e])
                if hi == lo:
                    continue

                def run(en, e=e, lo=lo, hi=hi):
                    for i in range(lo, hi):
                        r = self.ops[e][i]
                        for key, val in r["waits"]:
                            if key[0] == "e":
                                en.wait_ge(self.esem[key[1]], sigval[(key[1], val)])
                            else:
                                en.wait_ge(dsem[key[1]], val)
                        strict = (e in STRICT_ENGS) or (e == "tensor" and PE_PARTIAL_CHAIN and
                                                        (r["chain"] or self.prev_chained))
                        if e == "tensor" and r["inc"] and r["dma"] is None:
                            self.prev_chained = r["chain"]
                        if strict and r["inc"] and r["dma"] is None and sigval[(e, i)] > 1:
                            en.wait_ge(self.esem[e], sigval[(e, i)] - 1)
                        ins = r["fn"](en)
                        if r["dma"] is not None:
                            ins.then_inc(r["dma"].sem, 16)
                        elif r["inc"]:
                            ins.then_inc(self.esem[e], 1)

                getattr(block, e)(run)
                self.nemit[e] = hi


def _col(v):
    v = np.asarray(v, np.float32).reshape(-1, 128)
    return np.ascontiguousarray(v.T)


VEC_LAYOUT = {}
_off = 0
for _n, _w in [("cc", 16), ("ada_b0", 48), ("ada_b1", 48), ("nmw0", 8), ("nmw1", 8), ("nfw0", 8), ("nfw1", 8),
               ("bu", 24), ("lng", 24), ("lb0", 16), ("lb1", 16), ("hnw", 8), ("fnw", 8)]:
    VEC_LAYOUT[_n] = (_off, _w)
    _off += _w
NV = _off


def pos_table(n):
    half = D // 2

    def sincos(pos, dim):
        h = dim // 2
        omega = (1.0 / (10000.0 ** (np.arange(h, dtype=np.float32) / np.float32(h)))).astype(np.float32)
        ang = pos.astype(np.float32)[:, None] * omega[None, :]
        return np.concatenate([np.sin(ang), np.cos(ang)], axis=-1).astype(np.float32)

    rows = n // GRID_W
    rc = sincos(np.arange(rows), half)
    cc = sincos(np.arange(GRID_W), half)
    code = np.concatenate([np.broadcast_to(rc[:, None, :], (rows, GRID_W, half)),
                           np.broadcast_to(cc[None, :, :], (rows, GRID_W, half))], axis=-1)
    return np.ascontiguousarray(code.reshape(rows * GRID_W, D).T.astype(np.float32))


def build(S=SEQ, stop_after=None, dbg_out=None):
    nc = bass.Bass("TRN2", target_bir_lowering=False)
    es = contextlib.ExitStack()
    P = Prog(nc, es)

    def din(name, shape, dt=F32):
        return nc.dram_tensor(name, list(shape), dt, kind="ExternalInput").ap()

    def dscr(name, shape, dt):
        kind = "ExternalOutput" if (dbg_out and name in dbg_out) else "Internal"
        return nc.dram_tensor(name, list(shape), dt, kind=kind).ap()

    xT = din("xT", [D, S])
    ctxT = din("ctxT", [D, CTX])
    posT = din("posT", [D, S])
    vecs = din("vecs", [128, NV])
    bvrow = din("bvrow", [1, GMH])
    lnbrow = din("lnbrow", [1, GMH])
    bsrow = din("bsrow", [1, 1024])
    wsT_d = din("wsT", [128, 1024])
    ada_w = din("ada_w", [2, D, 6 * D])
    gm_w_in = din("gm_w_in", [D, 2 * GMH])
    gm_w_out = din("gm_w_out", [GMH, D])
    hg_w_in = din("hg_w_in", [D, 5 * D])
    hg_w_out = din("hg_w_out", [D, D])
    ffn_w_in = din("ffn_w_in", [2, D, 2 * DFF])
    ffn_w_out = din("ffn_w_out", [2, DFF, D])
    outT = nc.dram_tensor("outT", [D, S], F32, kind="ExternalOutput").ap()

    V1 = dscr("V1", [GMH, S + CTX], BF16)
    XA = dscr("XA", [D, S + CTX], F32)
    XB = dscr("XB", [D, S + CTX], F32)
    XC = dscr("XC", [D, S], F32)
    OF = dscr("OF", [D, S], F32)
    QDB = dscr("QDB", [D, S], BF16)
    KXB = dscr("KXB", [D, S], BF16)
    SG = dscr("SG", [D, S], BF16)
    VT = dscr("VT", [S, D], BF16)
    bV1, bXA, bXB, bXC, bOF, bQDB, bKXB, bSG, bVT, bOUT = [P.buf(n) for n in
                                                            ["V1", "XA", "XB", "XC", "OF", "QDB", "KXB", "SG", "VT", "OUT"]]
    bIN = P.buf("inputs")

    def fm(ap3, c0, c1):
        return ap3.rearrange("(k p) t -> p k t", p=128)[:, :, c0:c1]

    ARENA_BYTES = 184 * 1024
    AR = es.enter_context(nc.sbuf_tensor("AR", [128, ARENA_BYTES // 2], BF16))
    aoff = [0]

    class _Phase:
        def close(self):
            pass

    def new_phase():
        aoff[0] = 0
        return _Phase()

    def sb(name, shape, dt, stack=None):
        if stack is None:
            return es.enter_context(nc.sbuf_tensor(name, list(shape), dt))
        esz = 4 if dt == F32 else 2
        nel = 1
        for d_ in shape[1:]:
            nel *= d_
        nb = (nel * esz + 63) // 64 * 64
        assert aoff[0] + nb <= ARENA_BYTES, ("arena overflow", name, aoff[0], nb)
        v = AR[:, aoff[0] // 2:(aoff[0] + nb) // 2]
        aoff[0] += nb
        if dt == F32:
            v = v.bitcast(F32)
        v = v[:, 0:nel]
        if len(shape) == 3:
            v = v.rearrange("p (a b) -> p a b", a=shape[1])
        if shape[0] < 128:
            v = v[0:shape[0]]
        return v

    wcnt = [0]

    def wview(e0, kc, n):
        wcnt[0] += 1
        ap = sb("w%d" % wcnt[0], [128, kc, n], BF16, True)
        return ap, [P.buf("w%d" % wcnt[0])]

    Vv = sb("Vv", [128, NV], F32)
    bVv = P.buf("Vv")
    MV = sb("MV", [128, 2 * 2 * 6 * 8], F32)
    bMV = P.buf("MV")
    OML = sb("OML", [128, 32], F32)
    bOML = P.buf("OML")
    ones_bf = sb("ones_bf", [128, 128], BF16)
    ident_bf = sb("ident_bf", [128, 128], BF16)
    mask01 = sb("mask01", [128, 256], BF16)
    maskF = sb("maskF", [64, 512], BF16)
    maskB = sb("maskB", [64, 512], BF16)
    bCONST = P.buf("const")
    NCHL = S // 64
    NCH = NCHL + CTX // 64
    Df = sb("Df", [128, 8, NCH + 1], F32)
    Db = sb("Db", [128, 8, NCH + 1], F32)
    bDf = P.buf("Df")
    bDb = P.buf("Db")
    Yf = sb("Yf", [128, 8, 128], F32)
    Yb = sb("Yb", [128, 8, 128], F32)
    bYf = [P.buf("Yf%d" % h) for h in range(8)]
    bYb = [P.buf("Yb%d" % h) for h in range(8)]

    psum = [es.enter_context(nc.psum_tensor("ps%d" % i, [128, 512], F32)) for i in range(8)]
    bps = [P.buf("ps%d" % i) for i in range(8)]
    psc = [0]

    def ps_next():
        i = psc[0] % 8
        psc[0] += 1
        return psum[i], bps[i]

    def mvcol(l, who, kind, k):
        c = ((l * 2 + who) * 6 + kind) * 8 + k
        return MV[:, c:c + 1]

    A_M, S_M, G_M, A_F, S_F, G_F = range(6)

    def vcol(name, k=0, n=1):
        o, w = VEC_LAYOUT[name]
        return Vv[:, o + k:o + k + n]

    def wload(dst_ap, dst_bufs, src2d, kc, n):
        step = 2048
        for k in range(kc):
            for c0 in range(0, n, step):
                c1 = min(n, c0 + step)
                P.op("gpsimd", lambda en, k=k, c0=c0, c1=c1: en.dma_start(
                    out=dst_ap[:, k, c0:c1], in_=src2d[k * 128:(k + 1) * 128, c0:c1]),
                    reads=[bIN], writes=dst_bufs, dma=True, nowaw=True)

    ph = new_phase()
    ident_f = sb("ident_f", [128, 128], F32, ph)
    P.op("sync", lambda en: en.dma_start(out=Vv[:], in_=vecs[:, :]), reads=[bIN], writes=[bVv], dma=True)

    for t_, b_, v_ in [(ones_bf, bCONST, 1.0), (mask01, bCONST, 1.0), (Df, bDf, 1.0), (Db, bDb, 1.0)]:
        P.op("vector", lambda en, t_=t_, v_=v_: en.memset(t_[:], v_), writes=[b_])
    P.op("vector", lambda en: en.memset(mask01[:, 0:256:64], 0.0), writes=[bCONST])
    P.op("vector", lambda en: en.memset(Yf[:], 0.0), writes=bYf)
    P.op("vector", lambda en: en.memset(Yb[:], 0.0), writes=bYb)
    bMK = P.buf("masks")
    P.op("gpsimd", lambda en: en.memset(ident_f[:], 1.0), writes=[bMK])
    P.op("gpsimd", lambda en: en.affine_select(out=ident_f[:], in_=ident_f[:], pattern=[[-1, 128]],
                                               compare_op=ALU.is_equal, fill=0.0, base=0, channel_multiplier=1),
         writes=[bMK])
    P.op("gpsimd", lambda en: en.memset(maskF[:], 1.0), writes=[bMK])
    P.op("gpsimd", lambda en: en.memset(maskB[:], 1.0), writes=[bMK])
    mf_ = maskF[:].rearrange("p (h t) -> p h t", h=8)
    mb_ = maskB[:].rearrange("p (h t) -> p h t", h=8)
    P.op("gpsimd", lambda en: en.affine_select(out=mf_, in_=mf_, pattern=[[0, 8], [1, 64]], compare_op=ALU.is_ge,
                                               fill=0.0, base=0, channel_multiplier=-1), writes=[bMK])
    P.op("gpsimd", lambda en: en.affine_select(out=mb_, in_=mb_, pattern=[[0, 8], [-1, 64]], compare_op=ALU.is_ge,
                                               fill=0.0, base=0, channel_multiplier=1), writes=[bMK])
    P.op("vector", lambda en: en.tensor_copy(out=ident_bf[:], in_=ident_f[:]), reads=[bMK], writes=[bCONST])
    s_bf = sb("s_bf", [128, 16], BF16, ph)
    bs_bf = P.buf("s_bf")
    P.op("scalar", lambda en: en.activation(out=s_bf[:], in_=vcol("cc", 0, 16), func=AF.Silu),
         reads=[bVv], writes=[bs_bf])
    lbd = sb("lbd", [128, 16], F32, ph)
    blbd = P.buf("lbd")
    P.op("vector", lambda en: en.tensor_tensor(out=lbd[:], in0=vcol("lb0", 0, 16), in1=vcol("lb1", 0, 16),
                                               op=ALU.subtract), reads=[bVv], writes=[blbd])
    P.op("scalar", lambda en: en.activation(out=OML[:, 0:16], in_=lbd[:], func=AF.Sigmoid),
         reads=[blbd], writes=[bOML])
    P.op("vector", lambda en: en.tensor_scalar(out=OML[:, 16:32], in0=OML[:, 0:16], scalar1=-1.0, scalar2=None,
                                               op0=ALU.mult), reads=[bOML], writes=[bOML])
    modt = sb("modt", [128, 96], F32, ph)
    bmodt = P.buf("modt")
    WA, bWA = wview(0, KC, 6 * D)
    for l in range(2):
        wload(WA, bWA, ada_w[l], KC, 6 * D)
        pst, bpst = ps_next()

        def _mod(en, WA=WA, pst=pst):
            for dc in range(48):
                for k in range(KC):
                    ins = en.matmul(pst[:, dc * 2:dc * 2 + 2], WA[:, k, dc * 128:(dc + 1) * 128],
                                    s_bf[:, 2 * k:2 * k + 2], start=(k == 0), stop=(k == KC - 1))
            return ins

        P.op("tensor", _mod, reads=bWA + [bs_bf], writes=[bpst])
        for who in range(2):
            P.op("vector", lambda en, pst=pst, who=who, l=l: en.tensor_tensor(
                out=modt[:, who * 48:(who + 1) * 48], in0=pst[:, who:96:2], in1=vcol("ada_b%d" % l, 0, 48), op=ALU.add),
                reads=[bpst, bVv], writes=[bmodt])

            def _mv(en, who=who, l=l):
                m = modt[:, who * 48:(who + 1) * 48]
                base = ((l * 2 + who) * 6) * 8
                en.scalar_tensor_tensor(out=MV[:, base + A_M * 8:base + A_M * 8 + 8], in0=m[:, 8:16], scalar=1.0,
                                        in1=vcol("nmw%d" % l, 0, 8), op0=ALU.add, op1=ALU.mult)
                en.tensor_copy(out=MV[:, base + S_M * 8:base + S_M * 8 + 8], in_=m[:, 0:8])
                en.tensor_copy(out=MV[:, base + G_M * 8:base + G_M * 8 + 8], in_=m[:, 16:24])
                en.scalar_tensor_tensor(out=MV[:, base + A_F * 8:base + A_F * 8 + 8], in0=m[:, 32:40], scalar=1.0,
                                        in1=vcol("nfw%d" % l, 0, 8), op0=ALU.add, op1=ALU.mult)
                en.tensor_copy(out=MV[:, base + S_F * 8:base + S_F * 8 + 8], in_=m[:, 24:32])
                return en.tensor_copy(out=MV[:, base + G_F * 8:base + G_F * 8 + 8], in_=m[:, 40:48])

            P.op("vector", _mv, reads=[bmodt, bVv], writes=[bMV])
    P.barrier()
    P.emit()
    ph.close()

    def tiles(T, with_ctx=True, lat=True):
        out = []
        if with_ctx:
            out.append(dict(ctx=True, t0=0, n=CTX, who=1))
        if lat:
            for t0 in range(0, S, T):
                out.append(dict(ctx=False, t0=t0, n=min(T, S - t0), who=0))
        return out

    class Ring:
        def __init__(self, name, shape, dt, n, stack):
            self.t = [sb("%s%d" % (name, i), shape, dt, stack) for i in range(n)]
            self.b = [P.buf("%s%d" % (name, i)) for i in range(n)]
            self.i = 0

        def next(self):
            j = self.i % len(self.t)
            self.i += 1
            return self.t[j], self.b[j]

    def norm_stage(x_t, bx, n, l, who, kinds, out_t, bout, sq_ring, rst_ring, tmp_ring, sq_eng="gpsimd",
                   custom_a=None, out_f32=False, lnexp=False, defer=None, defer1=None):
        sq_t, bsq = sq_ring.next()

        def _sq():
            P.op(sq_eng, lambda en: en.tensor_tensor(out=sq_t[:, :, 0:n], in0=x_t[:, :, 0:n], in1=x_t[:, :, 0:n],
                                                     op=ALU.mult), reads=[bx], writes=[bsq])

        if defer1 is not None:
            defer1.append(_sq)
        else:
            _sq()
        if defer is not None:
            defer.append(lambda: norm_post(x_t, bx, n, l, who, kinds, out_t, bout, sq_t, bsq, rst_ring, tmp_ring,
                                           custom_a, out_f32, lnexp))
        else:
            norm_post(x_t, bx, n, l, who, kinds, out_t, bout, sq_t, bsq, rst_ring, tmp_ring, custom_a, out_f32, lnexp)

    def norm_post(x_t, bx, n, l, who, kinds, out_t, bout, sq_t, bsq, rst_ring, tmp_ring, custom_a, out_f32, lnexp):
        pst, bpst = ps_next()

        def _ss(en):
            for k in range(KC):
                ins = en.matmul(pst[:, 0:n], ones_bf[:], sq_t[:, k, 0:n], start=(k == 0), stop=(k == KC - 1))
            return ins

        P.op("tensor", _ss, reads=[bsq, bCONST], writes=[bpst], chain=False)
        rst, brst = rst_ring.next()
        if lnexp:
            P.op("scalar", lambda en: en.activation(out=rst[:, 0:n], in_=pst[:, 0:n], func=AF.Ln, bias=EPS,
                                                    scale=1.0 / D), reads=[bpst], writes=[brst])
            P.op("scalar", lambda en: en.activation(out=rst[:, 0:n], in_=rst[:, 0:n], func=AF.Exp, scale=-0.5),
                 reads=[brst], writes=[brst])
        else:
            P.op("scalar", lambda en: en.activation(out=rst[:, 0:n], in_=pst[:, 0:n], func=AF.Sqrt, bias=EPS,
                                                    scale=1.0 / D), reads=[bpst], writes=[brst])
            P.op("vector", lambda en: en.reciprocal(out=rst[:, 0:n], in_=rst[:, 0:n]), reads=[brst], writes=[brst])
        for k in range(KC):
            a_ap = custom_a(k) if custom_a else mvcol(l, who, kinds[0], k)
            if out_f32:
                P.op("vector", lambda en, k=k, a_ap=a_ap: en.scalar_tensor_tensor(
                    out=out_t[:, k, 0:n], in0=x_t[:, k, 0:n], scalar=a_ap, in1=rst[:, 0:n], op0=ALU.mult,
                    op1=ALU.mult), reads=[bx, brst, bMV, bVv], writes=[bout])
                continue
            tmp, btmp = tmp_ring.next()
            P.op("vector", lambda en, k=k, a_ap=a_ap, tmp=tmp: en.scalar_tensor_tensor(
                out=tmp[:, 0:n], in0=x_t[:, k, 0:n], scalar=a_ap, in1=rst[:, 0:n], op0=ALU.mult, op1=ALU.mult),
                reads=[bx, brst, bMV], writes=[btmp])
            s_ap = mvcol(l, who, kinds[1], k)
            P.op("vector", lambda en, k=k, s_ap=s_ap, tmp=tmp: en.tensor_scalar(
                out=out_t[:, k, 0:n], in0=tmp[:, 0:n], scalar1=s_ap, scalar2=None, op0=ALU.add),
                reads=[btmp, bMV], writes=[bout])

    def src_cols(tl, lat_ap, ctx_ap_or_none, S_off=True):
        if tl["ctx"]:
            return (S, S + CTX)
        return (tl["t0"], tl["t0"] + tl["n"])

    def std_mm(W, bW, kc, col0, rhs_fn, n, reads):
        pst, bpst = ps_next()

        def _mm(en):
            for k in range(kc):
                ins = en.matmul(pst[:, 0:n], W[:, k, col0:col0 + 128], rhs_fn(k), start=(k == 0), stop=(k == kc - 1))
            return ins

        P.op("tensor", _mm, reads=bW + reads, writes=[bpst], chain=False)
        return pst, bpst

    def ffn_phase(l, src, bsrc, src_has_ctx, dst, bdst, final):
        T = 256
        ph = new_phase()
        W1, bW1 = wview(0, KC, 2 * DFF)
        W2, bW2 = wview(KC * 2 * DFF, NFC, D)
        wload(W1, bW1, ffn_w_in[l], KC, 2 * DFF)
        wload(W2, bW2, ffn_w_out[l], NFC, D)
        xr = Ring("fx", [128, KC, T], F32, 2, ph)
        hr = Ring("fh", [128, KC, T], BF16, 2, ph)
        sqr = Ring("fsq", [128, KC, T], BF16, 1, ph)
        rstr = Ring("frst", [128, T], F32, 2, ph)
        tmpr = Ring("ftmp", [128, T], F32, 2, ph)
        mr = Ring("fm", [128, NFC, T], BF16, 1, ph)
        sar = Ring("fsa", [128, T], F32, 3, ph)
        tls = tiles(T, with_ctx=src_has_ctx)
        st = {}

        def stageA(i, defer=None, defer1=None):
            tl = tls[i]
            n = tl["n"]
            x_t, bx = xr.next()
            h_t, bh = hr.next()
            c0, c1 = src_cols(tl, None, None)
            P.op("sync", lambda en: en.dma_start(out=x_t[:, :, 0:n], in_=fm(src, c0, c1)), reads=[bsrc], writes=[bx],
                 dma=True)
            norm_stage(x_t, bx, n, l, tl["who"], (A_F, S_F), h_t, bh, sqr, rstr, tmpr, defer=defer, defer1=defer1)
            st[i] = (x_t, bx, h_t, bh)

        def stageB(i):
            tl = tls[i]
            n = tl["n"]
            x_t, bx, h_t, bh = st.pop(i)
            m_t, bm = mr.next()
            pend = []
            pend1 = []
            if i + 1 < len(tls):
                stageA(i + 1, pend, pend1)
            for dc in range(NFC):
                pa, bpa = std_mm(W1, bW1, KC, dc * 128, lambda k: h_t[:, k, 0:n], n, [bh])
                pb, bpb = std_mm(W1, bW1, KC, DFF + dc * 128, lambda k: h_t[:, k, 0:n], n, [bh])
                sa, bsa = sar.next()
                P.op("scalar", lambda en, pa=pa, sa=sa: en.activation(out=sa[:, 0:n], in_=pa[:, 0:n], func=AF.Silu),
                     reads=[bpa], writes=[bsa])
                P.op("vector", lambda en, pb=pb, sa=sa, dc=dc: en.tensor_tensor(
                    out=m_t[:, dc, 0:n], in0=pb[:, 0:n], in1=sa[:, 0:n], op=ALU.mult), reads=[bpb, bsa], writes=[bm])
                if dc == 7:
                    for f_ in pend1:
                        f_()
                if dc == 15:
                    for f_ in pend:
                        f_()
            for oc in range(KC):
                py, bpy = std_mm(W2, bW2, NFC, oc * 128, lambda k: m_t[:, k, 0:n], n, [bm])
                g_ap = mvcol(l, tl["who"], G_F, oc)
                P.op("vector", lambda en, py=py, oc=oc, g_ap=g_ap: en.scalar_tensor_tensor(
                    out=x_t[:, oc, 0:n], in0=py[:, 0:n], scalar=g_ap, in1=x_t[:, oc, 0:n], op0=ALU.mult, op1=ALU.add),
                    reads=[bpy, bMV, bx], writes=[bx])
            if final:
                o_t, bo = x_t, bx
                norm_stage(x_t, bx, n, l, 0, None, o_t, bo, sqr, rstr, tmpr,
                           custom_a=lambda k: vcol("fnw", k), out_f32=True)
                P.op("sync", lambda en: en.dma_start(out=fm(dst, tl["t0"], tl["t0"] + n), in_=o_t[:, :, 0:n]),
                     reads=[bo], writes=[bdst], dma=True)
            else:
                c0, c1 = src_cols(tl, None, None)
                P.op("sync", lambda en: en.dma_start(out=fm(dst, c0, c1), in_=x_t[:, :, 0:n]), reads=[bx],
                     writes=[bdst], dma=True)

        stageA(0)
        for i in range(len(tls)):
            stageB(i)
        P.barrier()
        P.emit()
        ph.close()

    pos_eng = ["gpsimd"]

    def load_x0(tl, x_t, bx, pos_ring, defer1=None):
        n = tl["n"]
        if tl["ctx"]:
            P.op("sync", lambda en: en.dma_start(out=x_t[:, :, 0:n], in_=fm(ctxT, 0, CTX)), reads=[bIN], writes=[bx],
                 dma=True)
        else:
            p_t, bp = pos_ring.next()
            t0 = tl["t0"]
            P.op("sync", lambda en: en.dma_start(out=x_t[:, :, 0:n], in_=fm(xT, t0, t0 + n)), reads=[bIN],
                 writes=[bx], dma=True)
            P.op("sync", lambda en: en.dma_start(out=p_t[:, :, 0:n], in_=fm(posT, t0, t0 + n)), reads=[bIN],
                 writes=[bp], dma=True)
            def _pa():
                P.op(pos_eng[0], lambda en: en.tensor_tensor(out=x_t[:, :, 0:n], in0=x_t[:, :, 0:n],
                                                             in1=p_t[:, :, 0:n], op=ALU.add), reads=[bx, bp],
                     writes=[bx])

            if defer1 is not None:
                defer1.append(_pa)
            else:
                _pa()

    def phase1a():
        T = 256
        ph = new_phase()
        Wv, bWv = wview(0, KC, GMH)
        wload(Wv, bWv, gm_w_in[:, GMH:2 * GMH], KC, GMH)
        wsb = sb("wsb", [128, 1024], BF16, ph)
        bwsb = P.buf("wsb")
        P.op("gpsimd", lambda en: en.dma_start(out=wsb[:], in_=wsT_d[:, :]), reads=[bIN], writes=[bwsb], dma=True)
        bvb = sb("bvb", [128, GMH], BF16, ph)
        bbvb = P.buf("bvb")
        P.op("vector", lambda en: en.memset(bvb[:], 0.0), writes=[bbvb])
        P.op("gpsimd", lambda en: en.dma_start(out=bvb[0:1, :], in_=bvrow[:, :], max_dma_last_dim=4096), reads=[bIN],
             writes=[bbvb], dma=True)
        Ct = sb("Ct", [128, 24, 128], F32, ph)
        bCt = P.buf("Ct")
        amark = aoff[0]
        Rt = sb("Rt", [2, 1024], F32, ph)
        Lt = sb("Lt", [2, GMH], F32, ph)
        bRt = P.buf("Rt")
        bLt = P.buf("Lt")
        ws32 = sb("ws32", [128, 1024], F32, ph)
        bws32 = P.buf("ws32")
        P.op("sync", lambda en: en.dma_start(out=ws32[:], in_=wsT_d[:, :]), reads=[bIN], writes=[bws32], dma=True)
        ones32 = sb("ones32", [128, 2], F32, ph)
        bo32 = P.buf("ones32")
        P.op("vector", lambda en: en.memset(ones32[:], 1.0), writes=[bo32])
        P.op("vector", lambda en: en.memset(Lt[:], 1.0), writes=[bLt])
        P.op("sync", lambda en: en.dma_start(out=Lt[0:1, :], in_=lnbrow[:, :]), reads=[bIN], writes=[bLt], dma=True)
        P.op("sync", lambda en: en.dma_start(out=Rt[1:2, :], in_=bsrow[:, :]), reads=[bIN], writes=[bRt], dma=True)
        for hb in range(2):
            pst, bpst = ps_next()
            P.op("tensor", lambda en, pst=pst, hb=hb: en.matmul(pst[0:1, :], ones32[:, 0:1],
                                                               ws32[:, hb * 512:(hb + 1) * 512], start=True, stop=True),
                 reads=[bws32, bo32], writes=[bpst])
            P.op("vector", lambda en, pst=pst, hb=hb: en.tensor_copy(out=Rt[0:1, hb * 512:(hb + 1) * 512],
                                                                     in_=pst[0:1, :]), reads=[bpst], writes=[bRt])
        for dc in range(24):
            g = dc // 3
            pst, bpst = ps_next()
            P.op("tensor", lambda en, pst=pst, dc=dc, g=g: en.matmul(
                pst[:, 0:128], Lt[0:2, dc * 128:(dc + 1) * 128], Rt[0:2, g * 128:(g + 1) * 128], start=True, stop=True),
                reads=[bLt, bRt], writes=[bpst])
            P.op("vector", lambda en, pst=pst, dc=dc: en.tensor_copy(out=Ct[:, dc, :], in_=pst[:, 0:128]),
                 reads=[bpst], writes=[bCt])

        P.barrier()
        aoff[0] = amark
        xr = Ring("ax", [128, KC, T], F32, 2, ph)
        posr = Ring("apos", [128, KC, T], F32, 1, ph)
        hr = Ring("ah", [128, KC, T], BF16, 2, ph)
        sqr = Ring("asq", [128, KC, T], BF16, 1, ph)
        rstr = Ring("arst", [128, T], F32, 2, ph)
        tmpr = Ring("atmp", [128, T], F32, 3, ph)
        vgr = Ring("avg", [128, GMH], F32, 2, ph)
        vnr = Ring("avn", [128, GMH], BF16, 2, ph)
        str_ = Ring("ast", [128, 6, 6], F32, 2, ph)
        mvr = Ring("amv", [128, 8], F32, 2, ph)
        vpr = Ring("avp", [128, 24, T], BF16, 2, ph)
        tls = tiles(T)
        st = {}

        pending_sp = []

        def stageA(i, defer=None):
            tl = tls[i]
            x_t, bx = xr.next()
            h_t, bh = hr.next()
            load_x0(tl, x_t, bx, posr)
            norm_stage(x_t, bx, tl["n"], 0, tl["who"], (A_M, S_M), h_t, bh, sqr, rstr, tmpr, defer=defer)
            st[i] = (h_t, bh)

        def stageB(i):
            tl = tls[i]
            n = tl["n"]
            h_t, bh = st.pop(i)
            vp, bvp = vpr.next()
            pend = []
            if i + 1 < len(tls):
                stageA(i + 1, pend)
            nj = n // 128
            for j in range(nj):
                vg, bvg = vgr.next()
                stt, bstt = str_.next()
                for fb in range(6):
                    pst, bpst = ps_next()

                    def _mm(en, pst=pst, fb=fb, j=j):
                        for k in range(KC):
                            en.matmul(pst[:, :], h_t[:, k, j * 128:(j + 1) * 128], Wv[:, k, fb * 512:(fb + 1) * 512],
                                      start=(k == 0), stop=False)
                        return en.matmul(pst[:, :], ones_bf[:, :], bvb[:, fb * 512:(fb + 1) * 512], start=False,
                                         stop=True)

                    P.op("tensor", _mm, reads=bWv + [bh, bbvb, bCONST], writes=[bpst], chain=False)
                    P.op("scalar", lambda en, pst=pst, fb=fb, vg=vg: en.activation(
                        out=vg[:, fb * 512:(fb + 1) * 512], in_=pst[:, :], func=AF.Gelu), reads=[bpst], writes=[bvg])
                    P.op("vector", lambda en, fb=fb, vg=vg, stt=stt: en.bn_stats(
                        out=stt[:, fb, :], in_=vg[:, fb * 512:(fb + 1) * 512]), reads=[bvg], writes=[bstt])
                mv, bmv = mvr.next()

                P.op("vector", lambda en, stt=stt, mv=mv: en.bn_aggr(out=mv[:, 0:2],
                                                                     in_=stt[:].rearrange("p a b -> p (a b)")),
                     reads=[bstt], writes=[bmv])
                P.op("vector", lambda en, mv=mv: en.tensor_scalar(out=mv[:, 2:3], in0=mv[:, 1:2], scalar1=EPS,
                                                                   scalar2=None, op0=ALU.add), reads=[bmv], writes=[bmv])
                P.op("scalar", lambda en, mv=mv: en.activation(out=mv[:, 2:3], in_=mv[:, 2:3], func=AF.Sqrt),
                     reads=[bmv], writes=[bmv])
                P.op("vector", lambda en, mv=mv: en.reciprocal(out=mv[:, 3:4], in_=mv[:, 2:3]), reads=[bmv],
                     writes=[bmv])
                P.op("vector", lambda en, mv=mv: en.tensor_scalar(out=mv[:, 4:5], in0=mv[:, 0:1], scalar1=-1.0,
                                                                   scalar2=mv[:, 3:4], op0=ALU.mult, op1=ALU.mult),
                     reads=[bmv], writes=[bmv])
                vn, bvn = vnr.next()
                P.op("gpsimd", lambda en, vg=vg, vn=vn, mv=mv: en.tensor_scalar(
                    out=vn[:], in0=vg[:], scalar1=mv[:, 3:4], scalar2=mv[:, 4:5], op0=ALU.mult, op1=ALU.add),
                    reads=[bvg, bmv], writes=[bvn])
                while pending_sp:
                    pending_sp.pop(0)()
                pending_sp.append(lambda j=j, vn=vn, bvn=bvn, vp=vp, bvp=bvp, tl=tl, n=n, nj=nj: spatial(
                    j, vn, bvn, vp, bvp, tl, n, j == nj - 1))
                if j == 0:
                    for f_ in pend:
                        f_()

        def spatial(j, vn, bvn, vp, bvp, tl, n, last):
            if True:
                for q4 in range(6):
                    pst, bpst = ps_next()

                    def _sp(en, pst=pst, q4=q4, vn=vn):
                        for r in range(4):
                            dc = q4 * 4 + r
                            ins = en.matmul(pst[:, r * 128:(r + 1) * 128], vn[:, dc * 128:(dc + 1) * 128],
                                            wsb[:, (dc // 3) * 128:(dc // 3 + 1) * 128], start=True, stop=True)
                        return ins

                    P.op("tensor", _sp, reads=[bvn, bwsb], writes=[bpst], chain=False)
                    for r in range(4):
                        dc = q4 * 4 + r
                        P.op("vector", lambda en, pst=pst, r=r, dc=dc, j=j: en.scalar_tensor_tensor(
                            out=vp[:, dc, j * 128:(j + 1) * 128], in0=pst[:, r * 128:(r + 1) * 128],
                            scalar=vcol("lng", dc), in1=Ct[:, dc, :], op0=ALU.mult, op1=ALU.add),
                            reads=[bpst, bVv, bCt], writes=[bvp])
            if last:
                c0, c1 = src_cols(tl, None, None)
                P.op("sync", lambda en: en.dma_start(out=fm(V1, c0, c1), in_=vp[:, :, 0:n]), reads=[bvp], writes=[bV1],
                     dma=True)

        stageA(0)
        for i in range(len(tls)):
            stageB(i)
        while pending_sp:
            pending_sp.pop(0)()
        P.barrier()
        P.emit()
        ph.close()

    def phase1b():
        T = 256
        pos_eng[0] = "vector"
        ph = new_phase()
        Wu, bWu = wview(0, KC, GMH)
        Wo, bWo = wview(KC * GMH, 24, D)
        wload(Wu, bWu, gm_w_in[:, 0:GMH], KC, GMH)
        wload(Wo, bWo, gm_w_out, 24, D)
        xr = Ring("bx", [128, KC, T], F32, 3, ph)
        posr = Ring("bpos", [128, KC, T], F32, 1, ph)
        hr = Ring("bh", [128, KC, T], BF16, 2, ph)
        sqr = Ring("bsq", [128, KC, T], BF16, 1, ph)
        rstr = Ring("brst", [128, T], F32, 2, ph)
        tmpr = Ring("btmp", [128, T], F32, 3, ph)
        vpr = Ring("bvp", [128, 24, T], BF16, 2, ph)
        ur = Ring("bu", [128, T], BF16, 3, ph)
        tls = tiles(T)
        st = {}

        def stageA(i, defer=None, defer1=None):
            tl = tls[i]
            n = tl["n"]
            x_t, bx = xr.next()
            h_t, bh = hr.next()
            vp, bvp = vpr.next()
            load_x0(tl, x_t, bx, posr, defer1)
            c0, c1 = src_cols(tl, None, None)
            P.op("sync", lambda en: en.dma_start(out=vp[:, :, 0:n], in_=fm(V1, c0, c1)), reads=[bV1], writes=[bvp],
                 dma=True)
            norm_stage(x_t, bx, n, 0, tl["who"], (A_M, S_M), h_t, bh, sqr, rstr, tmpr, defer=defer, defer1=defer1)
            st[i] = (x_t, bx, h_t, bh, vp, bvp)

        def stageB(i):
            tl = tls[i]
            n = tl["n"]
            x_t, bx, h_t, bh, vp, bvp = st.pop(i)
            pend = []
            pend1 = []
            if i + 1 < len(tls):
                stageA(i + 1, pend, pend1)
            for dc in range(24):
                pu, bpu = std_mm(Wu, bWu, KC, dc * 128, lambda k: h_t[:, k, 0:n], n, [bh])
                u_t, bu = ur.next()
                P.op("scalar", lambda en, pu=pu, u_t=u_t, dc=dc: en.activation(
                    out=u_t[:, 0:n], in_=pu[:, 0:n], func=AF.Gelu, bias=vcol("bu", dc), scale=1.0),
                    reads=[bpu, bVv], writes=[bu])
                P.op("vector", lambda en, u_t=u_t, dc=dc: en.tensor_tensor(
                    out=vp[:, dc, 0:n], in0=vp[:, dc, 0:n], in1=u_t[:, 0:n], op=ALU.mult), reads=[bu, bvp],
                    writes=[bvp])
                if dc == 9:
                    for f_ in pend1:
                        f_()
                if dc == 17:
                    for f_ in pend:
                        f_()
            for oc in range(KC):
                py, bpy = std_mm(Wo, bWo, 24, oc * 128, lambda k: vp[:, k, 0:n], n, [bvp])
                g_ap = mvcol(0, tl["who"], G_M, oc)
                P.op("vector", lambda en, py=py, oc=oc, g_ap=g_ap: en.scalar_tensor_tensor(
                    out=x_t[:, oc, 0:n], in0=py[:, 0:n], scalar=g_ap, in1=x_t[:, oc, 0:n], op0=ALU.mult, op1=ALU.add),
                    reads=[bpy, bMV, bx], writes=[bx])
            c0, c1 = src_cols(tl, None, None)
            P.op("sync", lambda en: en.dma_start(out=fm(XA, c0, c1), in_=x_t[:, :, 0:n]), reads=[bx], writes=[bXA],
                 dma=True)

        stageA(0)
        for i in range(len(tls)):
            stageB(i)
        P.barrier()
        P.emit()
        ph.close()

    def chain_pre(kx_t, bkx, qd_t, bqd, vt_t, bvt, c, direction, rings, do_out):
        attr, kxtr, xbr, x32r = rings
        c64 = slice(c * 64, (c + 1) * 64)
        mask = maskF if direction == 0 else maskB
        attm = battm = None
        if do_out:
            pa, bpa = ps_next()

            def _att(en):
                for h in range(8):
                    ins = en.matmul(pa[0:64, h * 64:(h + 1) * 64], kx_t[:, h, c64], qd_t[:, h, c64], start=True,
                                    stop=True)
                return ins

            P.op("tensor", _att, reads=[bkx, bqd], writes=[bpa])
            attm, battm = attr.next()
            P.op("vector", lambda en: en.tensor_tensor(out=attm[:, :], in0=pa[0:64, :], in1=mask[:, :], op=ALU.mult),
                 reads=[bpa, bCONST], writes=[battm])
        pt, bpt = ps_next()
        ptb = pt[:].bitcast(BF16)

        def _tr(en):
            for h in range(8):
                ins = en.transpose(ptb[0:64, h * 128:(h + 1) * 128], kx_t[:, h, c64], ident_bf[:])
            return ins

        P.op("tensor", _tr, reads=[bkx, bCONST], writes=[bpt])
        kxt, bkxt = kxtr.next()
        P.op("scalar", lambda en: en.activation(out=kxt[:, :], in_=ptb[0:64, :], func=AF.Copy), reads=[bpt],
             writes=[bkxt])
        pks = []
        for hb in range(2):
            pk, bpk = ps_next()

            def _kv(en, hb=hb, pk=pk):
                for r in range(4):
                    h = hb * 4 + r
                    ins = en.matmul(pk[:, r * 128:(r + 1) * 128], kxt[:, h * 128:(h + 1) * 128],
                                    vt_t[0:64, c, h * 128:(h + 1) * 128], start=True, stop=True)
                return ins

            P.op("tensor", _kv, reads=[bkxt, bvt], writes=[bpk])
            pks.append((pk, bpk))
        return (attm, battm, pks)

    def chain_rec(pre, qd_t, bqd, vt_t, bvt, c, gc, Y, bY, Dt, bD, rings, do_out, of_t=None, bof=None, add_prev=False):
        attr, kxtr, xbr, x32r = rings
        attm, battm, pks = pre
        c64 = slice(c * 64, (c + 1) * 64)
        x32, bx32 = x32r.next()
        Dbc = Dt[:, :, gc:gc + 1].broadcast_to([128, 8, 128])
        P.op("vector", lambda en: en.tensor_tensor(out=x32[:, :, :], in0=Y[:, :, :], in1=Dbc, op=ALU.mult),
             reads=bY + [bD], writes=[bx32])
        for hb in range(2):
            pk, bpk = pks[hb]
            P.op("vector", lambda en, hb=hb, pk=pk: en.tensor_tensor(
                out=Y[:, hb * 4:(hb + 1) * 4, :], in0=x32[:, hb * 4:(hb + 1) * 4, :],
                in1=pk[:].rearrange("p (r v) -> p r v", r=4), op=ALU.add), reads=[bx32, bpk],
                writes=bY[hb * 4:(hb + 1) * 4])
        if do_out:
            xb, bxb = xbr.next()
            P.op("scalar", lambda en: en.activation(out=xb[:, :, :], in_=x32[:, :, :], func=AF.Copy), reads=[bx32],
                 writes=[bxb])
            po, bpo = ps_next()

            def _o(en):
                for h in range(8):
                    en.matmul(po[:, h * 64:(h + 1) * 64], vt_t[0:64, c, h * 128:(h + 1) * 128],
                              attm[:, h * 64:(h + 1) * 64], start=True, stop=False)
                    ins = en.matmul(po[:, h * 64:(h + 1) * 64], xb[:, h, :], qd_t[:, h, c64], start=False, stop=True)
                return ins

            P.op("tensor", _o, reads=[bvt, battm, bxb, bqd], writes=[bpo])
            pov = po[:].rearrange("p (h t) -> p h t", h=8)
            if add_prev:
                P.op("vector", lambda en: en.tensor_tensor(out=of_t[:, :, c64], in0=pov, in1=of_t[:, :, c64],
                                                           op=ALU.add), reads=[bpo, bof], writes=[bof])
            else:
                P.op("scalar", lambda en: en.activation(out=of_t[:, :, c64], in_=pov, func=AF.Copy), reads=[bpo],
                     writes=[bof])

    def run_chain(chunks, kx_t, bkx, qd_t, bqd, vt_t, bvt, direction, Y, bY, Dt, bD, rings, do_out, of_t=None,
                  bof=None, add_prev=False):
        pre = chain_pre(kx_t, bkx, qd_t, bqd, vt_t, bvt, chunks[0][0], direction, rings, do_out)
        for i_, (c, gc) in enumerate(chunks):
            nxt = None
            if i_ + 1 < len(chunks):
                nxt = chain_pre(kx_t, bkx, qd_t, bqd, vt_t, bvt, chunks[i_ + 1][0], direction, rings, do_out)
            chain_rec(pre, qd_t, bqd, vt_t, bvt, c, gc, Y, bY, Dt, bD, rings, do_out, of_t, bof, add_prev)
            pre = nxt

    def phase3():
        T = 256
        NC4 = T // 64
        ph = new_phase()
        W, bW = wview(0, KC, 5 * D)
        wload(W, bW, hg_w_in, KC, 5 * D)
        xr = Ring("cx", [128, KC, T], F32, 1, ph)
        hr = Ring("ch", [128, KC, T], BF16, 1, ph)
        sqr = Ring("csq", [128, KC, T], BF16, 1, ph)
        rstr = Ring("crst", [128, T], F32, 1, ph)
        tmpr = Ring("ctmp", [128, T], F32, 2, ph)
        qr = Ring("cq", [128, 8, T], BF16, 1, ph)
        sgr = Ring("csg", [128, 8, T], BF16, 1, ph)
        vtr = Ring("cvt", [64, NC4, D], BF16, 1, ph)
        qdr = [Ring("cqd%d" % d, [128, 8, T], BF16, 1, ph) for d in range(2)]
        kxr = [Ring("ckx%d" % d, [128, 8, T], BF16, 1, ph) for d in range(2)]
        ofr = Ring("cof", [128, 8, T], F32, 1, ph)
        tAr = Ring("ctA", [128, T], F32, 4, ph)
        tBr = Ring("ctB", [128, T + 1], F32, 4, ph)
        tCr = Ring("ctC", [128, T], F32, 4, ph)
        tDr = Ring("ctD", [128, T], F32, 4, ph)
        bscr = Ring("cbs", [128, T], F32, 4, ph)
        tsr = Ring("cts", [128, NC4], F32, 4, ph)
        rings = (Ring("catt", [64, 512], BF16, 2, ph), Ring("ckxt", [64, 1024], BF16, 2, ph),
                 Ring("cxb", [128, 8, 128], BF16, 2, ph), Ring("cx32", [128, 8, 128], F32, 2, ph))
        for t_, b_ in zip(tBr.t, tBr.b):
            P.op("vector", lambda en, t_=t_: en.memset(t_[:], 0.0), writes=[b_])
        tls = tiles(T)
        ctx_kxb = None
        def p3_tile(i, tl):
            n = tl["n"]
            isctx = tl["ctx"]
            gc0 = 0 if isctx else CTX // 64 + tl["t0"] // 64
            x_t, bx = xr.next()
            h_t, bh = hr.next()
            c0, c1 = src_cols(tl, None, None)
            P.op("sync", lambda en, x_t=x_t, c0=c0, c1=c1, n=n: en.dma_start(out=x_t[:, :, 0:n], in_=fm(XB, c0, c1)),
                 reads=[bXB], writes=[bx], dma=True)
            norm_stage(x_t, bx, n, 1, tl["who"], (A_M, S_M), h_t, bh, sqr, rstr, tmpr, lnexp=True)
            q_t = bq = sg_t = bsg = None
            if not isctx:
                q_t, bq = qr.next()
                sg_t, bsg = sgr.next()
                for h in range(8):
                    pq, bpq = std_mm(W, bW, KC, h * 128, lambda k: h_t[:, k, 0:n], n, [bh])
                    P.op("scalar", lambda en, pq=pq, h=h, q_t=q_t: en.activation(out=q_t[:, h, 0:n], in_=pq[:, 0:n],
                                                                                func=AF.Silu), reads=[bpq], writes=[bq])
                for h in range(8):
                    pq, bpq = std_mm(W, bW, KC, 4 * D + h * 128, lambda k: h_t[:, k, 0:n], n, [bh])
                    P.op("scalar", lambda en, pq=pq, h=h, sg_t=sg_t: en.activation(out=sg_t[:, h, 0:n], in_=pq[:, 0:n],
                                                                                  func=AF.Silu), reads=[bpq],
                         writes=[bsg])
                t0 = tl["t0"]
                P.op("sync", lambda en, sg_t=sg_t, t0=t0, n=n: en.dma_start(out=fm(SG, t0, t0 + n), in_=sg_t[:, :, 0:n]),
                     reads=[bsg], writes=[bSG], dma=True)
            vt_t, bvt = vtr.next()
            for c in range(n // 64):
                for nb in range(2):
                    pst, bpst = ps_next()

                    def _mi(en, pst=pst, c=c, nb=nb):
                        for k in range(KC):
                            ins = en.matmul(pst[0:64, :], h_t[:, k, c * 64:(c + 1) * 64],
                                            W[:, k, 3 * D + nb * 512:3 * D + (nb + 1) * 512], start=(k == 0),
                                            stop=(k == KC - 1))
                        return ins

                    P.op("tensor", _mi, reads=bW + [bh], writes=[bpst])
                    P.op("vector", lambda en, pst=pst, c=c, nb=nb, vt_t=vt_t: en.tensor_copy(
                        out=vt_t[:, c, nb * 512:(nb + 1) * 512], in_=pst[0:64, :]), reads=[bpst], writes=[bvt])
            if not isctx:
                t0 = tl["t0"]
                P.op("sync", lambda en, vt_t=vt_t, t0=t0, n=n: en.dma_start(
                    out=VT.rearrange("(c s) f -> s c f", s=64)[:, t0 // 64:(t0 + n) // 64, :], in_=vt_t[:, 0:n // 64, :]),
                    reads=[bvt], writes=[bVT], dma=True)
            qd = [None, None]
            kx = [None, None]
            for d in range(2):
                kx[d] = kxr[d].next()
                if not isctx:
                    qd[d] = qdr[d].next()
            nch = n // 64
            items = [(d, h) for d in range(2) for h in range(8)]
            WAVE = 4
            for w0 in range(0, 16, WAVE):
                grp = items[w0:w0 + WAVE]
                bufs = {}
                for (d, h) in grp:
                    pz, bpz = std_mm(W, bW, KC, (1 + d) * D + h * 128, lambda k: h_t[:, k, 0:n], n, [bh])
                    tA, btA = tAr.next()
                    tB, btB = tBr.next()
                    tC, btC = tCr.next()
                    tD, btD = tDr.next()
                    bs, bbs = bscr.next()
                    bufs[(d, h)] = (tA, btA, tB, btB, tC, btC, tD, btD, bs, bbs)
                    P.op("scalar", lambda en, pz=pz, tA=tA: en.activation(out=tA[:, 0:n], in_=pz[:, 0:n],
                                                                         func=AF.Exp), reads=[bpz], writes=[btA])
                for (d, h) in grp:
                    tA, btA, tB, btB, tC, btC, tD, btD, bs, bbs = bufs[(d, h)]
                    P.op("scalar", lambda en, tA=tA, tC=tC: en.activation(out=tC[:, 0:n], in_=tA[:, 0:n], func=AF.Ln,
                                                                         bias=1.0, scale=1.0), reads=[btA],
                         writes=[btC])
                    P.op("scalar", lambda en, tA=tA, tC=tC: en.activation(out=tA[:, 0:n], in_=tC[:, 0:n], func=AF.Exp,
                                                                         scale=-1.0), reads=[btC], writes=[btA])
                for (d, h) in grp:
                    tA, btA, tB, btB, tC, btC, tD, btD, bs, bbs = bufs[(d, h)]
                    P.op("scalar", lambda en, tA=tA, tB=tB, d=d, h=h: en.activation(
                        out=tB[:, 1:n + 1], in_=tA[:, 0:n], func=AF.Ln, bias=1.0,
                        scale=OML[:, 16 + d * 8 + h:17 + d * 8 + h]), reads=[btA, bOML], writes=[btB])
                for (d, h) in grp:
                    tA, btA, tB, btB, tC, btC, tD, btD, bs, bbs = bufs[(d, h)]
                    if d == 0:
                        P.op("vector", lambda en, tB=tB, bs=bs: en.tensor_tensor_scan(
                            out=bs[:, 0:n], data0=mask01[:, 0:n], data1=tB[:, 1:n + 1], initial=0.0, op0=ALU.mult,
                            op1=ALU.add), reads=[btB, bCONST], writes=[bbs])
                    else:
                        P.op("vector", lambda en, tB=tB, bs=bs: en.tensor_tensor_scan(
                            out=bs[:, 0:n], data0=tB[:, 0:n], data1=mask01[:, 0:n], initial=0.0, op0=ALU.add,
                            op1=ALU.mult), reads=[btB, bCONST], writes=[bbs])
                for (d, h) in grp:
                    tA, btA, tB, btB, tC, btC, tD, btD, bs, bbs = bufs[(d, h)]
                    sq_, sk_ = (1.0, -1.0) if d == 0 else (-1.0, 1.0)
                    P.op("scalar", lambda en, bs=bs, tC=tC, sq_=sq_: en.activation(out=tC[:, 0:n], in_=bs[:, 0:n],
                                                                                  func=AF.Exp, scale=sq_),
                         reads=[bbs], writes=[btC])
                    P.op("scalar", lambda en, bs=bs, tD=tD, sk_=sk_: en.activation(out=tD[:, 0:n], in_=bs[:, 0:n],
                                                                                  func=AF.Exp, scale=sk_),
                         reads=[bbs], writes=[btD])
                for (d, h) in grp:
                    tA, btA, tB, btB, tC, btC, tD, btD, bs, bbs = bufs[(d, h)]
                    if not isctx:
                        P.op("vector", lambda en, tC=tC, h=h, d=d: en.tensor_tensor(
                            out=qd[d][0][:, h, 0:n], in0=q_t[:, h, 0:n], in1=tC[:, 0:n], op=ALU.mult),
                            reads=[btC, bq], writes=[qd[d][1]])
                    P.op("vector", lambda en, tA=tA, tD=tD, h=h, d=d: en.scalar_tensor_tensor(
                        out=kx[d][0][:, h, 0:n], in0=tA[:, 0:n], scalar=OML[:, d * 8 + h:d * 8 + h + 1], in1=tD[:, 0:n],
                        op0=ALU.mult, op1=ALU.mult), reads=[btA, btD, bOML], writes=[kx[d][1]])
                    if d == 0:
                        P.op("vector", lambda en, tC=tC, h=h: en.tensor_copy(
                            out=Df[:, h, gc0 + 1:gc0 + 1 + nch], in_=tC[:, 63:n:64]), reads=[btC], writes=[bDf])
                    else:
                        ts_, bts = tsr.next()
                        P.op("vector", lambda en, bs=bs, tB=tB, ts_=ts_: en.tensor_tensor(
                            out=ts_[:, 0:nch], in0=bs[:, 63:n:64], in1=tB[:, 64:n + 1:64], op=ALU.add),
                            reads=[bbs, btB], writes=[bts])
                        for c in range(nch):
                            if isctx:
                                gb = CTX // 64 - 1 - c
                            else:
                                gb = CTX // 64 + (NCHL - 1 - (tl["t0"] // 64 + c))
                            P.op("scalar", lambda en, ts_=ts_, c=c, gb=gb, h=h: en.activation(
                                out=Db[:, h, gb:gb + 1], in_=ts_[:, c:c + 1], func=AF.Exp), reads=[bts], writes=[bDb])
            if isctx:
                run_chain([(c, gc0 + c) for c in range(n // 64)], kx[0][0], kx[0][1], None, None, vt_t, bvt, 0, Yf,
                          bYf, Df, bDf, rings, False)
                run_chain([(c, CTX // 64 - 1 - c) for c in reversed(range(n // 64))], kx[1][0], kx[1][1], None, None,
                          vt_t, bvt, 1, Yb, bYb, Db, bDb, rings, False)
            else:
                of_t, bof = ofr.next()
                run_chain([(c, gc0 + c) for c in range(n // 64)], kx[0][0], kx[0][1], qd[0][0], qd[0][1], vt_t, bvt,
                          0, Yf, bYf, Df, bDf, rings, True, of_t, bof)
                t0 = tl["t0"]
                P.op("sync", lambda en, of_t=of_t, t0=t0, n=n: en.dma_start(out=fm(OF, t0, t0 + n), in_=of_t[:, :, 0:n]),
                     reads=[bof], writes=[bOF], dma=True)
                P.op("sync", lambda en, q=qd[1][0], t0=t0, n=n: en.dma_start(out=fm(QDB, t0, t0 + n), in_=q[:, :, 0:n]),
                     reads=[qd[1][1]], writes=[bQDB], dma=True)
                P.op("sync", lambda en, q=kx[1][0], t0=t0, n=n: en.dma_start(out=fm(KXB, t0, t0 + n), in_=q[:, :, 0:n]),
                     reads=[kx[1][1]], writes=[bKXB], dma=True)

        for i, tl in enumerate(tls):
            p3_tile(i, tl)
        P.barrier()
        P.emit()
        ph.close()

    def phase4():
        T = 256
        NC4 = T // 64
        ph = new_phase()
        Wo, bWo = wview(0, KC, D)
        wload(Wo, bWo, hg_w_out, KC, D)
        xr = Ring("dx", [128, KC, T], F32, 2, ph)
        qdr = Ring("dqd", [128, 8, T], BF16, 2, ph)
        kxr = Ring("dkx", [128, 8, T], BF16, 2, ph)
        sgr = Ring("dsg", [128, 8, T], BF16, 2, ph)
        vtr = Ring("dvt", [64, NC4, D], BF16, 2, ph)
        ofr = Ring("dof", [128, 8, T], F32, 2, ph)
        sqr = Ring("dsq", [128, 8, T], BF16, 1, ph)
        rsr = Ring("drs", [128, 8, T], F32, 1, ph)
        tmpr = Ring("dtmp", [128, T], F32, 3, ph)
        mr = Ring("dm", [128, 8, T], BF16, 1, ph)
        rings = (Ring("datt", [64, 512], BF16, 2, ph), Ring("dkxt", [64, 1024], BF16, 2, ph),
                 Ring("dxb", [128, 8, 128], BF16, 2, ph), Ring("dx32", [128, 8, 128], F32, 2, ph))
        tls = list(reversed(tiles(T, with_ctx=False)))
        st = {}

        def stageA(i):
            tl = tls[i]
            n, t0 = tl["n"], tl["t0"]
            x_t, bx = xr.next()
            qd, bqd = qdr.next()
            kx, bkx = kxr.next()
            sg, bsg = sgr.next()
            vt, bvt = vtr.next()
            of, bof = ofr.next()
            P.op("sync", lambda en: en.dma_start(out=qd[:, :, 0:n], in_=fm(QDB, t0, t0 + n)), reads=[bQDB], writes=[bqd],
                 dma=True)
            P.op("sync", lambda en: en.dma_start(out=kx[:, :, 0:n], in_=fm(KXB, t0, t0 + n)), reads=[bKXB], writes=[bkx],
                 dma=True)
            P.op("sync", lambda en: en.dma_start(
                out=vt[:, 0:n // 64, :], in_=VT.rearrange("(c s) f -> s c f", s=64)[:, t0 // 64:(t0 + n) // 64, :]),
                reads=[bVT], writes=[bvt], dma=True)
            P.op("sync", lambda en: en.dma_start(out=of[:, :, 0:n], in_=fm(OF, t0, t0 + n)), reads=[bOF], writes=[bof],
                 dma=True)
            P.op("sync", lambda en: en.dma_start(out=sg[:, :, 0:n], in_=fm(SG, t0, t0 + n)), reads=[bSG], writes=[bsg],
                 dma=True)
            P.op("sync", lambda en: en.dma_start(out=x_t[:, :, 0:n], in_=fm(XB, t0, t0 + n)), reads=[bXB], writes=[bx],
                 dma=True)
            st[i] = (x_t, bx, qd, bqd, kx, bkx, sg, bsg, vt, bvt, of, bof)

        def stageB(i):
            tl = tls[i]
            n, t0 = tl["n"], tl["t0"]
            x_t, bx, qd, bqd, kx, bkx, sg, bsg, vt, bvt, of, bof = st.pop(i)
            run_chain([(c, CTX // 64 + (NCHL - 1 - (t0 // 64 + c))) for c in reversed(range(n // 64))], kx, bkx, qd,
                      bqd, vt, bvt, 1, Yb, bYb, Db, bDb, rings, True, of, bof, add_prev=True)
            if i + 1 < len(tls):
                stageA(i + 1)
            sq_t, bsq = sqr.next()
            P.op("gpsimd", lambda en: en.tensor_tensor(out=sq_t[:, :, 0:n], in0=of[:, :, 0:n], in1=of[:, :, 0:n],
                                                       op=ALU.mult), reads=[bof], writes=[bsq])
            rs, brs = rsr.next()
            for h in range(8):
                pst, bpst = ps_next()
                P.op("tensor", lambda en, pst=pst, h=h: en.matmul(pst[:, 0:n], ones_bf[:], sq_t[:, h, 0:n], start=True,
                                                                 stop=True), reads=[bsq, bCONST], writes=[bpst])
                P.op("scalar", lambda en, pst=pst, h=h: en.activation(out=rs[:, h, 0:n], in_=pst[:, 0:n], func=AF.Sqrt,
                                                                     bias=EPS, scale=1.0 / 128), reads=[bpst],
                     writes=[brs])
            P.op("vector", lambda en: en.reciprocal(out=rs[:, :, 0:n], in_=rs[:, :, 0:n]), reads=[brs], writes=[brs])
            m_t, bm = mr.next()
            for h in range(8):
                tmp, btmp = tmpr.next()
                P.op("vector", lambda en, h=h, tmp=tmp: en.scalar_tensor_tensor(
                    out=tmp[:, 0:n], in0=of[:, h, 0:n], scalar=vcol("hnw", h), in1=rs[:, h, 0:n], op0=ALU.mult,
                    op1=ALU.mult), reads=[bof, brs, bVv], writes=[btmp])
                P.op("gpsimd", lambda en, h=h, tmp=tmp: en.tensor_tensor(
                    out=m_t[:, h, 0:n], in0=tmp[:, 0:n], in1=sg[:, h, 0:n], op=ALU.mult), reads=[btmp, bsg],
                    writes=[bm])
            for oc in range(KC):
                py, bpy = std_mm(Wo, bWo, KC, oc * 128, lambda k: m_t[:, k, 0:n], n, [bm])
                g_ap = mvcol(1, 0, G_M, oc)
                P.op("vector", lambda en, py=py, oc=oc, g_ap=g_ap: en.scalar_tensor_tensor(
                    out=x_t[:, oc, 0:n], in0=py[:, 0:n], scalar=g_ap, in1=x_t[:, oc, 0:n], op0=ALU.mult, op1=ALU.add),
                    reads=[bpy, bMV, bx], writes=[bx])
            P.op("sync", lambda en: en.dma_start(out=fm(XC, t0, t0 + n), in_=x_t[:, :, 0:n]), reads=[bx], writes=[bXC],
                 dma=True)

        stageA(0)
        for i in range(len(tls)):
            stageB(i)
        P.barrier()
        P.emit()
        ph.close()

    phases = [("p1a", phase1a), ("p1b", phase1b), ("p2", lambda: ffn_phase(0, XA, bXA, True, XB, bXB, False)),
              ("p3", phase3), ("p4", phase4), ("p5", lambda: ffn_phase(1, XC, bXC, False, outT, bOUT, True))]
    for name, f in phases:
        f()
        if stop_after == name:
            break
    P.barrier()
    P.emit()
    es.close()
    return nc


def make_in_maps(inputs, S):
    f = lambda a: np.ascontiguousarray(np.asarray(a, dtype=np.float32))
    x = f(inputs["x"])
    B = x.shape[0]
    pos = pos_table(S)
    gm_w_s = f(inputs["gm_w_s"])[0]
    wsT = np.ascontiguousarray(gm_w_s.transpose(2, 0, 1).reshape(128, 1024))
    shared = {
        "posT": pos,
        "bvrow": f(inputs["gm_b_in"])[0, GMH:].reshape(1, GMH).copy(),
        "lnbrow": f(inputs["gm_ln_b"])[0].reshape(1, GMH).copy(),
        "bsrow": f(inputs["gm_b_s"])[0].reshape(1, 1024).copy(),
        "wsT": wsT,
        "ada_w": f(inputs["ada_w"]),
        "gm_w_in": f(inputs["gm_w_in"])[0],
        "gm_w_out": f(inputs["gm_w_out"])[0],
        "hg_w_in": f(inputs["hg_w_in"])[0],
        "hg_w_out": f(inputs["hg_w_out"])[0],
        "ffn_w_in": f(inputs["ffn_w_in"]),
        "ffn_w_out": f(inputs["ffn_w_out"]),
    }
    hg_lb = f(inputs["hg_lb"])
    maps = []
    for b in range(B):
        vec = np.zeros((128, NV), np.float32)

        def put(name, arr):
            o, w = VEC_LAYOUT[name]
            assert arr.shape == (128, w), (name, arr.shape)
            vec[:, o:o + w] = arr

        cc = np.stack([_col(inputs["c"][b]), _col(inputs["c_ctx"])], axis=-1).reshape(128, 16)
        put("cc", cc)
        for l in range(2):
            put("ada_b%d" % l, _col(inputs["ada_b"][l]))
            put("nmw%d" % l, _col(inputs["norm_mix_w"][l]))
            put("nfw%d" % l, _col(inputs["norm_ffn_w"][l]))
            put("lb%d" % l, np.concatenate([_col(hg_lb[l, 0]), _col(hg_lb[l, 1])], axis=1))
        put("bu", _col(f(inputs["gm_b_in"])[0, :GMH]))
        put("lng", _col(inputs["gm_ln_g"][0]))
        put("hnw", _col(inputs["hg_norm_w"][0]))
        put("fnw", _col(inputs["final_norm_w"]))
        m = dict(shared)
        m["xT"] = np.ascontiguousarray(x[b, :S].T)
        m["ctxT"] = np.ascontiguousarray(f(inputs["ctx"])[b].T)
        m["vecs"] = vec
        maps.append(m)
    return maps


def kernel(**inputs):
    S = inputs["x"].shape[1]
    nc = build(S)
    maps = make_in_maps(inputs, S)
    res = run_bass_kernel_spmd(nc, maps, core_ids=list(range(len(maps))))
    out = np.stack([np.ascontiguousarray(r["outT"].T) for r in res.results], axis=0)
    return out.astype(np.float32)
```

```python
import contextlib
import numpy as np
import concourse.bass as bass
import concourse.mybir as mybir
from concourse.alu_op_type import AluOpType as ALU
from concourse.bass_utils import run_bass_kernel_spmd

F32 = mybir.dt.float32
BF16 = mybir.dt.bfloat16
AF = mybir.ActivationFunctionType
ENGS = ["tensor", "vector", "scalar", "gpsimd", "sync"]

D = 1024
KC = 8
CTX = 256
SEQ = 8192
GRID_W = 64
EPS = 1e-6
GMH = 3072
DFF = 2816
NFC = 22
PE_PARTIAL_CHAIN = True
STRICT_ENGS = {"vector", "scalar", "gpsimd", "sync"}


class Buf:
    __slots__ = ("name", "last_w", "readers", "dcount", "sem")

    def __init__(self, name):
        self.name = name
        self.last_w = None
        self.readers = []
        self.dcount = 0
        self.sem = None


class Prog:
    def __init__(self, nc, es):
        self.nc = nc
        self.es = es
        self.ops = {e: [] for e in ENGS}
        self.nemit = {e: 0 for e in ENGS}
        self.seen = {e: {} for e in ENGS}
        self.esem = {e: es.enter_context(nc.semaphore("e_" + e)) for e in ENGS}
        self.ecount = {e: 0 for e in ENGS}
        self.dbufs = []
        self.allbufs = []
        self.prev_chained = True

    def buf(self, name):
        b = Buf(name)
        self.allbufs.append(b)
        return b

    def _need(self, eng, tok):
        key, val = tok
        if self.seen[eng].get(key, -1) >= val:
            return False
        self.seen[eng][key] = val
        return True

    def op(self, eng, fn, reads=(), writes=(), dma=False, chain=True, nowaw=False):
        idx = len(self.ops[eng])
        deps = []
        for b in reads:
            if b.last_w is not None:
                deps.append(b.last_w)
        for b in writes:
            if b.last_w is not None and not (nowaw and b.last_w[0] == ("d", id(b))):
                deps.append(b.last_w)
            deps.extend(b.readers)
        waits = []
        for tok in deps:
            key, val = tok
            if key == ("e", "tensor") and eng == "tensor" and not dma:
                continue
            if self._need(eng, tok):
                waits.append(tok)
        rec = {"fn": fn, "waits": waits, "inc": False, "dma": None, "chain": chain}
        if dma:
            dst = writes[0]
            if dst.sem is None:
                dst.sem = self.es.enter_context(self.nc.semaphore("d%d_%s" % (len(self.dbufs), dst.name)))
                self.dbufs.append(dst)
            dst.dcount += 16
            mytok = (("d", id(dst)), dst.dcount)
            rec["dma"] = dst
        else:
            mytok = (("e", eng), idx)
        self.ops[eng].append(rec)
        for tok in waits:
            if tok[0][0] == "e":
                self.ops[tok[0][1]][tok[1]]["inc"] = True
        for b in reads:
            b.readers = [t for t in b.readers if t[0] != mytok[0]] + [mytok]
        for b in writes:
            b.last_w = mytok
            b.readers = []
        return mytok

    def barrier(self):
        bars = []
        for e in ENGS:
            b = Buf("bar_" + e)
            rd = list(self.dbufs) if e in ("sync", "gpsimd") else []
            self.op(e, lambda en: en.nop(), reads=rd, writes=[b])
            bars.append(b)
        for e in ENGS:
            self.op(e, lambda en: en.nop(), reads=bars)

    def emit(self):
        nc = self.nc
        sigval = getattr(self, "sigval", {})
        self.sigval = sigval
        for e in ENGS:
            for i in range(self.nemit[e], len(self.ops[e])):
                r = self.ops[e][i]
                if r["inc"] and r["dma"] is None:
                    self.ecount[e] += 1
                    sigval[(e, i)] = self.ecount[e]
        dsem = {id(b): b.sem for b in self.dbufs}
        with nc.Block() as block:
            for e in ENGS:
                lo, hi = self.nemit[e], len(self.ops[e])
                if hi == lo:
                    continue

                def run(en, e=e, lo=lo, hi=hi):
                    for i in range(lo, hi):
                        r = self.ops[e][i]
                        for key, val in r["waits"]:
                            if key[0] == "e":
                                en.wait_ge(self.esem[key[1]], sigval[(key[1], val)])
                            else:
                                en.wait_ge(dsem[key[1]], val)
                        strict = (e in STRICT_ENGS) or (e == "tensor" and PE_PARTIAL_CHAIN and
                                                        (r["chain"] or self.prev_chained))
                        if e == "tensor" and r["inc"] and r["dma"] is None:
                            self.prev_chained = r["chain"]
                        if strict and r["inc"] and r["dma"] is None and sigval[(e, i)] > 1:
                            en.wait_ge(self.esem[e], sigval[(e, i)] - 1)
                        ins = r["fn"](en)
                        if r["dma"] is not None:
                            ins.then_inc(r["dma"].sem, 16)
                        elif r["inc"]:
                            ins.then_inc(self.esem[e], 1)

                getattr(block, e)(run)
                self.nemit[e] = hi


def _col(v):
    v = np.asarray(v, np.float32).reshape(-1, 128)
    return np.ascontiguousarray(v.T)


VEC_LAYOUT = {}
_off = 0
for _n, _w in [("cc", 16), ("ada_b0", 48), ("ada_b1", 48), ("nmw0", 8), ("nmw1", 8), ("nfw0", 8), ("nfw1", 8),
               ("bu", 24), ("lng", 24), ("lb0", 16), ("lb1", 16), ("hnw", 8), ("fnw", 8)]:
    VEC_LAYOUT[_n] = (_off, _w)
    _off += _w
NV = _off


def pos_table(n):
    half = D // 2

    def sincos(pos, dim):
        h = dim // 2
        omega = (1.0 / (10000.0 ** (np.arange(h, dtype=np.float32) / np.float32(h)))).astype(np.float32)
        ang = pos.astype(np.float32)[:, None] * omega[None, :]
        return np.concatenate([np.sin(ang), np.cos(ang)], axis=-1).astype(np.float32)

    rows = n // GRID_W
    rc = sincos(np.arange(rows), half)
    cc = sincos(np.arange(GRID_W), half)
    code = np.concatenate([np.broadcast_to(rc[:, None, :], (rows, GRID_W, half)),
                           np.broadcast_to(cc[None, :, :], (rows, GRID_W, half))], axis=-1)
    return np.ascontiguousarray(code.reshape(rows * GRID_W, D).T.astype(np.float32))


def build(S=SEQ, stop_after=None, dbg_out=None):
    nc = bass.Bass("TRN2", target_bir_lowering=False)
    es = contextlib.ExitStack()
    P = Prog(nc, es)

    def din(name, shape, dt=F32):
        return nc.dram_tensor(name, list(shape), dt, kind="ExternalInput").ap()

    def dscr(name, shape, dt):
        kind = "ExternalOutput" if (dbg_out and name in dbg_out) else "Internal"
        return nc.dram_tensor(name, list(shape), dt, kind=kind).ap()

    xT = din("xT", [D, S])
    ctxT = din("ctxT", [D, CTX])
    posT = din("posT", [D, S])
    vecs = din("vecs", [128, NV])
    bvrow = din("bvrow", [1, GMH])
    lnbrow = din("lnbrow", [1, GMH])
    bsrow = din("bsrow", [1, 1024])
    wsT_d = din("wsT", [128, 1024])
    ada_w = din("ada_w", [2, D, 6 * D])
    gm_w_in = din("gm_w_in", [D, 2 * GMH])
    gm_w_out = din("gm_w_out", [GMH, D])
    hg_w_in = din("hg_w_in", [D, 5 * D])
    hg_w_out = din("hg_w_out", [D, D])
    ffn_w_in = din("ffn_w_in", [2, D, 2 * DFF])
    ffn_w_out = din("ffn_w_out", [2, DFF, D])
    outT = nc.dram_tensor("outT", [D, S], F32, kind="ExternalOutput").ap()

    V1 = dscr("V1", [GMH, S + CTX], BF16)
    XA = dscr("XA", [D, S + CTX], F32)
    XB = dscr("XB", [D, S + CTX], F32)
    XC = dscr("XC", [D, S], F32)
    OF = dscr("OF", [D, S], F32)
    QDB = dscr("QDB", [D, S], BF16)
    KXB = dscr("KXB", [D, S], BF16)
    SG = dscr("SG", [D, S], BF16)
    VT = dscr("VT", [S, D], BF16)
    bV1, bXA, bXB, bXC, bOF, bQDB, bKXB, bSG, bVT, bOUT = [P.buf(n) for n in
                                                            ["V1", "XA", "XB", "XC", "OF", "QDB", "KXB", "SG", "VT", "OUT"]]
    bIN = P.buf("inputs")

    def fm(ap3, c0, c1):
        return ap3.rearrange("(k p) t -> p k t", p=128)[:, :, c0:c1]

    ARENA_BYTES = 184 * 1024
    AR = es.enter_context(nc.sbuf_tensor("AR", [128, ARENA_BYTES // 2], BF16))
    aoff = [0]

    class _Phase:
        def close(self):
            pass

    def new_phase():
        aoff[0] = 0
        return _Phase()

    def sb(name, shape, dt, stack=None):
        if stack is None:
            return es.enter_context(nc.sbuf_tensor(name, list(shape), dt))
        esz = 4 if dt == F32 else 2
        nel = 1
        for d_ in shape[1:]:
            nel *= d_
        nb = (nel * esz + 63) // 64 * 64
        assert aoff[0] + nb <= ARENA_BYTES, ("arena overflow", name, aoff[0], nb)
        v = AR[:, aoff[0] // 2:(aoff[0] + nb) // 2]
        aoff[0] += nb
        if dt == F32:
            v = v.bitcast(F32)
        v = v[:, 0:nel]
        if len(shape) == 3:
            v = v.rearrange("p (a b) -> p a b", a=shape[1])
        if shape[0] < 128:
            v = v[0:shape[0]]
        return v

    wcnt = [0]

    def wview(e0, kc, n):
        wcnt[0] += 1
        ap = sb("w%d" % wcnt[0], [128, kc, n], BF16, True)
        return ap, [P.buf("w%d" % wcnt[0])]

    Vv = sb("Vv", [128, NV], F32)
    bVv = P.buf("Vv")
    MV = sb("MV", [128, 2 * 2 * 6 * 8], F32)
    bMV = P.buf("MV")
    OML = sb("OML", [128, 32], F32)
    bOML = P.buf("OML")
    ones_bf = sb("ones_bf", [128, 128], BF16)
    ident_bf = sb("ident_bf", [128, 128], BF16)
    mask01 = sb("mask01", [128, 256], BF16)
    maskF = sb("maskF", [64, 512], BF16)
    maskB = sb("maskB", [64, 512], BF16)
    bCONST = P.buf("const")
    NCHL = S // 64
    NCH = NCHL + CTX // 64
    Df = sb("Df", [128, 8, NCH + 1], F32)
    Db = sb("Db", [128, 8, NCH + 1], F32)
    bDf = P.buf("Df")
    bDb = P.buf("Db")
    Yf = sb("Yf", [128, 8, 128], F32)
    Yb = sb("Yb", [128, 8, 128], F32)
    bYf = [P.buf("Yf%d" % h) for h in range(8)]
    bYb = [P.buf("Yb%d" % h) for h in range(8)]

    psum = [es.enter_context(nc.psum_tensor("ps%d" % i, [128, 512], F32)) for i in range(8)]
    bps = [P.buf("ps%d" % i) for i in range(8)]
    psc = [0]

    def ps_next():
        i = psc[0] % 8
        psc[0] += 1
        return psum[i], bps[i]

    def mvcol(l, who, kind, k):
        c = ((l * 2 + who) * 6 + kind) * 8 + k
        return MV[:, c:c + 1]

    A_M, S_M, G_M, A_F, S_F, G_F = range(6)

    def vcol(name, k=0, n=1):
        o, w = VEC_LAYOUT[name]
        return Vv[:, o + k:o + k + n]

    def wload(dst_ap, dst_bufs, src2d, kc, n):
        step = 2048
        for k in range(kc):
            for c0 in range(0, n, step):
                c1 = min(n, c0 + step)
                P.op("gpsimd", lambda en, k=k, c0=c0, c1=c1: en.dma_start(
                    out=dst_ap[:, k, c0:c1], in_=src2d[k * 128:(k + 1) * 128, c0:c1]),
                    reads=[bIN], writes=dst_bufs, dma=True, nowaw=True)

    ph = new_phase()
    ident_f = sb("ident_f", [128, 128], F32, ph)
    P.op("sync", lambda en: en.dma_start(out=Vv[:], in_=vecs[:, :]), reads=[bIN], writes=[bVv], dma=True)

    for t_, b_, v_ in [(ones_bf, bCONST, 1.0), (mask01, bCONST, 1.0), (Df, bDf, 1.0), (Db, bDb, 1.0)]:
        P.op("vector", lambda en, t_=t_, v_=v_: en.memset(t_[:], v_), writes=[b_])
    P.op("vector", lambda en: en.memset(mask01[:, 0:256:64], 0.0), writes=[bCONST])
    P.op("vector", lambda en: en.memset(Yf[:], 0.0), writes=bYf)
    P.op("vector", lambda en: en.memset(Yb[:], 0.0), writes=bYb)
    bMK = P.buf("masks")
    P.op("gpsimd", lambda en: en.memset(ident_f[:], 1.0), writes=[bMK])
    P.op("gpsimd", lambda en: en.affine_select(out=ident_f[:], in_=ident_f[:], pattern=[[-1, 128]],
                                               compare_op=ALU.is_equal, fill=0.0, base=0, channel_multiplier=1),
         writes=[bMK])
    P.op("gpsimd", lambda en: en.memset(maskF[:], 1.0), writes=[bMK])
    P.op("gpsimd", lambda en: en.memset(maskB[:], 1.0), writes=[bMK])
    mf_ = maskF[:].rearrange("p (h t) -> p h t", h=8)
    mb_ = maskB[:].rearrange("p (h t) -> p h t", h=8)
    P.op("gpsimd", lambda en: en.affine_select(out=mf_, in_=mf_, pattern=[[0, 8], [1, 64]], compare_op=ALU.is_ge,
                                               fill=0.0, base=0, channel_multiplier=-1), writes=[bMK])
    P.op("gpsimd", lambda en: en.affine_select(out=mb_, in_=mb_, pattern=[[0, 8], [-1, 64]], compare_op=ALU.is_ge,
                                               fill=0.0, base=0, channel_multiplier=1), writes=[bMK])
    P.op("vector", lambda en: en.tensor_copy(out=ident_bf[:], in_=ident_f[:]), reads=[bMK], writes=[bCONST])
    s_bf = sb("s_bf", [128, 16], BF16, ph)
    bs_bf = P.buf("s_bf")
    P.op("scalar", lambda en: en.activation(out=s_bf[:], in_=vcol("cc", 0, 16), func=AF.Silu),
         reads=[bVv], writes=[bs_bf])
    lbd = sb("lbd", [128, 16], F32, ph)
    blbd = P.buf("lbd")
    P.op("vector", lambda en: en.tensor_tensor(out=lbd[:], in0=vcol("lb0", 0, 16), in1=vcol("lb1", 0, 16),
                                               op=ALU.subtract), reads=[bVv], writes=[blbd])
    P.op("scalar", lambda en: en.activation(out=OML[:, 0:16], in_=lbd[:], func=AF.Sigmoid),
         reads=[blbd], writes=[bOML])
    P.op("vector", lambda en: en.tensor_scalar(out=OML[:, 16:32], in0=OML[:, 0:16], scalar1=-1.0, scalar2=None,
                                               op0=ALU.mult), reads=[bOML], writes=[bOML])
    modt = sb("modt", [128, 96], F32, ph)
    bmodt = P.buf("modt")
    WA, bWA = wview(0, KC, 6 * D)
    for l in range(2):
        wload(WA, bWA, ada_w[l], KC, 6 * D)
        pst, bpst = ps_next()

        def _mod(en, WA=WA, pst=pst):
            for dc in range(48):
                for k in range(KC):
                    ins = en.matmul(pst[:, dc * 2:dc * 2 + 2], WA[:, k, dc * 128:(dc + 1) * 128],
                                    s_bf[:, 2 * k:2 * k + 2], start=(k == 0), stop=(k == KC - 1))
            return ins

        P.op("tensor", _mod, reads=bWA + [bs_bf], writes=[bpst])
        for who in range(2):
            P.op("vector", lambda en, pst=pst, who=who, l=l: en.tensor_tensor(
                out=modt[:, who * 48:(who + 1) * 48], in0=pst[:, who:96:2], in1=vcol("ada_b%d" % l, 0, 48), op=ALU.add),
                reads=[bpst, bVv], writes=[bmodt])

            def _mv(en, who=who, l=l):
                m = modt[:, who * 48:(who + 1) * 48]
                base = ((l * 2 + who) * 6) * 8
                en.scalar_tensor_tensor(out=MV[:, base + A_M * 8:base + A_M * 8 + 8], in0=m[:, 8:16], scalar=1.0,
                                        in1=vcol("nmw%d" % l, 0, 8), op0=ALU.add, op1=ALU.mult)
                en.tensor_copy(out=MV[:, base + S_M * 8:base + S_M * 8 + 8], in_=m[:, 0:8])
                en.tensor_copy(out=MV[:, base + G_M * 8:base + G_M * 8 + 8], in_=m[:, 16:24])
                en.scalar_tensor_tensor(out=MV[:, base + A_F * 8:base + A_F * 8 + 8], in0=m[:, 32:40], scalar=1.0,
                                        in1=vcol("nfw%d" % l, 0, 8), op0=ALU.add, op1=ALU.mult)
                en.tensor_copy(out=MV[:, base + S_F * 8:base + S_F * 8 + 8], in_=m[:, 24:32])
                return en.tensor_copy(out=MV[:, base + G_F * 8:base + G_F * 8 + 8], in_=m[:, 40:48])

            P.op("vector", _mv, reads=[bmodt, bVv], writes=[bMV])
    P.barrier()
    P.emit()
    ph.close()

    def tiles(T, with_ctx=True, lat=True):
        out = []
        if with_ctx:
            out.append(dict(ctx=True, t0=0, n=CTX, who=1))
        if lat:
            for t0 in range(0, S, T):
                out.append(dict(ctx=False, t0=t0, n=min(T, S - t0), who=0))
        return out

    class Ring:
        def __init__(self, name, shape, dt, n, stack):
            self.t = [sb("%s%d" % (name, i), shape, dt, stack) for i in range(n)]
            self.b = [P.buf("%s%d" % (name, i)) for i in range(n)]
            self.i = 0

        def next(self):
            j = self.i % len(self.t)
            self.i += 1
            return self.t[j], self.b[j]

    def norm_stage(x_t, bx, n, l, who, kinds, out_t, bout, sq_ring, rst_ring, tmp_ring, sq_eng="gpsimd",
                   custom_a=None, out_f32=False, lnexp=False, defer=None, defer1=None):
        sq_t, bsq = sq_ring.next()

        def _sq():
            P.op(sq_eng, lambda en: en.tensor_tensor(out=sq_t[:, :, 0:n], in0=x_t[:, :, 0:n], in1=x_t[:, :, 0:n],
                                                     op=ALU.mult), reads=[bx], writes=[bsq])

        if defer1 is not None:
            defer1.append(_sq)
        else:
            _sq()
        if defer is not None:
            defer.append(lambda: norm_post(x_t, bx, n, l, who, kinds, out_t, bout, sq_t, bsq, rst_ring, tmp_ring,
                                           custom_a, out_f32, lnexp))
        else:
            norm_post(x_t, bx, n, l, who, kinds, out_t, bout, sq_t, bsq, rst_ring, tmp_ring, custom_a, out_f32, lnexp)

    def norm_post(x_t, bx, n, l, who, kinds, out_t, bout, sq_t, bsq, rst_ring, tmp_ring, custom_a, out_f32, lnexp):
        pst, bpst = ps_next()

        def _ss(en):
            for k in range(KC):
                ins = en.matmul(pst[:, 0:n], ones_bf[:], sq_t[:, k, 0:n], start=(k == 0), stop=(k == KC - 1))
            return ins

        P.op("tensor", _ss, reads=[bsq, bCONST], writes=[bpst], chain=False)
        rst, brst = rst_ring.next()
        if lnexp:
            P.op("scalar", lambda en: en.activation(out=rst[:, 0:n], in_=pst[:, 0:n], func=AF.Ln, bias=EPS,
                                                    scale=1.0 / D), reads=[bpst], writes=[brst])
            P.op("scalar", lambda en: en.activation(out=rst[:, 0:n], in_=rst[:, 0:n], func=AF.Exp, scale=-0.5),
                 reads=[brst], writes=[brst])
        else:
            P.op("scalar", lambda en: en.activation(out=rst[:, 0:n], in_=pst[:, 0:n], func=AF.Sqrt, bias=EPS,
                                                    scale=1.0 / D), reads=[bpst], writes=[brst])
            P.op("vector", lambda en: en.reciprocal(out=rst[:, 0:n], in_=rst[:, 0:n]), reads=[brst], writes=[brst])
        for k in range(KC):
            a_ap = custom_a(k) if custom_a else mvcol(l, who, kinds[0], k)
            if out_f32:
                P.op("vector", lambda en, k=k, a_ap=a_ap: en.scalar_tensor_tensor(
                    out=out_t[:, k, 0:n], in0=x_t[:, k, 0:n], scalar=a_ap, in1=rst[:, 0:n], op0=ALU.mult,
                    op1=ALU.mult), reads=[bx, brst, bMV, bVv], writes=[bout])
                continue
            tmp, btmp = tmp_ring.next()
            P.op("vector", lambda en, k=k, a_ap=a_ap, tmp=tmp: en.scalar_tensor_tensor(
                out=tmp[:, 0:n], in0=x_t[:, k, 0:n], scalar=a_ap, in1=rst[:, 0:n], op0=ALU.mult, op1=ALU.mult),
                reads=[bx, brst, bMV], writes=[btmp])
            s_ap = mvcol(l, who, kinds[1], k)
            P.op("vector", lambda en, k=k, s_ap=s_ap, tmp=tmp: en.tensor_scalar(
                out=out_t[:, k, 0:n], in0=tmp[:, 0:n], scalar1=s_ap, scalar2=None, op0=ALU.add),
                reads=[btmp, bMV], writes=[bout])

    def src_cols(tl, lat_ap, ctx_ap_or_none, S_off=True):
        if tl["ctx"]:
            return (S, S + CTX)
        return (tl["t0"], tl["t0"] + tl["n"])

    def std_mm(W, bW, kc, col0, rhs_fn, n, reads):
        pst, bpst = ps_next()

        def _mm(en):
            for k in range(kc):
                ins = en.matmul(pst[:, 0:n], W[:, k, col0:col0 + 128], rhs_fn(k), start=(k == 0), stop=(k == kc - 1))
            return ins

        P.op("tensor", _mm, reads=bW + reads, writes=[bpst], chain=False)
        return pst, bpst

    def ffn_phase(l, src, bsrc, src_has_ctx, dst, bdst, final):
        T = 256
        ph = new_phase()
        W1, bW1 = wview(0, KC, 2 * DFF)
        W2, bW2 = wview(KC * 2 * DFF, NFC, D)
        wload(W1, bW1, ffn_w_in[l], KC, 2 * DFF)
        wload(W2, bW2, ffn_w_out[l], NFC, D)
        xr = Ring("fx", [128, KC, T], F32, 2, ph)
        hr = Ring("fh", [128, KC, T], BF16, 2, ph)
        sqr = Ring("fsq", [128, KC, T], BF16, 1, ph)
        rstr = Ring("frst", [128, T], F32, 2, ph)
        tmpr = Ring("ftmp", [128, T], F32, 2, ph)
        mr = Ring("fm", [128, NFC, T], BF16, 1, ph)
        sar = Ring("fsa", [128, T], F32, 3, ph)
        tls = tiles(T, with_ctx=src_has_ctx)
        st = {}

        def stageA(i, defer=None, defer1=None):
            tl = tls[i]
            n = tl["n"]
            x_t, bx = xr.next()
            h_t, bh = hr.next()
            c0, c1 = src_cols(tl, None, None)
            P.op("sync", lambda en: en.dma_start(out=x_t[:, :, 0:n], in_=fm(src, c0, c1)), reads=[bsrc], writes=[bx],
                 dma=True)
            norm_stage(x_t, bx, n, l, tl["who"], (A_F, S_F), h_t, bh, sqr, rstr, tmpr, defer=defer, defer1=defer1)
            st[i] = (x_t, bx, h_t, bh)

        def stageB(i):
            tl = tls[i]
            n = tl["n"]
            x_t, bx, h_t, bh = st.pop(i)
            m_t, bm = mr.next()
            pend = []
            pend1 = []
            if i + 1 < len(tls):
                stageA(i + 1, pend, pend1)
            for dc in range(NFC):
                pa, bpa = std_mm(W1, bW1, KC, dc * 128, lambda k: h_t[:, k, 0:n], n, [bh])
                pb, bpb = std_mm(W1, bW1, KC, DFF + dc * 128, lambda k: h_t[:, k, 0:n], n, [bh])
                sa, bsa = sar.next()
                P.op("scalar", lambda en, pa=pa, sa=sa: en.activation(out=sa[:, 0:n], in_=pa[:, 0:n], func=AF.Silu),
                     reads=[bpa], writes=[bsa])
                P.op("vector", lambda en, pb=pb, sa=sa, dc=dc: en.tensor_tensor(
                    out=m_t[:, dc, 0:n], in0=pb[:, 0:n], in1=sa[:, 0:n], op=ALU.mult), reads=[bpb, bsa], writes=[bm])
                if dc == 7:
                    for f_ in pend1:
                        f_()
                if dc == 15:
                    for f_ in pend:
                        f_()
            for oc in range(KC):
                py, bpy = std_mm(W2, bW2, NFC, oc * 128, lambda k: m_t[:, k, 0:n], n, [bm])
                g_ap = mvcol(l, tl["who"], G_F, oc)
                P.op("vector", lambda en, py=py, oc=oc, g_ap=g_ap: en.scalar_tensor_tensor(
                    out=x_t[:, oc, 0:n], in0=py[:, 0:n], scalar=g_ap, in1=x_t[:, oc, 0:n], op0=ALU.mult, op1=ALU.add),
                    reads=[bpy, bMV, bx], writes=[bx])
            if final:
                o_t, bo = x_t, bx
                norm_stage(x_t, bx, n, l, 0, None, o_t, bo, sqr, rstr, tmpr,
                           custom_a=lambda k: vcol("fnw", k), out_f32=True)
                P.op("sync", lambda en: en.dma_start(out=fm(dst, tl["t0"], tl["t0"] + n), in_=o_t[:, :, 0:n]),
                     reads=[bo], writes=[bdst], dma=True)
            else:
                c0, c1 = src_cols(tl, None, None)
                P.op("sync", lambda en: en.dma_start(out=fm(dst, c0, c1), in_=x_t[:, :, 0:n]), reads=[bx],
                     writes=[bdst], dma=True)

        stageA(0)
        for i in range(len(tls)):
            stageB(i)
        P.barrier()
        P.emit()
        ph.close()

    pos_eng = ["gpsimd"]

    def load_x0(tl, x_t, bx, pos_ring, defer1=None):
        n = tl["n"]
        if tl["ctx"]:
            P.op("sync", lambda en: en.dma_start(out=x_t[:, :, 0:n], in_=fm(ctxT, 0, CTX)), reads=[bIN], writes=[bx],
                 dma=True)
        else:
            p_t, bp = pos_ring.next()
            t0 = tl["t0"]
            P.op("sync", lambda en: en.dma_start(out=x_t[:, :, 0:n], in_=fm(xT, t0, t0 + n)), reads=[bIN],
                 writes=[bx], dma=True)
            P.op("sync", lambda en: en.dma_start(out=p_t[:, :, 0:n], in_=fm(posT, t0, t0 + n)), reads=[bIN],
                 writes=[bp], dma=True)
            def _pa():
                P.op(pos_eng[0], lambda en: en.tensor_tensor(out=x_t[:, :, 0:n], in0=x_t[:, :, 0:n],
                                                             in1=p_t[:, :, 0:n], op=ALU.add), reads=[bx, bp],
                     writes=[bx])

            if defer1 is not None:
                defer1.append(_pa)
            else:
                _pa()

    def phase1a():
        T = 256
        ph = new_phase()
        Wv, bWv = wview(0, KC, GMH)
        wload(Wv, bWv, gm_w_in[:, GMH:2 * GMH], KC, GMH)
        wsb = sb("wsb", [128, 1024], BF16, ph)
        bwsb = P.buf("wsb")
        P.op("gpsimd", lambda en: en.dma_start(out=wsb[:], in_=wsT_d[:, :]), reads=[bIN], writes=[bwsb], dma=True)
        bvb = sb("bvb", [128, GMH], BF16, ph)
        bbvb = P.buf("bvb")
        P.op("vector", lambda en: en.memset(bvb[:], 0.0), writes=[bbvb])
        P.op("gpsimd", lambda en: en.dma_start(out=bvb[0:1, :], in_=bvrow[:, :], max_dma_last_dim=4096), reads=[bIN],
             writes=[bbvb], dma=True)
        Ct = sb("Ct", [128, 24, 128], F32, ph)
        bCt = P.buf("Ct")
        amark = aoff[0]
        Rt = sb("Rt", [2, 1024], F32, ph)
        Lt = sb("Lt", [2, GMH], F32, ph)
        bRt = P.buf("Rt")
        bLt = P.buf("Lt")
        ws32 = sb("ws32", [128, 1024], F32, ph)
        bws32 = P.buf("ws32")
        P.op("sync", lambda en: en.dma_start(out=ws32[:], in_=wsT_d[:, :]), reads=[bIN], writes=[bws32], dma=True)
        ones32 = sb("ones32", [128, 2], F32, ph)
        bo32 = P.buf("ones32")
        P.op("vector", lambda en: en.memset(ones32[:], 1.0), writes=[bo32])
        P.op("vector", lambda en: en.memset(Lt[:], 1.0), writes=[bLt])
        P.op("sync", lambda en: en.dma_start(out=Lt[0:1, :], in_=lnbrow[:, :]), reads=[bIN], writes=[bLt], dma=True)
        P.op("sync", lambda en: en.dma_start(out=Rt[1:2, :], in_=bsrow[:, :]), reads=[bIN], writes=[bRt], dma=True)
        for hb in range(2):
            pst, bpst = ps_next()
            P.op("tensor", lambda en, pst=pst, hb=hb: en.matmul(pst[0:1, :], ones32[:, 0:1],
                                                               ws32[:, hb * 512:(hb + 1) * 512], start=True, stop=True),
                 reads=[bws32, bo32], writes=[bpst])
            P.op("vector", lambda en, pst=pst, hb=hb: en.tensor_copy(out=Rt[0:1, hb * 512:(hb + 1) * 512],
                                                                     in_=pst[0:1, :]), reads=[bpst], writes=[bRt])
        for dc in range(24):
            g = dc // 3
            pst, bpst = ps_next()
            P.op("tensor", lambda en, pst=pst, dc=dc, g=g: en.matmul(
                pst[:, 0:128], Lt[0:2, dc * 128:(dc + 1) * 128], Rt[0:2, g * 128:(g + 1) * 128], start=True, stop=True),
                reads=[bLt, bRt], writes=[bpst])
            P.op("vector", lambda en, pst=pst, dc=dc: en.tensor_copy(out=Ct[:, dc, :], in_=pst[:, 0:128]),
                 reads=[bpst], writes=[bCt])

        P.barrier()
        aoff[0] = amark
        xr = Ring("ax", [128, KC, T], F32, 2, ph)
        posr = Ring("apos", [128, KC, T], F32, 1, ph)
        hr = Ring("ah", [128, KC, T], BF16, 2, ph)
        sqr = Ring("asq", [128, KC, T], BF16, 1, ph)
        rstr = Ring("arst", [128, T], F32, 2, ph)
        tmpr = Ring("atmp", [128, T], F32, 3, ph)
        vgr = Ring("avg", [128, GMH], F32, 2, ph)
        vnr = Ring("avn", [128, GMH], BF16, 2, ph)
        str_ = Ring("ast", [128, 6, 6], F32, 2, ph)
        mvr = Ring("amv", [128, 8], F32, 2, ph)
        vpr = Ring("avp", [128, 24, T], BF16, 2, ph)
        tls = tiles(T)
        st = {}

        pending_sp = []

        def stageA(i, defer=None):
            tl = tls[i]
            x_t, bx = xr.next()
            h_t, bh = hr.next()
            load_x0(tl, x_t, bx, posr)
            norm_stage(x_t, bx, tl["n"], 0, tl["who"], (A_M, S_M), h_t, bh, sqr, rstr, tmpr, defer=defer)
            st[i] = (h_t, bh)

        def stageB(i):
            tl = tls[i]
            n = tl["n"]
            h_t, bh = st.pop(i)
            vp, bvp = vpr.next()
            pend = []
            if i + 1 < len(tls):
                stageA(i + 1, pend)
            nj = n // 128
            for j in range(nj):
                vg, bvg = vgr.next()
                stt, bstt = str_.next()
                for fb in range(6):
                    pst, bpst = ps_next()

                    def _mm(en, pst=pst, fb=fb, j=j):
                        for k in range(KC):
                            en.matmul(pst[:, :], h_t[:, k, j * 128:(j + 1) * 128], Wv[:, k, fb * 512:(fb + 1) * 512],
                                      start=(k == 0), stop=False)
                        return en.matmul(pst[:, :], ones_bf[:, :], bvb[:, fb * 512:(fb + 1) * 512], start=False,
                                         stop=True)

                    P.op("tensor", _mm, reads=bWv + [bh, bbvb, bCONST], writes=[bpst], chain=False)
                    P.op("scalar", lambda en, pst=pst, fb=fb, vg=vg: en.activation(
                        out=vg[:, fb * 512:(fb + 1) * 512], in_=pst[:, :], func=AF.Gelu), reads=[bpst], writes=[bvg])
                    P.op("vector", lambda en, fb=fb, vg=vg, stt=stt: en.bn_stats(
                        out=stt[:, fb, :], in_=vg[:, fb * 512:(fb + 1) * 512]), reads=[bvg], writes=[bstt])
                mv, bmv = mvr.next()

                P.op("vector", lambda en, stt=stt, mv=mv: en.bn_aggr(out=mv[:, 0:2],
                                                                     in_=stt[:].rearrange("p a b -> p (a b)")),
                     reads=[bstt], writes=[bmv])
                P.op("vector", lambda en, mv=mv: en.tensor_scalar(out=mv[:, 2:3], in0=mv[:, 1:2], scalar1=EPS,
                                                                   scalar2=None, op0=ALU.add), reads=[bmv], writes=[bmv])
                P.op("scalar", lambda en, mv=mv: en.activation(out=mv[:, 2:3], in_=mv[:, 2:3], func=AF.Sqrt),
                     reads=[bmv], writes=[bmv])
                P.op("vector", lambda en, mv=mv: en.reciprocal(out=mv[:, 3:4], in_=mv[:, 2:3]), reads=[bmv],
                     writes=[bmv])
                P.op("vector", lambda en, mv=mv: en.tensor_scalar(out=mv[:, 4:5], in0=mv[:, 0:1], scalar1=-1.0,
                                                                   scalar2=mv[:, 3:4], op0=ALU.mult, op1=ALU.mult),
                     reads=[bmv], writes=[bmv])
                vn, bvn = vnr.next()
                P.op("gpsimd", lambda en, vg=vg, vn=vn, mv=mv: en.tensor_scalar(
                    out=vn[:], in0=vg[:], scalar1=mv[:, 3:4], scalar2=mv[:, 4:5], op0=ALU.mult, op1=ALU.add),
                    reads=[bvg, bmv], writes=[bvn])
                while pending_sp:
                    pending_sp.pop(0)()
                pending_sp.append(lambda j=j, vn=vn, bvn=bvn, vp=vp, bvp=bvp, tl=tl, n=n, nj=nj: spatial(
                    j, vn, bvn, vp, bvp, tl, n, j == nj - 1))
                if j == 0:
                    for f_ in pend:
                        f_()

        def spatial(j, vn, bvn, vp, bvp, tl, n, last):
            if True:
                for q4 in range(6):
                    pst, bpst = ps_next()

                    def _sp(en, pst=pst, q4=q4, vn=vn):
                        for r in range(4):
                            dc = q4 * 4 + r
                            ins = en.matmul(pst[:, r * 128:(r + 1) * 128], vn[:, dc * 128:(dc + 1) * 128],
                                            wsb[:, (dc // 3) * 128:(dc // 3 + 1) * 128], start=True, stop=True)
                        return ins

                    P.op("tensor", _sp, reads=[bvn, bwsb], writes=[bpst], chain=False)
                    for r in range(4):
                        dc = q4 * 4 + r
                        P.op("vector", lambda en, pst=pst, r=r, dc=dc, j=j: en.scalar_tensor_tensor(
                            out=vp[:, dc, j * 128:(j + 1) * 128], in0=pst[:, r * 128:(r + 1) * 128],
                            scalar=vcol("lng", dc), in1=Ct[:, dc, :], op0=ALU.mult, op1=ALU.add),
                            reads=[bpst, bVv, bCt], writes=[bvp])
            if last:
                c0, c1 = src_cols(tl, None, None)
                P.op("sync", lambda en: en.dma_start(out=fm(V1, c0, c1), in_=vp[:, :, 0:n]), reads=[bvp], writes=[bV1],
                     dma=True)

        stageA(0)
        for i in range(len(tls)):
            stageB(i)
        while pending_sp:
            pending_sp.pop(0)()
        P.barrier()
        P.emit()
        ph.close()

    def phase1b():
        T = 256
        pos_eng[0] = "vector"
        ph = new_phase()
        Wu, bWu = wview(0, KC, GMH)
        Wo, bWo = wview(KC * GMH, 24, D)
        wload(Wu, bWu, gm_w_in[:, 0:GMH], KC, GMH)
        wload(Wo, bWo, gm_w_out, 24, D)
        xr = Ring("bx", [128, KC, T], F32, 3, ph)
        posr = Ring("bpos", [128, KC, T], F32, 1, ph)
        hr = Ring("bh", [128, KC, T], BF16, 2, ph)
        sqr = Ring("bsq", [128, KC, T], BF16, 1, ph)
        rstr = Ring("brst", [128, T], F32, 2, ph)
        tmpr = Ring("btmp", [128, T], F32, 3, ph)
        vpr = Ring("bvp", [128, 24, T], BF16, 2, ph)
        ur = Ring("bu", [128, T], BF16, 3, ph)
        tls = tiles(T)
        st = {}

        def stageA(i, defer=None, defer1=None):
            tl = tls[i]
            n = tl["n"]
            x_t, bx = xr.next()
            h_t, bh = hr.next()
            vp, bvp = vpr.next()
            load_x0(tl, x_t, bx, posr, defer1)
            c0, c1 = src_cols(tl, None, None)
            P.op("sync", lambda en: en.dma_start(out=vp[:, :, 0:n], in_=fm(V1, c0, c1)), reads=[bV1], writes=[bvp],
                 dma=True)
            norm_stage(x_t, bx, n, 0, tl["who"], (A_M, S_M), h_t, bh, sqr, rstr, tmpr, defer=defer, defer1=defer1)
            st[i] = (x_t, bx, h_t, bh, vp, bvp)

        def stageB(i):
            tl = tls[i]
            n = tl["n"]
            x_t, bx, h_t, bh, vp, bvp = st.pop(i)
            pend = []
            pend1 = []
            if i + 1 < len(tls):
                stageA(i + 1, pend, pend1)
            for dc in range(24):
                pu, bpu = std_mm(Wu, bWu, KC, dc * 128, lambda k: h_t[:, k, 0:n], n, [bh])
                u_t, bu = ur.next()
                P.op("scalar", lambda en, pu=pu, u_t=u_t, dc=dc: en.activation(
                    out=u_t[:, 0:n], in_=pu[:, 0:n], func=AF.Gelu, bias=vcol("bu", dc), scale=1.0),
                    reads=[bpu, bVv], writes=[bu])
                P.op("vector", lambda en, u_t=u_t, dc=dc: en.tensor_tensor(
                    out=vp[:, dc, 0:n], in0=vp[:, dc, 0:n], in1=u_t[:, 0:n], op=ALU.mult), reads=[bu, bvp],
                    writes=[bvp])
                if dc == 9:
                    for f_ in pend1:
                        f_()
                if dc == 17:
                    for f_ in pend:
                        f_()
            for oc in range(KC):
                py, bpy = std_mm(Wo, bWo, 24, oc * 128, lambda k: vp[:, k, 0:n], n, [bvp])
                g_ap = mvcol(0, tl["who"], G_M, oc)
                P.op("vector", lambda en, py=py, oc=oc, g_ap=g_ap: en.scalar_tensor_tensor(
                    out=x_t[:, oc, 0:n], in0=py[:, 0:n], scalar=g_ap, in1=x_t[:, oc, 0:n], op0=ALU.mult, op1=ALU.add),
                    reads=[bpy, bMV, bx], writes=[bx])
            c0, c1 = src_cols(tl, None, None)
            P.op("sync", lambda en: en.dma_start(out=fm(XA, c0, c1), in_=x_t[:, :, 0:n]), reads=[bx], writes=[bXA],
                 dma=True)

        stageA(0)
        for i in range(len(tls)):
            stageB(i)
        P.barrier()
        P.emit()
        ph.close()

    def chain_pre(kx_t, bkx, qd_t, bqd, vt_t, bvt, c, direction, rings, do_out):
        attr, kxtr, xbr, x32r = rings
        c64 = slice(c * 64, (c + 1) * 64)
        mask = maskF if direction == 0 else maskB
        attm = battm = None
        if do_out:
            pa, bpa = ps_next()

            def _att(en):
                for h in range(8):
                    ins = en.matmul(pa[0:64, h * 64:(h + 1) * 64], kx_t[:, h, c64], qd_t[:, h, c64], start=True,
                                    stop=True)
                return ins

            P.op("tensor", _att, reads=[bkx, bqd], writes=[bpa])
            attm, battm = attr.next()
            P.op("vector", lambda en: en.tensor_tensor(out=attm[:, :], in0=pa[0:64, :], in1=mask[:, :], op=ALU.mult),
                 reads=[bpa, bCONST], writes=[battm])
        pt, bpt = ps_next()
        ptb = pt[:].bitcast(BF16)

        def _tr(en):
            for h in range(8):
                ins = en.transpose(ptb[0:64, h * 128:(h + 1) * 128], kx_t[:, h, c64], ident_bf[:])
            return ins

        P.op("tensor", _tr, reads=[bkx, bCONST], writes=[bpt])
        kxt, bkxt = kxtr.next()
        P.op("scalar", lambda en: en.activation(out=kxt[:, :], in_=ptb[0:64, :], func=AF.Copy), reads=[bpt],
             writes=[bkxt])
        pks = []
        for hb in range(2):
            pk, bpk = ps_next()

            def _kv(en, hb=hb, pk=pk):
                for r in range(4):
                    h = hb * 4 + r
                    ins = en.matmul(pk[:, r * 128:(r + 1) * 128], kxt[:, h * 128:(h + 1) * 128],
                                    vt_t[0:64, c, h * 128:(h + 1) * 128], start=True, stop=True)
                return ins

            P.op("tensor", _kv, reads=[bkxt, bvt], writes=[bpk])
            pks.append((pk, bpk))
        return (attm, battm, pks)

    def chain_rec(pre, qd_t, bqd, vt_t, bvt, c, gc, Y, bY, Dt, bD, rings, do_out, of_t=None, bof=None, add_prev=False):
        attr, kxtr, xbr, x32r = rings
        attm, battm, pks = pre
        c64 = slice(c * 64, (c + 1) * 64)
        x32, bx32 = x32r.next()
        Dbc = Dt[:, :, gc:gc + 1].broadcast_to([128, 8, 128])
        P.op("vector", lambda en: en.tensor_tensor(out=x32[:, :, :], in0=Y[:, :, :], in1=Dbc, op=ALU.mult),
             reads=bY + [bD], writes=[bx32])
        for hb in range(2):
            pk, bpk = pks[hb]
            P.op("vector", lambda en, hb=hb, pk=pk: en.tensor_tensor(
                out=Y[:, hb * 4:(hb + 1) * 4, :], in0=x32[:, hb * 4:(hb + 1) * 4, :],
                in1=pk[:].rearrange("p (r v) -> p r v", r=4), op=ALU.add), reads=[bx32, bpk],
                writes=bY[hb * 4:(hb + 1) * 4])
        if do_out:
            xb, bxb = xbr.next()
            P.op("scalar", lambda en: en.activation(out=xb[:, :, :], in_=x32[:, :, :], func=AF.Copy), reads=[bx32],
                 writes=[bxb])
            po, bpo = ps_next()

            def _o(en):
                for h in range(8):
                    en.matmul(po[:, h * 64:(h + 1) * 64], vt_t[0:64, c, h * 128:(h + 1) * 128],
                              attm[:, h * 64:(h + 1) * 64], start=True, stop=False)
                    ins = en.matmul(po[:, h * 64:(h + 1) * 64], xb[:, h, :], qd_t[:, h, c64], start=False, stop=True)
                return ins

            P.op("tensor", _o, reads=[bvt, battm, bxb, bqd], writes=[bpo])
            pov = po[:].rearrange("p (h t) -> p h t", h=8)
            if add_prev:
                P.op("vector", lambda en: en.tensor_tensor(out=of_t[:, :, c64], in0=pov, in1=of_t[:, :, c64],
                                                           op=ALU.add), reads=[bpo, bof], writes=[bof])
            else:
                P.op("scalar", lambda en: en.activation(out=of_t[:, :, c64], in_=pov, func=AF.Copy), reads=[bpo],
                     writes=[bof])

    def run_chain(chunks, kx_t, bkx, qd_t, bqd, vt_t, bvt, direction, Y, bY, Dt, bD, rings, do_out, of_t=None,
                  bof=None, add_prev=False):
        pre = chain_pre(kx_t, bkx, qd_t, bqd, vt_t, bvt, chunks[0][0], direction, rings, do_out)
        for i_, (c, gc) in enumerate(chunks):
            nxt = None
            if i_ + 1 < len(chunks):
                nxt = chain_pre(kx_t, bkx, qd_t, bqd, vt_t, bvt, chunks[i_ + 1][0], direction, rings, do_out)
            chain_rec(pre, qd_t, bqd, vt_t, bvt, c, gc, Y, bY, Dt, bD, rings, do_out, of_t, bof, add_prev)
            pre = nxt

    def phase3():
        T = 256
        NC4 = T // 64
        ph = new_phase()
        W, bW = wview(0, KC, 5 * D)
        wload(W, bW, hg_w_in, KC, 5 * D)
        xr = Ring("cx", [128, KC, T], F32, 1, ph)
        hr = Ring("ch", [128, KC, T], BF16, 1, ph)
        sqr = Ring("csq", [128, KC, T], BF16, 1, ph)
        rstr = Ring("crst", [128, T], F32, 1, ph)
        tmpr = Ring("ctmp", [128, T], F32, 2, ph)
        qr = Ring("cq", [128, 8, T], BF16, 1, ph)
        sgr = Ring("csg", [128, 8, T], BF16, 1, ph)
        vtr = Ring("cvt", [64, NC4, D], BF16, 1, ph)
        qdr = [Ring("cqd%d" % d, [128, 8, T], BF16, 1, ph) for d in range(2)]
        kxr = [Ring("ckx%d" % d, [128, 8, T], BF16, 1, ph) for d in range(2)]
        ofr = Ring("cof", [128, 8, T], F32, 1, ph)
        tAr = Ring("ctA", [128, T], F32, 4, ph)
        tBr = Ring("ctB", [128, T + 1], F32, 4, ph)
        tCr = Ring("ctC", [128, T], F32, 4, ph)
        tDr = Ring("ctD", [128, T], F32, 4, ph)
        bscr = Ring("cbs", [128, T], F32, 4, ph)
        tsr = Ring("cts", [128, NC4], F32, 4, ph)
        rings = (Ring("catt", [64, 512], BF16, 2, ph), Ring("ckxt", [64, 1024], BF16, 2, ph),
                 Ring("cxb", [128, 8, 128], BF16, 2, ph), Ring("cx32", [128, 8, 128], F32, 2, ph))
        for t_, b_ in zip(tBr.t, tBr.b):
            P.op("vector", lambda en, t_=t_: en.memset(t_[:], 0.0), writes=[b_])
        tls = tiles(T)
        ctx_kxb = None
        def p3_tile(i, tl):
            n = tl["n"]
            isctx = tl["ctx"]
            gc0 = 0 if isctx else CTX // 64 + tl["t0"] // 64
            x_t, bx = xr.next()
            h_t, bh = hr.next()
            c0, c1 = src_cols(tl, None, None)
            P.op("sync", lambda en, x_t=x_t, c0=c0, c1=c1, n=n: en.dma_start(out=x_t[:, :, 0:n], in_=fm(XB, c0, c1)),
                 reads=[bXB], writes=[bx], dma=True)
            norm_stage(x_t, bx, n, 1, tl["who"], (A_M, S_M), h_t, bh, sqr, rstr, tmpr, lnexp=True)
            q_t = bq = sg_t = bsg = None
            if not isctx:
                q_t, bq = qr.next()
                sg_t, bsg = sgr.next()
                for h in range(8):
                    pq, bpq = std_mm(W, bW, KC, h * 128, lambda k: h_t[:, k, 0:n], n, [bh])
                    P.op("scalar", lambda en, pq=pq, h=h, q_t=q_t: en.activation(out=q_t[:, h, 0:n], in_=pq[:, 0:n],
                                                                                func=AF.Silu), reads=[bpq], writes=[bq])
                for h in range(8):
                    pq, bpq = std_mm(W, bW, KC, 4 * D + h * 128, lambda k: h_t[:, k, 0:n], n, [bh])
                    P.op("scalar", lambda en, pq=pq, h=h, sg_t=sg_t: en.activation(out=sg_t[:, h, 0:n], in_=pq[:, 0:n],
                                                                                  func=AF.Silu), reads=[bpq],
                         writes=[bsg])
                t0 = tl["t0"]
                P.op("sync", lambda en, sg_t=sg_t, t0=t0, n=n: en.dma_start(out=fm(SG, t0, t0 + n), in_=sg_t[:, :, 0:n]),
                     reads=[bsg], writes=[bSG], dma=True)
            vt_t, bvt = vtr.next()
            for c in range(n // 64):
                for nb in range(2):
                    pst, bpst = ps_next()

                    def _mi(en, pst=pst, c=c, nb=nb):
                        for k in range(KC):
                            ins = en.matmul(pst[0:64, :], h_t[:, k, c * 64:(c + 1) * 64],
                                            W[:, k, 3 * D + nb * 512:3 * D + (nb + 1) * 512], start=(k == 0),
                                            stop=(k == KC - 1))
                        return ins

                    P.op("tensor", _mi, reads=bW + [bh], writes=[bpst])
                    P.op("vector", lambda en, pst=pst, c=c, nb=nb, vt_t=vt_t: en.tensor_copy(
                        out=vt_t[:, c, nb * 512:(nb + 1) * 512], in_=pst[0:64, :]), reads=[bpst], writes=[bvt])
            if not isctx:
                t0 = tl["t0"]
                P.op("sync", lambda en, vt_t=vt_t, t0=t0, n=n: en.dma_start(
                    out=VT.rearrange("(c s) f -> s c f", s=64)[:, t0 // 64:(t0 + n) // 64, :], in_=vt_t[:, 0:n // 64, :]),
                    reads=[bvt], writes=[bVT], dma=True)
            qd = [None, None]
            kx = [None, None]
            for d in range(2):
                kx[d] = kxr[d].next()
                if not isctx:
                    qd[d] = qdr[d].next()
            nch = n // 64
            items = [(d, h) for d in range(2) for h in range(8)]
            WAVE = 4
            for w0 in range(0, 16, WAVE):
                grp = items[w0:w0 + WAVE]
                bufs = {}
                for (d, h) in grp:
                    pz, bpz = std_mm(W, bW, KC, (1 + d) * D + h * 128, lambda k: h_t[:, k, 0:n], n, [bh])
                    tA, btA = tAr.next()
                    tB, btB = tBr.next()
                    tC, btC = tCr.next()
                    tD, btD = tDr.next()
                    bs, bbs = bscr.next()
                    bufs[(d, h)] = (tA, btA, tB, btB, tC, btC, tD, btD, bs, bbs)
                    P.op("scalar", lambda en, pz=pz, tA=tA: en.activation(out=tA[:, 0:n], in_=pz[:, 0:n],
                                                                         func=AF.Exp), reads=[bpz], writes=[btA])
                for (d, h) in grp:
                    tA, btA, tB, btB, tC, btC, tD, btD, bs, bbs = bufs[(d, h)]
                    P.op("scalar", lambda en, tA=tA, tC=tC: en.activation(out=tC[:, 0:n], in_=tA[:, 0:n], func=AF.Ln,
                                                                         bias=1.0, scale=1.0), reads=[btA],
                         writes=[btC])
                    P.op("scalar", lambda en, tA=tA, tC=tC: en.activation(out=tA[:, 0:n], in_=tC[:, 0:n], func=AF.Exp,
                                                                         scale=-1.0), reads=[btC], writes=[btA])
                for (d, h) in grp:
                    tA, btA, tB, btB, tC, btC, tD, btD, bs, bbs = bufs[(d, h)]
                    P.op("scalar", lambda en, tA=tA, tB=tB, d=d, h=h: en.activation(
                        out=tB[:, 1:n + 1], in_=tA[:, 0:n], func=AF.Ln, bias=1.0,
                        scale=OML[:, 16 + d * 8 + h:17 + d * 8 + h]), reads=[btA, bOML], writes=[btB])
                for (d, h) in grp:
                    tA, btA, tB, btB, tC, btC, tD, btD, bs, bbs = bufs[(d, h)]
                    if d == 0:
                        P.op("vector", lambda en, tB=tB, bs=bs: en.tensor_tensor_scan(
                            out=bs[:, 0:n], data0=mask01[:, 0:n], data1=tB[:, 1:n + 1], initial=0.0, op0=ALU.mult,
                            op1=ALU.add), reads=[btB, bCONST], writes=[bbs])
                    else:
                        P.op("vector", lambda en, tB=tB, bs=bs: en.tensor_tensor_scan(
                            out=bs[:, 0:n], data0=tB[:, 0:n], data1=mask01[:, 0:n], initial=0.0, op0=ALU.add,
                            op1=ALU.mult), reads=[btB, bCONST], writes=[bbs])
                for (d, h) in grp:
                    tA, btA, tB, btB, tC, btC, tD, btD, bs, bbs = bufs[(d, h)]
                    sq_, sk_ = (1.0, -1.0) if d == 0 else (-1.0, 1.0)
                    P.op("scalar", lambda en, bs=bs, tC=tC, sq_=sq_: en.activation(out=tC[:, 0:n], in_=bs[:, 0:n],
                                                                                  func=AF.Exp, scale=sq_),
                         reads=[bbs], writes=[btC])
                    P.op("scalar", lambda en, bs=bs, tD=tD, sk_=sk_: en.activation(out=tD[:, 0:n], in_=bs[:, 0:n],
                                                                                  func=AF.Exp, scale=sk_),
                         reads=[bbs], writes=[btD])
                for (d, h) in grp:
                    tA, btA, tB, btB, tC, btC, tD, btD, bs, bbs = bufs[(d, h)]
                    if not isctx:
                        P.op("vector", lambda en, tC=tC, h=h, d=d: en.tensor_tensor(
                            out=qd[d][0][:, h, 0:n], in0=q_t[:, h, 0:n], in1=tC[:, 0:n], op=ALU.mult),
                            reads=[btC, bq], writes=[qd[d][1]])
                    P.op("vector", lambda en, tA=tA, tD=tD, h=h, d=d: en.scalar_tensor_tensor(
                        out=kx[d][0][:, h, 0:n], in0=tA[:, 0:n], scalar=OML[:, d * 8 + h:d * 8 + h + 1], in1=tD[:, 0:n],
                        op0=ALU.mult, op1=ALU.mult), reads=[btA, btD, bOML], writes=[kx[d][1]])
                    if d == 0:
                        P.op("vector", lambda en, tC=tC, h=h: en.tensor_copy(
                            out=Df[:, h, gc0 + 1:gc0 + 1 + nch], in_=tC[:, 63:n:64]), reads=[btC], writes=[bDf])
                    else:
                        ts_, bts = tsr.next()
                        P.op("vector", lambda en, bs=bs, tB=tB, ts_=ts_: en.tensor_tensor(
                            out=ts_[:, 0:nch], in0=bs[:, 63:n:64], in1=tB[:, 64:n + 1:64], op=ALU.add),
                            reads=[bbs, btB], writes=[bts])
                        P.op("scalar", lambda en, ts_=ts_, h=h: en.activation(
                            out=Db[:, h, gc0:gc0 + nch], in_=ts_[:, 0:nch], func=AF.Exp), reads=[bts], writes=[bDb])
            if isctx:
                run_chain([(c, gc0 + c) for c in range(n // 64)], kx[0][0], kx[0][1], None, None, vt_t, bvt, 0, Yf,
                          bYf, Df, bDf, rings, False)
                run_chain([(c, gc0 + c) for c in reversed(range(n // 64))], kx[1][0], kx[1][1], None, None,
                          vt_t, bvt, 1, Yb, bYb, Db, bDb, rings, False)
            else:
                of_t, bof = ofr.next()
                run_chain([(c, gc0 + c) for c in range(n // 64)], kx[0][0], kx[0][1], qd[0][0], qd[0][1], vt_t, bvt,
                          0, Yf, bYf, Df, bDf, rings, True, of_t, bof)
                t0 = tl["t0"]
                P.op("sync", lambda en, of_t=of_t, t0=t0, n=n: en.dma_start(out=fm(OF, t0, t0 + n), in_=of_t[:, :, 0:n]),
                     reads=[bof], writes=[bOF], dma=True)
                P.op("sync", lambda en, q=qd[1][0], t0=t0, n=n: en.dma_start(out=fm(QDB, t0, t0 + n), in_=q[:, :, 0:n]),
                     reads=[qd[1][1]], writes=[bQDB], dma=True)
                P.op("sync", lambda en, q=kx[1][0], t0=t0, n=n: en.dma_start(out=fm(KXB, t0, t0 + n), in_=q[:, :, 0:n]),
                     reads=[kx[1][1]], writes=[bKXB], dma=True)

        for i, tl in enumerate(tls):
            p3_tile(i, tl)
        P.barrier()
        P.emit()
        ph.close()

    def phase4():
        T = 256
        NC4 = T // 64
        ph = new_phase()
        Wo, bWo = wview(0, KC, D)
        wload(Wo, bWo, hg_w_out, KC, D)
        xr = Ring("dx", [128, KC, T], F32, 2, ph)
        qdr = Ring("dqd", [128, 8, T], BF16, 2, ph)
        kxr = Ring("dkx", [128, 8, T], BF16, 2, ph)
        sgr = Ring("dsg", [128, 8, T], BF16, 2, ph)
        vtr = Ring("dvt", [64, NC4, D], BF16, 2, ph)
        ofr = Ring("dof", [128, 8, T], F32, 2, ph)
        sqr = Ring("dsq", [128, 8, T], BF16, 1, ph)
        rsr = Ring("drs", [128, 8, T], F32, 1, ph)
        tmpr = Ring("dtmp", [128, T], F32, 3, ph)
        mr = Ring("dm", [128, 8, T], BF16, 1, ph)
        rings = (Ring("datt", [64, 512], BF16, 2, ph), Ring("dkxt", [64, 1024], BF16, 2, ph),
                 Ring("dxb", [128, 8, 128], BF16, 2, ph), Ring("dx32", [128, 8, 128], F32, 2, ph))
        tls = list(reversed(tiles(T, with_ctx=False)))
        st = {}

        def stageA(i):
            tl = tls[i]
            n, t0 = tl["n"], tl["t0"]
            x_t, bx = xr.next()
            qd, bqd = qdr.next()
            kx, bkx = kxr.next()
            sg, bsg = sgr.next()
            vt, bvt = vtr.next()
            of, bof = ofr.next()
            P.op("sync", lambda en: en.dma_start(out=qd[:, :, 0:n], in_=fm(QDB, t0, t0 + n)), reads=[bQDB], writes=[bqd],
                 dma=True)
            P.op("sync", lambda en: en.dma_start(out=kx[:, :, 0:n], in_=fm(KXB, t0, t0 + n)), reads=[bKXB], writes=[bkx],
                 dma=True)
            P.op("sync", lambda en: en.dma_start(
                out=vt[:, 0:n // 64, :], in_=VT.rearrange("(c s) f -> s c f", s=64)[:, t0 // 64:(t0 + n) // 64, :]),
                reads=[bVT], writes=[bvt], dma=True)
            P.op("sync", lambda en: en.dma_start(out=of[:, :, 0:n], in_=fm(OF, t0, t0 + n)), reads=[bOF], writes=[bof],
                 dma=True)
            P.op("sync", lambda en: en.dma_start(out=sg[:, :, 0:n], in_=fm(SG, t0, t0 + n)), reads=[bSG], writes=[bsg],
                 dma=True)
            P.op("sync", lambda en: en.dma_start(out=x_t[:, :, 0:n], in_=fm(XB, t0, t0 + n)), reads=[bXB], writes=[bx],
                 dma=True)
            st[i] = (x_t, bx, qd, bqd, kx, bkx, sg, bsg, vt, bvt, of, bof)

        def stageB(i):
            tl = tls[i]
            n, t0 = tl["n"], tl["t0"]
            x_t, bx, qd, bqd, kx, bkx, sg, bsg, vt, bvt, of, bof = st.pop(i)
            run_chain([(c, CTX // 64 + t0 // 64 + c) for c in reversed(range(n // 64))], kx, bkx, qd,
                      bqd, vt, bvt, 1, Yb, bYb, Db, bDb, rings, True, of, bof, add_prev=True)
            if i + 1 < len(tls):
                stageA(i + 1)
            sq_t, bsq = sqr.next()
            P.op("gpsimd", lambda en: en.tensor_tensor(out=sq_t[:, :, 0:n], in0=of[:, :, 0:n], in1=of[:, :, 0:n],
                                                       op=ALU.mult), reads=[bof], writes=[bsq])
            rs, brs = rsr.next()
            for h in range(8):
                pst, bpst = ps_next()
                P.op("tensor", lambda en, pst=pst, h=h: en.matmul(pst[:, 0:n], ones_bf[:], sq_t[:, h, 0:n], start=True,
                                                                 stop=True), reads=[bsq, bCONST], writes=[bpst])
                P.op("scalar", lambda en, pst=pst, h=h: en.activation(out=rs[:, h, 0:n], in_=pst[:, 0:n], func=AF.Sqrt,
                                                                     bias=EPS, scale=1.0 / 128), reads=[bpst],
                     writes=[brs])
            P.op("vector", lambda en: en.reciprocal(out=rs[:, :, 0:n], in_=rs[:, :, 0:n]), reads=[brs], writes=[brs])
            m_t, bm = mr.next()
            for h in range(8):
                tmp, btmp = tmpr.next()
                P.op("vector", lambda en, h=h, tmp=tmp: en.scalar_tensor_tensor(
                    out=tmp[:, 0:n], in0=of[:, h, 0:n], scalar=vcol("hnw", h), in1=rs[:, h, 0:n], op0=ALU.mult,
                    op1=ALU.mult), reads=[bof, brs, bVv], writes=[btmp])
                P.op("gpsimd", lambda en, h=h, tmp=tmp: en.tensor_tensor(
                    out=m_t[:, h, 0:n], in0=tmp[:, 0:n], in1=sg[:, h, 0:n], op=ALU.mult), reads=[btmp, bsg],
                    writes=[bm])
            for oc in range(KC):
                py, bpy = std_mm(Wo, bWo, KC, oc * 128, lambda k: m_t[:, k, 0:n], n, [bm])
                g_ap = mvcol(1, 0, G_M, oc)
                P.op("vector", lambda en, py=py, oc=oc, g_ap=g_ap: en.scalar_tensor_tensor(
                    out=x_t[:, oc, 0:n], in0=py[:, 0:n], scalar=g_ap, in1=x_t[:, oc, 0:n], op0=ALU.mult, op1=ALU.add),
                    reads=[bpy, bMV, bx], writes=[bx])
            P.op("sync", lambda en: en.dma_start(out=fm(XC, t0, t0 + n), in_=x_t[:, :, 0:n]), reads=[bx], writes=[bXC],
                 dma=True)

        stageA(0)
        for i in range(len(tls)):
            stageB(i)
        P.barrier()
        P.emit()
        ph.close()

    phases = [("p1a", phase1a), ("p1b", phase1b), ("p2", lambda: ffn_phase(0, XA, bXA, True, XB, bXB, False)),
              ("p3", phase3), ("p4", phase4), ("p5", lambda: ffn_phase(1, XC, bXC, False, outT, bOUT, True))]
    for name, f in phases:
        f()
        if stop_after == name:
            break
    P.barrier()
    P.emit()
    es.close()
    return nc


def make_in_maps(inputs, S):
    f = lambda a: np.ascontiguousarray(np.asarray(a, dtype=np.float32))
    x = f(inputs["x"])
    B = x.shape[0]
    pos = pos_table(S)
    gm_w_s = f(inputs["gm_w_s"])[0]
    wsT = np.ascontiguousarray(gm_w_s.transpose(2, 0, 1).reshape(128, 1024))
    shared = {
        "posT": pos,
        "bvrow": f(inputs["gm_b_in"])[0, GMH:].reshape(1, GMH).copy(),
        "lnbrow": f(inputs["gm_ln_b"])[0].reshape(1, GMH).copy(),
        "bsrow": f(inputs["gm_b_s"])[0].reshape(1, 1024).copy(),
        "wsT": wsT,
        "ada_w": f(inputs["ada_w"]),
        "gm_w_in": f(inputs["gm_w_in"])[0],
        "gm_w_out": f(inputs["gm_w_out"])[0],
        "hg_w_in": f(inputs["hg_w_in"])[0],
        "hg_w_out": f(inputs["hg_w_out"])[0],
        "ffn_w_in": f(inputs["ffn_w_in"]),
        "ffn_w_out": f(inputs["ffn_w_out"]),
    }
    hg_lb = f(inputs["hg_lb"])
    maps = []
    for b in range(B):
        vec = np.zeros((128, NV), np.float32)

        def put(name, arr):
            o, w = VEC_LAYOUT[name]
            assert arr.shape == (128, w), (name, arr.shape)
            vec[:, o:o + w] = arr

        cc = np.stack([_col(inputs["c"][b]), _col(inputs["c_ctx"])], axis=-1).reshape(128, 16)
        put("cc", cc)
        for l in range(2):
            put("ada_b%d" % l, _col(inputs["ada_b"][l]))
            put("nmw%d" % l, _col(inputs["norm_mix_w"][l]))
            put("nfw%d" % l, _col(inputs["norm_ffn_w"][l]))
            put("lb%d" % l, np.concatenate([_col(hg_lb[l, 0]), _col(hg_lb[l, 1])], axis=1))
        put("bu", _col(f(inputs["gm_b_in"])[0, :GMH]))
        put("lng", _col(inputs["gm_ln_g"][0]))
        put("hnw", _col(inputs["hg_norm_w"][0]))
        put("fnw", _col(inputs["final_norm_w"]))
        m = dict(shared)
        m["xT"] = np.ascontiguousarray(x[b, :S].T)
        m["ctxT"] = np.ascontiguousarray(f(inputs["ctx"])[b].T)
        m["vecs"] = vec
        maps.append(m)
    return maps


def kernel(**inputs):
    S = inputs["x"].shape[1]
    nc = build(S)
    maps = make_in_maps(inputs, S)
    res = run_bass_kernel_spmd(nc, maps, core_ids=list(range(len(maps))))
    out = np.stack([np.ascontiguousarray(r["outT"].T) for r in res.results], axis=0)
    return out.astype(np.float32)
```

```python
import contextlib
import numpy as np
import concourse.bass as bass
import concourse.mybir as mybir
from concourse.alu_op_type import AluOpType as ALU
from concourse.bass_utils import run_bass_kernel_spmd

F32 = mybir.dt.float32
BF16 = mybir.dt.bfloat16
AF = mybir.ActivationFunctionType
ENGS = ["tensor", "vector", "scalar", "gpsimd", "sync"]

D = 1024
KC = 8
CTX = 256
SEQ = 8192
GRID_W = 64
EPS = 1e-6
GMH = 3072
DFF = 2816
NFC = 22
PE_PARTIAL_CHAIN = True
STRICT_ENGS = {"vector", "scalar", "gpsimd", "sync"}


class Buf:
    __slots__ = ("name", "last_w", "readers", "dcount", "sem")

    def __init__(self, name):
        self.name = name
        self.last_w = None
        self.readers = []
        self.dcount = 0
        self.sem = None


class Prog:
    def __init__(self, nc, es):
        self.nc = nc
        self.es = es
        self.ops = {e: [] for e in ENGS}
        self.nemit = {e: 0 for e in ENGS}
        self.seen = {e: {} for e in ENGS}
        self.esem = {e: es.enter_context(nc.semaphore("e_" + e)) for e in ENGS}
        self.ecount = {e: 0 for e in ENGS}
        self.dbufs = []
        self.allbufs = []
        self.prev_chained = True

    def buf(self, name):
        b = Buf(name)
        self.allbufs.append(b)
        return b

    def _need(self, eng, tok):
        key, val = tok
        if self.seen[eng].get(key, -1) >= val:
            return False
        self.seen[eng][key] = val
        return True

    def op(self, eng, fn, reads=(), writes=(), dma=False, chain=True, nowaw=False):
        idx = len(self.ops[eng])
        deps = []
        for b in reads:
            if b.last_w is not None:
                deps.append(b.last_w)
        for b in writes:
            if b.last_w is not None and not (nowaw and b.last_w[0] == ("d", id(b))):
                deps.append(b.last_w)
            deps.extend(b.readers)
        waits = []
        for tok in deps:
            key, val = tok
            if key == ("e", "tensor") and eng == "tensor" and not dma:
                continue
            if self._need(eng, tok):
                waits.append(tok)
        rec = {"fn": fn, "waits": waits, "inc": False, "dma": None, "chain": chain}
        if dma:
            dst = writes[0]
            if dst.sem is None:
                dst.sem = self.es.enter_context(self.nc.semaphore("d%d_%s" % (len(self.dbufs), dst.name)))
                self.dbufs.append(dst)
            dst.dcount += 16
            mytok = (("d", id(dst)), dst.dcount)
            rec["dma"] = dst
        else:
            mytok = (("e", eng), idx)
        self.ops[eng].append(rec)
        for tok in waits:
            if tok[0][0] == "e":
                self.ops[tok[0][1]][tok[1]]["inc"] = True
        for b in reads:
            b.readers = [t for t in b.readers if t[0] != mytok[0]] + [mytok]
        for b in writes:
            b.last_w = mytok
            b.readers = []
        return mytok

    def barrier(self):
        bars = []
        for e in ENGS:
            b = Buf("bar_" + e)
            rd = list(self.dbufs) if e in ("sync", "gpsimd") else []
            self.op(e, lambda en: en.nop(), reads=rd, writes=[b])
            bars.append(b)
        for e in ENGS:
            self.op(e, lambda en: en.nop(), reads=bars)

    def emit(self):
        nc = self.nc
        sigval = getattr(self, "sigval", {})
        self.sigval = sigval
        for e in ENGS:
            for i in range(self.nemit[e], len(self.ops[e])):
                r = self.ops[e][i]
                if r["inc"] and r["dma"] is None:
                    self.ecount[e] += 1
                    sigval[(e, i)] = self.ecount[e]
        dsem = {id(b): b.sem for b in self.dbufs}
        with nc.Block() as block:
            for e in ENGS:
                lo, hi = self.nemit[e], len(self.ops[e])
                if hi == lo:
                    continue

                def run(en, e=e, lo=lo, hi=hi):
                    for i in range(lo, hi):
                        r = self.ops[e][i]
                        for key, val in r["waits"]:
                            if key[0] == "e":
                                en.wait_ge(self.esem[key[1]], sigval[(key[1], val)])
                            else:
                                en.wait_ge(dsem[key[1]], val)
                        strict = (e in STRICT_ENGS) or (e == "tensor" and PE_PARTIAL_CHAIN and
                                                        (r["chain"] or self.prev_chained))
                        if e == "tensor" and r["inc"] and r["dma"] is None:
                            self.prev_chained = r["chain"]
                        if strict and r["inc"] and r["dma"] is None and sigval[(e, i)] > 1:
                            en.wait_ge(self.esem[e], sigval[(e, i)] - 1)
                        ins = r["fn"](en)
                        if r["dma"] is not None:
                            ins.then_inc(r["dma"].sem, 16)
                        elif r["inc"]:
                            ins.then_inc(self.esem[e], 1)

                getattr(block, e)(run)
                self.nemit[e] = hi


def _col(v):
    v = np.asarray(v, np.float32).reshape(-1, 128)
    return np.ascontiguousarray(v.T)


VEC_LAYOUT = {}
_off = 0
for _n, _w in [("cc", 16), ("ada_b0", 48), ("ada_b1", 48), ("nmw0", 8), ("nmw1", 8), ("nfw0", 8), ("nfw1", 8),
               ("bu", 24), ("lng", 24), ("lb0", 16), ("lb1", 16), ("hnw", 8), ("fnw", 8)]:
    VEC_LAYOUT[_n] = (_off, _w)
    _off += _w
NV = _off


def pos_table(n):
    half = D // 2

    def sincos(pos, dim):
        h = dim // 2
        omega = (1.0 / (10000.0 ** (np.arange(h, dtype=np.float32) / np.float32(h)))).astype(np.float32)
        ang = pos.astype(np.float32)[:, None] * omega[None, :]
        return np.concatenate([np.sin(ang), np.cos(ang)], axis=-1).astype(np.float32)

    rows = n // GRID_W
    rc = sincos(np.arange(rows), half)
    cc = sincos(np.arange(GRID_W), half)
    code = np.concatenate([np.broadcast_to(rc[:, None, :], (rows, GRID_W, half)),
                           np.broadcast_to(cc[None, :, :], (rows, GRID_W, half))], axis=-1)
    return np.ascontiguousarray(code.reshape(rows * GRID_W, D).T.astype(np.float32))


def build(S=SEQ, stop_after=None, dbg_out=None):
    nc = bass.Bass("TRN2", target_bir_lowering=False)
    es = contextlib.ExitStack()
    P = Prog(nc, es)

    def din(name, shape, dt=F32):
        return nc.dram_tensor(name, list(shape), dt, kind="ExternalInput").ap()

    def dscr(name, shape, dt):
        kind = "ExternalOutput" if (dbg_out and name in dbg_out) else "Internal"
        return nc.dram_tensor(name, list(shape), dt, kind=kind).ap()

    xT = din("xT", [D, S])
    ctxT = din("ctxT", [D, CTX])
    posT = din("posT", [D, S])
    vecs = din("vecs", [128, NV])
    bvrow = din("bvrow", [1, GMH])
    lnbrow = din("lnbrow", [1, GMH])
    bsrow = din("bsrow", [1, 1024])
    wsT_d = din("wsT", [128, 1024])
    ada_w = din("ada_w", [2, D, 6 * D])
    gm_w_in = din("gm_w_in", [D, 2 * GMH])
    gm_w_out = din("gm_w_out", [GMH, D])
    hg_w_in = din("hg_w_in", [D, 5 * D])
    hg_w_out = din("hg_w_out", [D, D])
    ffn_w_in = din("ffn_w_in", [2, D, 2 * DFF])
    ffn_w_out = din("ffn_w_out", [2, DFF, D])
    outT = nc.dram_tensor("outT", [D, S], F32, kind="ExternalOutput").ap()

    V1 = dscr("V1", [GMH, S + CTX], BF16)
    XA = dscr("XA", [D, S + CTX], F32)
    XB = dscr("XB", [D, S + CTX], F32)
    XC = dscr("XC", [D, S], F32)
    OF = dscr("OF", [D, S], F32)
    QDB = dscr("QDB", [D, S], BF16)
    KXB = dscr("KXB", [D, S], BF16)
    SG = dscr("SG", [D, S], BF16)
    VT = dscr("VT", [S, D], BF16)
    bV1, bXA, bXB, bXC, bOF, bQDB, bKXB, bSG, bVT, bOUT = [P.buf(n) for n in
                                                            ["V1", "XA", "XB", "XC", "OF", "QDB", "KXB", "SG", "VT", "OUT"]]
    bIN = P.buf("inputs")

    def fm(ap3, c0, c1):
        return ap3.rearrange("(k p) t -> p k t", p=128)[:, :, c0:c1]

    ARENA_BYTES = 184 * 1024
    AR = es.enter_context(nc.sbuf_tensor("AR", [128, ARENA_BYTES // 2], BF16))
    aoff = [0]

    class _Phase:
        def close(self):
            pass

    def new_phase():
        aoff[0] = 0
        return _Phase()

    def sb(name, shape, dt, stack=None):
        if stack is None:
            return es.enter_context(nc.sbuf_tensor(name, list(shape), dt))
        esz = 4 if dt == F32 else 2
        nel = 1
        for d_ in shape[1:]:
            nel *= d_
        nb = (nel * esz + 63) // 64 * 64
        assert aoff[0] + nb <= ARENA_BYTES, ("arena overflow", name, aoff[0], nb)
        v = AR[:, aoff[0] // 2:(aoff[0] + nb) // 2]
        aoff[0] += nb
        if dt == F32:
            v = v.bitcast(F32)
        v = v[:, 0:nel]
        if len(shape) == 3:
            v = v.rearrange("p (a b) -> p a b", a=shape[1])
        if shape[0] < 128:
            v = v[0:shape[0]]
        return v

    wcnt = [0]

    def wview(e0, kc, n):
        wcnt[0] += 1
        ap = sb("w%d" % wcnt[0], [128, kc, n], BF16, True)
        return ap, [P.buf("w%d" % wcnt[0])]

    Vv = sb("Vv", [128, NV], F32)
    bVv = P.buf("Vv")
    MV = sb("MV", [128, 2 * 2 * 6 * 8], F32)
    bMV = P.buf("MV")
    OML = sb("OML", [128, 32], F32)
    bOML = P.buf("OML")
    ones_bf = sb("ones_bf", [128, 128], BF16)
    ident_bf = sb("ident_bf", [128, 128], BF16)
    mask01 = sb("mask01", [128, 256], BF16)
    maskF = sb("maskF", [64, 512], BF16)
    maskB = sb("maskB", [64, 512], BF16)
    bCONST = P.buf("const")
    NCHL = S // 64
    NCH = NCHL + CTX // 64
    Df = sb("Df", [128, 8, NCH + 1], F32)
    Db = sb("Db", [128, 8, NCH + 1], F32)
    bDf = P.buf("Df")
    bDb = P.buf("Db")
    Yf = sb("Yf", [128, 8, 128], F32)
    Yb = sb("Yb", [128, 8, 128], F32)
    bYf = [P.buf("Yf%d" % h) for h in range(8)]
    bYb = [P.buf("Yb%d" % h) for h in range(8)]

    psum = [es.enter_context(nc.psum_tensor("ps%d" % i, [128, 512], F32)) for i in range(8)]
    bps = [P.buf("ps%d" % i) for i in range(8)]
    psc = [0]

    def ps_next():
        i = psc[0] % 8
        psc[0] += 1
        return psum[i], bps[i]

    def mvcol(l, who, kind, k):
        c = ((l * 2 + who) * 6 + kind) * 8 + k
        return MV[:, c:c + 1]

    A_M, S_M, G_M, A_F, S_F, G_F = range(6)

    def vcol(name, k=0, n=1):
        o, w = VEC_LAYOUT[name]
        return Vv[:, o + k:o + k + n]

    def wload(dst_ap, dst_bufs, src2d, kc, n):
        step = 2048
        for k in range(kc):
            for c0 in range(0, n, step):
                c1 = min(n, c0 + step)
                P.op("gpsimd", lambda en, k=k, c0=c0, c1=c1: en.dma_start(
                    out=dst_ap[:, k, c0:c1], in_=src2d[k * 128:(k + 1) * 128, c0:c1]),
                    reads=[bIN], writes=dst_bufs, dma=True, nowaw=True)

    ph = new_phase()
    ident_f = sb("ident_f", [128, 128], F32, ph)
    P.op("sync", lambda en: en.dma_start(out=Vv[:], in_=vecs[:, :]), reads=[bIN], writes=[bVv], dma=True)

    for t_, b_, v_ in [(ones_bf, bCONST, 1.0), (mask01, bCONST, 1.0), (Df, bDf, 1.0), (Db, bDb, 1.0)]:
        P.op("vector", lambda en, t_=t_, v_=v_: en.memset(t_[:], v_), writes=[b_])
    P.op("vector", lambda en: en.memset(mask01[:, 0:256:64], 0.0), writes=[bCONST])
    P.op("vector", lambda en: en.memset(Yf[:], 0.0), writes=bYf)
    P.op("vector", lambda en: en.memset(Yb[:], 0.0), writes=bYb)
    bMK = P.buf("masks")
    P.op("gpsimd", lambda en: en.memset(ident_f[:], 1.0), writes=[bMK])
    P.op("gpsimd", lambda en: en.affine_select(out=ident_f[:], in_=ident_f[:], pattern=[[-1, 128]],
                                               compare_op=ALU.is_equal, fill=0.0, base=0, channel_multiplier=1),
         writes=[bMK])
    P.op("gpsimd", lambda en: en.memset(maskF[:], 1.0), writes=[bMK])
    P.op("gpsimd", lambda en: en.memset(maskB[:], 1.0), writes=[bMK])
    mf_ = maskF[:].rearrange("p (h t) -> p h t", h=8)
    mb_ = maskB[:].rearrange("p (h t) -> p h t", h=8)
    P.op("gpsimd", lambda en: en.affine_select(out=mf_, in_=mf_, pattern=[[0, 8], [1, 64]], compare_op=ALU.is_ge,
                                               fill=0.0, base=0, channel_multiplier=-1), writes=[bMK])
    P.op("gpsimd", lambda en: en.affine_select(out=mb_, in_=mb_, pattern=[[0, 8], [-1, 64]], compare_op=ALU.is_ge,
                                               fill=0.0, base=0, channel_multiplier=1), writes=[bMK])
    P.op("vector", lambda en: en.tensor_copy(out=ident_bf[:], in_=ident_f[:]), reads=[bMK], writes=[bCONST])
    s_bf = sb("s_bf", [128, 16], BF16, ph)
    bs_bf = P.buf("s_bf")
    P.op("scalar", lambda en: en.activation(out=s_bf[:], in_=vcol("cc", 0, 16), func=AF.Silu),
         reads=[bVv], writes=[bs_bf])
    lbd = sb("lbd", [128, 16], F32, ph)
    blbd = P.buf("lbd")
    P.op("vector", lambda en: en.tensor_tensor(out=lbd[:], in0=vcol("lb0", 0, 16), in1=vcol("lb1", 0, 16),
                                               op=ALU.subtract), reads=[bVv], writes=[blbd])
    P.op("scalar", lambda en: en.activation(out=OML[:, 0:16], in_=lbd[:], func=AF.Sigmoid),
         reads=[blbd], writes=[bOML])
    P.op("vector", lambda en: en.tensor_scalar(out=OML[:, 16:32], in0=OML[:, 0:16], scalar1=-1.0, scalar2=None,
                                               op0=ALU.mult), reads=[bOML], writes=[bOML])
    modt = sb("modt", [128, 96], F32, ph)
    bmodt = P.buf("modt")
    WA, bWA = wview(0, KC, 6 * D)
    for l in range(2):
        wload(WA, bWA, ada_w[l], KC, 6 * D)
        pst, bpst = ps_next()

        def _mod(en, WA=WA, pst=pst):
            for dc in range(48):
                for k in range(KC):
                    ins = en.matmul(pst[:, dc * 2:dc * 2 + 2], WA[:, k, dc * 128:(dc + 1) * 128],
                                    s_bf[:, 2 * k:2 * k + 2], start=(k == 0), stop=(k == KC - 1))
            return ins

        P.op("tensor", _mod, reads=bWA + [bs_bf], writes=[bpst])
        for who in range(2):
            P.op("vector", lambda en, pst=pst, who=who, l=l: en.tensor_tensor(
                out=modt[:, who * 48:(who + 1) * 48], in0=pst[:, who:96:2], in1=vcol("ada_b%d" % l, 0, 48), op=ALU.add),
                reads=[bpst, bVv], writes=[bmodt])

            def _mv(en, who=who, l=l):
                m = modt[:, who * 48:(who + 1) * 48]
                base = ((l * 2 + who) * 6) * 8
                en.scalar_tensor_tensor(out=MV[:, base + A_M * 8:base + A_M * 8 + 8], in0=m[:, 8:16], scalar=1.0,
                                        in1=vcol("nmw%d" % l, 0, 8), op0=ALU.add, op1=ALU.mult)
                en.tensor_copy(out=MV[:, base + S_M * 8:base + S_M * 8 + 8], in_=m[:, 0:8])
                en.tensor_copy(out=MV[:, base + G_M * 8:base + G_M * 8 + 8], in_=m[:, 16:24])
                en.scalar_tensor_tensor(out=MV[:, base + A_F * 8:base + A_F * 8 + 8], in0=m[:, 32:40], scalar=1.0,
                                        in1=vcol("nfw%d" % l, 0, 8), op0=ALU.add, op1=ALU.mult)
                en.tensor_copy(out=MV[:, base + S_F * 8:base + S_F * 8 + 8], in_=m[:, 24:32])
                return en.tensor_copy(out=MV[:, base + G_F * 8:base + G_F * 8 + 8], in_=m[:, 40:48])

            P.op("vector", _mv, reads=[bmodt, bVv], writes=[bMV])
    P.barrier()
    P.emit()
    ph.close()

    def tiles(T, with_ctx=True, lat=True):
        out = []
        if with_ctx:
            out.append(dict(ctx=True, t0=0, n=CTX, who=1))
        if lat:
            for t0 in range(0, S, T):
                out.append(dict(ctx=False, t0=t0, n=min(T, S - t0), who=0))
        return out

    class Ring:
        def __init__(self, name, shape, dt, n, stack):
            self.t = [sb("%s%d" % (name, i), shape, dt, stack) for i in range(n)]
            self.b = [P.buf("%s%d" % (name, i)) for i in range(n)]
            self.i = 0

        def next(self):
            j = self.i % len(self.t)
            self.i += 1
            return self.t[j], self.b[j]

    def norm_stage(x_t, bx, n, l, who, kinds, out_t, bout, sq_ring, rst_ring, tmp_ring, sq_eng="gpsimd",
                   custom_a=None, out_f32=False, lnexp=False, defer=None, defer1=None):
        sq_t, bsq = sq_ring.next()

        def _sq():
            P.op(sq_eng, lambda en: en.tensor_tensor(out=sq_t[:, :, 0:n], in0=x_t[:, :, 0:n], in1=x_t[:, :, 0:n],
                                                     op=ALU.mult), reads=[bx], writes=[bsq])

        if defer1 is not None:
            defer1.append(_sq)
        else:
            _sq()
        if defer is not None:
            defer.append(lambda: norm_post(x_t, bx, n, l, who, kinds, out_t, bout, sq_t, bsq, rst_ring, tmp_ring,
                                           custom_a, out_f32, lnexp))
        else:
            norm_post(x_t, bx, n, l, who, kinds, out_t, bout, sq_t, bsq, rst_ring, tmp_ring, custom_a, out_f32, lnexp)

    def norm_post(x_t, bx, n, l, who, kinds, out_t, bout, sq_t, bsq, rst_ring, tmp_ring, custom_a, out_f32, lnexp):
        pst, bpst = ps_next()

        def _ss(en):
            for k in range(KC):
                ins = en.matmul(pst[:, 0:n], ones_bf[:], sq_t[:, k, 0:n], start=(k == 0), stop=(k == KC - 1))
            return ins

        P.op("tensor", _ss, reads=[bsq, bCONST], writes=[bpst], chain=False)
        rst, brst = rst_ring.next()
        if lnexp:
            P.op("scalar", lambda en: en.activation(out=rst[:, 0:n], in_=pst[:, 0:n], func=AF.Ln, bias=EPS,
                                                    scale=1.0 / D), reads=[bpst], writes=[brst])
            P.op("scalar", lambda en: en.activation(out=rst[:, 0:n], in_=rst[:, 0:n], func=AF.Exp, scale=-0.5),
                 reads=[brst], writes=[brst])
        else:
            P.op("scalar", lambda en: en.activation(out=rst[:, 0:n], in_=pst[:, 0:n], func=AF.Sqrt, bias=EPS,
                                                    scale=1.0 / D), reads=[bpst], writes=[brst])
            P.op("vector", lambda en: en.reciprocal(out=rst[:, 0:n], in_=rst[:, 0:n]), reads=[brst], writes=[brst])
        for k in range(KC):
            a_ap = custom_a(k) if custom_a else mvcol(l, who, kinds[0], k)
            if out_f32:
                P.op("vector", lambda en, k=k, a_ap=a_ap: en.scalar_tensor_tensor(
                    out=out_t[:, k, 0:n], in0=x_t[:, k, 0:n], scalar=a_ap, in1=rst[:, 0:n], op0=ALU.mult,
                    op1=ALU.mult), reads=[bx, brst, bMV, bVv], writes=[bout])
                continue
            tmp, btmp = tmp_ring.next()
            P.op("vector", lambda en, k=k, a_ap=a_ap, tmp=tmp: en.scalar_tensor_tensor(
                out=tmp[:, 0:n], in0=x_t[:, k, 0:n], scalar=a_ap, in1=rst[:, 0:n], op0=ALU.mult, op1=ALU.mult),
                reads=[bx, brst, bMV], writes=[btmp])
            s_ap = mvcol(l, who, kinds[1], k)
            P.op("vector", lambda en, k=k, s_ap=s_ap, tmp=tmp: en.tensor_scalar(
                out=out_t[:, k, 0:n], in0=tmp[:, 0:n], scalar1=s_ap, scalar2=None, op0=ALU.add),
                reads=[btmp, bMV], writes=[bout])

    def src_cols(tl, lat_ap, ctx_ap_or_none, S_off=True):
        if tl["ctx"]:
            return (S, S + CTX)
        return (tl["t0"], tl["t0"] + tl["n"])

    def std_mm(W, bW, kc, col0, rhs_fn, n, reads):
        pst, bpst = ps_next()

        def _mm(en):
            for k in range(kc):
                ins = en.matmul(pst[:, 0:n], W[:, k, col0:col0 + 128], rhs_fn(k), start=(k == 0), stop=(k == kc - 1))
            return ins

        P.op("tensor", _mm, reads=bW + reads, writes=[bpst], chain=False)
        return pst, bpst

    def ffn_phase(l, src, bsrc, src_has_ctx, dst, bdst, final):
        T = 256
        ph = new_phase()
        W1, bW1 = wview(0, KC, 2 * DFF)
        W2, bW2 = wview(KC * 2 * DFF, NFC, D)
        wload(W1, bW1, ffn_w_in[l], KC, 2 * DFF)
        wload(W2, bW2, ffn_w_out[l], NFC, D)
        xr = Ring("fx", [128, KC, T], F32, 2, ph)
        hr = Ring("fh", [128, KC, T], BF16, 2, ph)
        sqr = Ring("fsq", [128, KC, T], BF16, 1, ph)
        rstr = Ring("frst", [128, T], F32, 2, ph)
        tmpr = Ring("ftmp", [128, T], F32, 2, ph)
        mr = Ring("fm", [128, NFC, T], BF16, 1, ph)
        sar = Ring("fsa", [128, T], F32, 3, ph)
        tls = tiles(T, with_ctx=src_has_ctx)
        st = {}

        def stageA(i, defer=None, defer1=None):
            tl = tls[i]
            n = tl["n"]
            x_t, bx = xr.next()
            h_t, bh = hr.next()
            c0, c1 = src_cols(tl, None, None)
            P.op("sync", lambda en: en.dma_start(out=x_t[:, :, 0:n], in_=fm(src, c0, c1)), reads=[bsrc], writes=[bx],
                 dma=True)
            norm_stage(x_t, bx, n, l, tl["who"], (A_F, S_F), h_t, bh, sqr, rstr, tmpr, defer=defer, defer1=defer1)
            st[i] = (x_t, bx, h_t, bh)

        def stageB(i):
            tl = tls[i]
            n = tl["n"]
            x_t, bx, h_t, bh = st.pop(i)
            m_t, bm = mr.next()
            pend = []
            pend1 = []
            if i + 1 < len(tls):
                stageA(i + 1, pend, pend1)
            for dc in range(NFC):
                pa, bpa = std_mm(W1, bW1, KC, dc * 128, lambda k: h_t[:, k, 0:n], n, [bh])
                pb, bpb = std_mm(W1, bW1, KC, DFF + dc * 128, lambda k: h_t[:, k, 0:n], n, [bh])
                sa, bsa = sar.next()
                P.op("scalar", lambda en, pa=pa, sa=sa: en.activation(out=sa[:, 0:n], in_=pa[:, 0:n], func=AF.Silu),
                     reads=[bpa], writes=[bsa])
                P.op("vector", lambda en, pb=pb, sa=sa, dc=dc: en.tensor_tensor(
                    out=m_t[:, dc, 0:n], in0=pb[:, 0:n], in1=sa[:, 0:n], op=ALU.mult), reads=[bpb, bsa], writes=[bm])
                if dc == 7:
                    for f_ in pend1:
                        f_()
                if dc == 15:
                    for f_ in pend:
                        f_()
            for oc in range(KC):
                py, bpy = std_mm(W2, bW2, NFC, oc * 128, lambda k: m_t[:, k, 0:n], n, [bm])
                g_ap = mvcol(l, tl["who"], G_F, oc)
                P.op("vector", lambda en, py=py, oc=oc, g_ap=g_ap: en.scalar_tensor_tensor(
                    out=x_t[:, oc, 0:n], in0=py[:, 0:n], scalar=g_ap, in1=x_t[:, oc, 0:n], op0=ALU.mult, op1=ALU.add),
                    reads=[bpy, bMV, bx], writes=[bx])
            if final:
                o_t, bo = x_t, bx
                norm_stage(x_t, bx, n, l, 0, None, o_t, bo, sqr, rstr, tmpr,
                           custom_a=lambda k: vcol("fnw", k), out_f32=True)
                P.op("sync", lambda en: en.dma_start(out=fm(dst, tl["t0"], tl["t0"] + n), in_=o_t[:, :, 0:n]),
                     reads=[bo], writes=[bdst], dma=True)
            else:
                c0, c1 = src_cols(tl, None, None)
                P.op("sync", lambda en: en.dma_start(out=fm(dst, c0, c1), in_=x_t[:, :, 0:n]), reads=[bx],
                     writes=[bdst], dma=True)

        stageA(0)
        for i in range(len(tls)):
            stageB(i)
        P.barrier()
        P.emit()
        ph.close()

    pos_eng = ["gpsimd"]

    def load_x0(tl, x_t, bx, pos_ring, defer1=None):
        n = tl["n"]
        if tl["ctx"]:
            P.op("sync", lambda en: en.dma_start(out=x_t[:, :, 0:n], in_=fm(ctxT, 0, CTX)), reads=[bIN], writes=[bx],
                 dma=True)
        else:
            p_t, bp = pos_ring.next()
            t0 = tl["t0"]
            P.op("sync", lambda en: en.dma_start(out=x_t[:, :, 0:n], in_=fm(xT, t0, t0 + n)), reads=[bIN],
                 writes=[bx], dma=True)
            P.op("sync", lambda en: en.dma_start(out=p_t[:, :, 0:n], in_=fm(posT, t0, t0 + n)), reads=[bIN],
                 writes=[bp], dma=True)
            def _pa():
                P.op(pos_eng[0], lambda en: en.tensor_tensor(out=x_t[:, :, 0:n], in0=x_t[:, :, 0:n],
                                                             in1=p_t[:, :, 0:n], op=ALU.add), reads=[bx, bp],
                     writes=[bx])

            if defer1 is not None:
                defer1.append(_pa)
            else:
                _pa()

    def phase1a():
        T = 256
        ph = new_phase()
        Wv, bWv = wview(0, KC, GMH)
        wload(Wv, bWv, gm_w_in[:, GMH:2 * GMH], KC, GMH)
        wsb = sb("wsb", [128, 1024], BF16, ph)
        bwsb = P.buf("wsb")
        P.op("gpsimd", lambda en: en.dma_start(out=wsb[:], in_=wsT_d[:, :]), reads=[bIN], writes=[bwsb], dma=True)
        bvb = sb("bvb", [128, GMH], BF16, ph)
        bbvb = P.buf("bvb")
        P.op("vector", lambda en: en.memset(bvb[:], 0.0), writes=[bbvb])
        P.op("gpsimd", lambda en: en.dma_start(out=bvb[0:1, :], in_=bvrow[:, :], max_dma_last_dim=4096), reads=[bIN],
             writes=[bbvb], dma=True)
        Ct = sb("Ct", [128, 24, 128], F32, ph)
        bCt = P.buf("Ct")
        amark = aoff[0]
        Rt = sb("Rt", [2, 1024], F32, ph)
        Lt = sb("Lt", [2, GMH], F32, ph)
        bRt = P.buf("Rt")
        bLt = P.buf("Lt")
        ws32 = sb("ws32", [128, 1024], F32, ph)
        bws32 = P.buf("ws32")
        P.op("sync", lambda en: en.dma_start(out=ws32[:], in_=wsT_d[:, :]), reads=[bIN], writes=[bws32], dma=True)
        ones32 = sb("ones32", [128, 2], F32, ph)
        bo32 = P.buf("ones32")
        P.op("vector", lambda en: en.memset(ones32[:], 1.0), writes=[bo32])
        P.op("vector", lambda en: en.memset(Lt[:], 1.0), writes=[bLt])
        P.op("sync", lambda en: en.dma_start(out=Lt[0:1, :], in_=lnbrow[:, :]), reads=[bIN], writes=[bLt], dma=True)
        P.op("sync", lambda en: en.dma_start(out=Rt[1:2, :], in_=bsrow[:, :]), reads=[bIN], writes=[bRt], dma=True)
        for hb in range(2):
            pst, bpst = ps_next()
            P.op("tensor", lambda en, pst=pst, hb=hb: en.matmul(pst[0:1, :], ones32[:, 0:1],
                                                               ws32[:, hb * 512:(hb + 1) * 512], start=True, stop=True),
                 reads=[bws32, bo32], writes=[bpst])
            P.op("vector", lambda en, pst=pst, hb=hb: en.tensor_copy(out=Rt[0:1, hb * 512:(hb + 1) * 512],
                                                                     in_=pst[0:1, :]), reads=[bpst], writes=[bRt])
        for dc in range(24):
            g = dc // 3
            pst, bpst = ps_next()
            P.op("tensor", lambda en, pst=pst, dc=dc, g=g: en.matmul(
                pst[:, 0:128], Lt[0:2, dc * 128:(dc + 1) * 128], Rt[0:2, g * 128:(g + 1) * 128], start=True, stop=True),
                reads=[bLt, bRt], writes=[bpst])
            P.op("vector", lambda en, pst=pst, dc=dc: en.tensor_copy(out=Ct[:, dc, :], in_=pst[:, 0:128]),
                 reads=[bpst], writes=[bCt])

        P.barrier()
        aoff[0] = amark
        xr = Ring("ax", [128, KC, T], F32, 2, ph)
        posr = Ring("apos", [128, KC, T], F32, 1, ph)
        hr = Ring("ah", [128, KC, T], BF16, 2, ph)
        sqr = Ring("asq", [128, KC, T], BF16, 1, ph)
        rstr = Ring("arst", [128, T], F32, 2, ph)
        tmpr = Ring("atmp", [128, T], F32, 3, ph)
        vgr = Ring("avg", [128, GMH], F32, 2, ph)
        vnr = Ring("avn", [128, GMH], BF16, 2, ph)
        str_ = Ring("ast", [128, 6, 6], F32, 2, ph)
        mvr = Ring("amv", [128, 8], F32, 2, ph)
        vpr = Ring("avp", [128, 24, T], BF16, 2, ph)
        tls = tiles(T)
        st = {}

        pending_sp = []

        def stageA(i, defer=None):
            tl = tls[i]
            x_t, bx = xr.next()
            h_t, bh = hr.next()
            load_x0(tl, x_t, bx, posr)
            norm_stage(x_t, bx, tl["n"], 0, tl["who"], (A_M, S_M), h_t, bh, sqr, rstr, tmpr, defer=defer)
            st[i] = (h_t, bh)

        def stageB(i):
            tl = tls[i]
            n = tl["n"]
            h_t, bh = st.pop(i)
            vp, bvp = vpr.next()
            pend = []
            if i + 1 < len(tls):
                stageA(i + 1, pend)
            nj = n // 128
            for j in range(nj):
                vg, bvg = vgr.next()
                stt, bstt = str_.next()
                for fb in range(6):
                    pst, bpst = ps_next()

                    def _mm(en, pst=pst, fb=fb, j=j):
                        for k in range(KC):
                            en.matmul(pst[:, :], h_t[:, k, j * 128:(j + 1) * 128], Wv[:, k, fb * 512:(fb + 1) * 512],
                                      start=(k == 0), stop=False)
                        return en.matmul(pst[:, :], ones_bf[:, :], bvb[:, fb * 512:(fb + 1) * 512], start=False,
                                         stop=True)

                    P.op("tensor", _mm, reads=bWv + [bh, bbvb, bCONST], writes=[bpst], chain=False)
                    P.op("scalar", lambda en, pst=pst, fb=fb, vg=vg: en.activation(
                        out=vg[:, fb * 512:(fb + 1) * 512], in_=pst[:, :], func=AF.Gelu), reads=[bpst], writes=[bvg])
                    P.op("vector", lambda en, fb=fb, vg=vg, stt=stt: en.bn_stats(
                        out=stt[:, fb, :], in_=vg[:, fb * 512:(fb + 1) * 512]), reads=[bvg], writes=[bstt])
                mv, bmv = mvr.next()

                P.op("vector", lambda en, stt=stt, mv=mv: en.bn_aggr(out=mv[:, 0:2],
                                                                     in_=stt[:].rearrange("p a b -> p (a b)")),
                     reads=[bstt], writes=[bmv])
                P.op("vector", lambda en, mv=mv: en.tensor_scalar(out=mv[:, 2:3], in0=mv[:, 1:2], scalar1=EPS,
                                                                   scalar2=None, op0=ALU.add), reads=[bmv], writes=[bmv])
                P.op("scalar", lambda en, mv=mv: en.activation(out=mv[:, 2:3], in_=mv[:, 2:3], func=AF.Sqrt),
                     reads=[bmv], writes=[bmv])
                P.op("vector", lambda en, mv=mv: en.reciprocal(out=mv[:, 3:4], in_=mv[:, 2:3]), reads=[bmv],
                     writes=[bmv])
                P.op("vector", lambda en, mv=mv: en.tensor_scalar(out=mv[:, 4:5], in0=mv[:, 0:1], scalar1=-1.0,
                                                                   scalar2=mv[:, 3:4], op0=ALU.mult, op1=ALU.mult),
                     reads=[bmv], writes=[bmv])
                vn, bvn = vnr.next()
                P.op("gpsimd", lambda en, vg=vg, vn=vn, mv=mv: en.tensor_scalar(
                    out=vn[:], in0=vg[:], scalar1=mv[:, 3:4], scalar2=mv[:, 4:5], op0=ALU.mult, op1=ALU.add),
                    reads=[bvg, bmv], writes=[bvn])
                while pending_sp:
                    pending_sp.pop(0)()
                pending_sp.append(lambda j=j, vn=vn, bvn=bvn, vp=vp, bvp=bvp, tl=tl, n=n, nj=nj: spatial(
                    j, vn, bvn, vp, bvp, tl, n, j == nj - 1))
                if j == 0:
                    for f_ in pend:
                        f_()

        def spatial(j, vn, bvn, vp, bvp, tl, n, last):
            if True:
                for q4 in range(6):
                    pst, bpst = ps_next()

                    def _sp(en, pst=pst, q4=q4, vn=vn):
                        for r in range(4):
                            dc = q4 * 4 + r
                            ins = en.matmul(pst[:, r * 128:(r + 1) * 128], vn[:, dc * 128:(dc + 1) * 128],
                                            wsb[:, (dc // 3) * 128:(dc // 3 + 1) * 128], start=True, stop=True)
                        return ins

                    P.op("tensor", _sp, reads=[bvn, bwsb], writes=[bpst], chain=False)
                    for r in range(4):
                        dc = q4 * 4 + r
                        P.op("vector", lambda en, pst=pst, r=r, dc=dc, j=j: en.scalar_tensor_tensor(
                            out=vp[:, dc, j * 128:(j + 1) * 128], in0=pst[:, r * 128:(r + 1) * 128],
                            scalar=vcol("lng", dc), in1=Ct[:, dc, :], op0=ALU.mult, op1=ALU.add),
                            reads=[bpst, bVv, bCt], writes=[bvp])
            if last:
                c0, c1 = src_cols(tl, None, None)
                P.op("sync", lambda en: en.dma_start(out=fm(V1, c0, c1), in_=vp[:, :, 0:n]), reads=[bvp], writes=[bV1],
                     dma=True)

        stageA(0)
        for i in range(len(tls)):
            stageB(i)
        while pending_sp:
            pending_sp.pop(0)()
        P.barrier()
        P.emit()
        ph.close()

    def phase1b():
        T = 256
        pos_eng[0] = "vector"
        ph = new_phase()
        Wu, bWu = wview(0, KC, GMH)
        Wo, bWo = wview(KC * GMH, 24, D)
        wload(Wu, bWu, gm_w_in[:, 0:GMH], KC, GMH)
        wload(Wo, bWo, gm_w_out, 24, D)
        xr = Ring("bx", [128, KC, T], F32, 3, ph)
        posr = Ring("bpos", [128, KC, T], F32, 1, ph)
        hr = Ring("bh", [128, KC, T], BF16, 2, ph)
        sqr = Ring("bsq", [128, KC, T], BF16, 1, ph)
        rstr = Ring("brst", [128, T], F32, 2, ph)
        tmpr = Ring("btmp", [128, T], F32, 3, ph)
        vpr = Ring("bvp", [128, 24, T], BF16, 2, ph)
        ur = Ring("bu", [128, T], BF16, 3, ph)
        tls = tiles(T)
        st = {}

        def stageA(i, defer=None, defer1=None):
            tl = tls[i]
            n = tl["n"]
            x_t, bx = xr.next()
            h_t, bh = hr.next()
            vp, bvp = vpr.next()
            load_x0(tl, x_t, bx, posr, defer1)
            c0, c1 = src_cols(tl, None, None)
            P.op("sync", lambda en: en.dma_start(out=vp[:, :, 0:n], in_=fm(V1, c0, c1)), reads=[bV1], writes=[bvp],
                 dma=True)
            norm_stage(x_t, bx, n, 0, tl["who"], (A_M, S_M), h_t, bh, sqr, rstr, tmpr, defer=defer, defer1=defer1)
            st[i] = (x_t, bx, h_t, bh, vp, bvp)

        def stageB(i):
            tl = tls[i]
            n = tl["n"]
            x_t, bx, h_t, bh, vp, bvp = st.pop(i)
            pend = []
            pend1 = []
            if i + 1 < len(tls):
                stageA(i + 1, pend, pend1)
            for dc in range(24):
                pu, bpu = std_mm(Wu, bWu, KC, dc * 128, lambda k: h_t[:, k, 0:n], n, [bh])
                u_t, bu = ur.next()
                P.op("scalar", lambda en, pu=pu, u_t=u_t, dc=dc: en.activation(
                    out=u_t[:, 0:n], in_=pu[:, 0:n], func=AF.Gelu, bias=vcol("bu", dc), scale=1.0),
                    reads=[bpu, bVv], writes=[bu])
                P.op("vector", lambda en, u_t=u_t, dc=dc: en.tensor_tensor(
                    out=vp[:, dc, 0:n], in0=vp[:, dc, 0:n], in1=u_t[:, 0:n], op=ALU.mult), reads=[bu, bvp],
                    writes=[bvp])
                if dc == 9:
                    for f_ in pend1:
                        f_()
                if dc == 17:
                    for f_ in pend:
                        f_()
            for oc in range(KC):
                py, bpy = std_mm(Wo, bWo, 24, oc * 128, lambda k: vp[:, k, 0:n], n, [bvp])
                g_ap = mvcol(0, tl["who"], G_M, oc)
                P.op("vector", lambda en, py=py, oc=oc, g_ap=g_ap: en.scalar_tensor_tensor(
                    out=x_t[:, oc, 0:n], in0=py[:, 0:n], scalar=g_ap, in1=x_t[:, oc, 0:n], op0=ALU.mult, op1=ALU.add),
                    reads=[bpy, bMV, bx], writes=[bx])
            c0, c1 = src_cols(tl, None, None)
            P.op("sync", lambda en: en.dma_start(out=fm(XA, c0, c1), in_=x_t[:, :, 0:n]), reads=[bx], writes=[bXA],
                 dma=True)

        stageA(0)
        for i in range(len(tls)):
            stageB(i)
        P.barrier()
        P.emit()
        ph.close()

    def chain_pre(kx_t, bkx, qd_t, bqd, vt_t, bvt, c, direction, rings, do_out):
        attr, kxtr, xbr, x32r = rings
        c64 = slice(c * 64, (c + 1) * 64)
        mask = maskF if direction == 0 else maskB
        attm = battm = None
        if do_out:
            pa, bpa = ps_next()

            def _att(en):
                for h in range(8):
                    ins = en.matmul(pa[0:64, h * 64:(h + 1) * 64], kx_t[:, h, c64], qd_t[:, h, c64], start=True,
                                    stop=True)
                return ins

            P.op("tensor", _att, reads=[bkx, bqd], writes=[bpa])
            attm, battm = attr.next()
            P.op("vector", lambda en: en.tensor_tensor(out=attm[:, :], in0=pa[0:64, :], in1=mask[:, :], op=ALU.mult),
                 reads=[bpa, bCONST], writes=[battm])
        pt, bpt = ps_next()
        ptb = pt[:].bitcast(BF16)

        def _tr(en):
            for h in range(8):
                ins = en.transpose(ptb[0:64, h * 128:(h + 1) * 128], kx_t[:, h, c64], ident_bf[:])
            return ins

        P.op("tensor", _tr, reads=[bkx, bCONST], writes=[bpt])
        kxt, bkxt = kxtr.next()
        P.op("scalar", lambda en: en.activation(out=kxt[:, :], in_=ptb[0:64, :], func=AF.Copy), reads=[bpt],
             writes=[bkxt])
        pks = []
        for hb in range(2):
            pk, bpk = ps_next()

            def _kv(en, hb=hb, pk=pk):
                for r in range(4):
                    h = hb * 4 + r
                    ins = en.matmul(pk[:, r * 128:(r + 1) * 128], kxt[:, h * 128:(h + 1) * 128],
                                    vt_t[0:64, c, h * 128:(h + 1) * 128], start=True, stop=True)
                return ins

            P.op("tensor", _kv, reads=[bkxt, bvt], writes=[bpk])
            pks.append((pk, bpk))
        return (attm, battm, pks)

    def chain_rec(pre, qd_t, bqd, vt_t, bvt, c, gc, Y, bY, Dt, bD, rings, do_out, of_t=None, bof=None, add_prev=False):
        attr, kxtr, xbr, x32r = rings
        attm, battm, pks = pre
        c64 = slice(c * 64, (c + 1) * 64)
        x32, bx32 = x32r.next()
        Dbc = Dt[:, :, gc:gc + 1].broadcast_to([128, 8, 128])
        P.op("vector", lambda en: en.tensor_tensor(out=x32[:, :, :], in0=Y[:, :, :], in1=Dbc, op=ALU.mult),
             reads=bY + [bD], writes=[bx32])
        for hb in range(2):
            pk, bpk = pks[hb]
            P.op("vector", lambda en, hb=hb, pk=pk: en.tensor_tensor(
                out=Y[:, hb * 4:(hb + 1) * 4, :], in0=x32[:, hb * 4:(hb + 1) * 4, :],
                in1=pk[:].rearrange("p (r v) -> p r v", r=4), op=ALU.add), reads=[bx32, bpk],
                writes=bY[hb * 4:(hb + 1) * 4])
        if do_out:
            xb, bxb = xbr.next()
            P.op("scalar", lambda en: en.activation(out=xb[:, :, :], in_=x32[:, :, :], func=AF.Copy), reads=[bx32],
                 writes=[bxb])
            po, bpo = ps_next()

            def _o(en):
                for h in range(8):
                    en.matmul(po[:, h * 64:(h + 1) * 64], vt_t[0:64, c, h * 128:(h + 1) * 128],
                              attm[:, h * 64:(h + 1) * 64], start=True, stop=False)
                    ins = en.matmul(po[:, h * 64:(h + 1) * 64], xb[:, h, :], qd_t[:, h, c64], start=False, stop=True)
                return ins

            P.op("tensor", _o, reads=[bvt, battm, bxb, bqd], writes=[bpo])
            pov = po[:].rearrange("p (h t) -> p h t", h=8)
            if add_prev:
                P.op("vector", lambda en: en.tensor_tensor(out=of_t[:, :, c64], in0=pov, in1=of_t[:, :, c64],
                                                           op=ALU.add), reads=[bpo, bof], writes=[bof])
            else:
                P.op("scalar", lambda en: en.activation(out=of_t[:, :, c64], in_=pov, func=AF.Copy), reads=[bpo],
                     writes=[bof])

    def run_chain(chunks, kx_t, bkx, qd_t, bqd, vt_t, bvt, direction, Y, bY, Dt, bD, rings, do_out, of_t=None,
                  bof=None, add_prev=False):
        pre = chain_pre(kx_t, bkx, qd_t, bqd, vt_t, bvt, chunks[0][0], direction, rings, do_out)
        for i_, (c, gc) in enumerate(chunks):
            nxt = None
            if i_ + 1 < len(chunks):
                nxt = chain_pre(kx_t, bkx, qd_t, bqd, vt_t, bvt, chunks[i_ + 1][0], direction, rings, do_out)
            chain_rec(pre, qd_t, bqd, vt_t, bvt, c, gc, Y, bY, Dt, bD, rings, do_out, of_t, bof, add_prev)
            pre = nxt

    def phase3():
        T = 256
        NC4 = T // 64
        ph = new_phase()
        W, bW = wview(0, KC, 5 * D)
        wload(W, bW, hg_w_in, KC, 5 * D)
        xr = Ring("cx", [128, KC, T], F32, 1, ph)
        hr = Ring("ch", [128, KC, T], BF16, 1, ph)
        sqr = Ring("csq", [128, KC, T], BF16, 1, ph)
        rstr = Ring("crst", [128, T], F32, 1, ph)
        tmpr = Ring("ctmp", [128, T], F32, 2, ph)
        qr = Ring("cq", [128, 8, T], BF16, 1, ph)
        sgr = Ring("csg", [128, 8, T], BF16, 1, ph)
        vtr = Ring("cvt", [64, NC4, D], BF16, 1, ph)
        qdr = [Ring("cqd%d" % d, [128, 8, T], BF16, 1, ph) for d in range(2)]
        kxr = [Ring("ckx%d" % d, [128, 8, T], BF16, 1, ph) for d in range(2)]
        ofr = Ring("cof", [128, 8, T], F32, 1, ph)
        tAr = Ring("ctA", [128, T], F32, 4, ph)
        tBr = Ring("ctB", [128, T + 1], F32, 4, ph)
        tCr = Ring("ctC", [128, T], F32, 4, ph)
        tDr = Ring("ctD", [128, T], F32, 4, ph)
        bscr = Ring("cbs", [128, T], F32, 4, ph)
        tsr = Ring("cts", [128, NC4], F32, 4, ph)
        rings = (Ring("catt", [64, 512], BF16, 2, ph), Ring("ckxt", [64, 1024], BF16, 2, ph),
                 Ring("cxb", [128, 8, 128], BF16, 2, ph), Ring("cx32", [128, 8, 128], F32, 2, ph))
        for t_, b_ in zip(tBr.t, tBr.b):
            P.op("vector", lambda en, t_=t_: en.memset(t_[:], 0.0), writes=[b_])
        tls = tiles(T)
        ctx_kxb = None
        def p3_tile(i, tl):
            n = tl["n"]
            isctx = tl["ctx"]
            gc0 = 0 if isctx else CTX // 64 + tl["t0"] // 64
            x_t, bx = xr.next()
            h_t, bh = hr.next()
            c0, c1 = src_cols(tl, None, None)
            P.op("sync", lambda en, x_t=x_t, c0=c0, c1=c1, n=n: en.dma_start(out=x_t[:, :, 0:n], in_=fm(XB, c0, c1)),
                 reads=[bXB], writes=[bx], dma=True)
            norm_stage(x_t, bx, n, 1, tl["who"], (A_M, S_M), h_t, bh, sqr, rstr, tmpr, lnexp=True)
            q_t = bq = sg_t = bsg = None
            if not isctx:
                q_t, bq = qr.next()
                sg_t, bsg = sgr.next()
                for h in range(8):
                    pq, bpq = std_mm(W, bW, KC, h * 128, lambda k: h_t[:, k, 0:n], n, [bh])
                    P.op("scalar", lambda en, pq=pq, h=h, q_t=q_t: en.activation(out=q_t[:, h, 0:n], in_=pq[:, 0:n],
                                                                                func=AF.Silu), reads=[bpq], writes=[bq])
                for h in range(8):
                    pq, bpq = std_mm(W, bW, KC, 4 * D + h * 128, lambda k: h_t[:, k, 0:n], n, [bh])
                    P.op("scalar", lambda en, pq=pq, h=h, sg_t=sg_t: en.activation(out=sg_t[:, h, 0:n], in_=pq[:, 0:n],
                                                                                  func=AF.Silu), reads=[bpq],
                         writes=[bsg])
                t0 = tl["t0"]
                P.op("sync", lambda en, sg_t=sg_t, t0=t0, n=n: en.dma_start(out=fm(SG, t0, t0 + n), in_=sg_t[:, :, 0:n]),
                     reads=[bsg], writes=[bSG], dma=True)
            vt_t, bvt = vtr.next()
            for c in range(n // 64):
                for nb in range(2):
                    pst, bpst = ps_next()

                    def _mi(en, pst=pst, c=c, nb=nb):
                        for k in range(KC):
                            ins = en.matmul(pst[0:64, :], h_t[:, k, c * 64:(c + 1) * 64],
                                            W[:, k, 3 * D + nb * 512:3 * D + (nb + 1) * 512], start=(k == 0),
                                            stop=(k == KC - 1))
                        return ins

                    P.op("tensor", _mi, reads=bW + [bh], writes=[bpst])
                    P.op("vector", lambda en, pst=pst, c=c, nb=nb, vt_t=vt_t: en.tensor_copy(
                        out=vt_t[:, c, nb * 512:(nb + 1) * 512], in_=pst[0:64, :]), reads=[bpst], writes=[bvt])
            if not isctx:
                t0 = tl["t0"]
                P.op("sync", lambda en, vt_t=vt_t, t0=t0, n=n: en.dma_start(
                    out=VT.rearrange("(c s) f -> s c f", s=64)[:, t0 // 64:(t0 + n) // 64, :], in_=vt_t[:, 0:n // 64, :]),
                    reads=[bvt], writes=[bVT], dma=True)
            qd = [None, None]
            kx = [None, None]
            for d in range(2):
                kx[d] = kxr[d].next()
                if not isctx:
                    qd[d] = qdr[d].next()
            nch = n // 64
            items = [(d, h) for d in range(2) for h in range(8)]
            WAVE = 4
            for w0 in range(0, 16, WAVE):
                grp = items[w0:w0 + WAVE]
                bufs = {}
                for (d, h) in grp:
                    pz, bpz = std_mm(W, bW, KC, (1 + d) * D + h * 128, lambda k: h_t[:, k, 0:n], n, [bh])
                    tA, btA = tAr.next()
                    tB, btB = tBr.next()
                    tC, btC = tCr.next()
                    tD, btD = tDr.next()
                    bs, bbs = bscr.next()
                    bufs[(d, h)] = (tA, btA, tB, btB, tC, btC, tD, btD, bs, bbs)
                    P.op("scalar", lambda en, pz=pz, tA=tA: en.activation(out=tA[:, 0:n], in_=pz[:, 0:n],
                                                                         func=AF.Exp), reads=[bpz], writes=[btA])
                for (d, h) in grp:
                    tA, btA, tB, btB, tC, btC, tD, btD, bs, bbs = bufs[(d, h)]
                    P.op("scalar", lambda en, tA=tA, tC=tC: en.activation(out=tC[:, 0:n], in_=tA[:, 0:n], func=AF.Ln,
                                                                         bias=1.0, scale=1.0), reads=[btA],
                         writes=[btC])
                    P.op("scalar", lambda en, tA=tA, tC=tC: en.activation(out=tA[:, 0:n], in_=tC[:, 0:n], func=AF.Exp,
                                                                         scale=-1.0), reads=[btC], writes=[btA])
                for (d, h) in grp:
                    tA, btA, tB, btB, tC, btC, tD, btD, bs, bbs = bufs[(d, h)]
                    P.op("scalar", lambda en, tA=tA, tB=tB, d=d, h=h: en.activation(
                        out=tB[:, 1:n + 1], in_=tA[:, 0:n], func=AF.Ln, bias=1.0,
                        scale=OML[:, 16 + d * 8 + h:17 + d * 8 + h]), reads=[btA, bOML], writes=[btB])
                for (d, h) in grp:
                    tA, btA, tB, btB, tC, btC, tD, btD, bs, bbs = bufs[(d, h)]
                    if d == 0:
                        P.op("vector", lambda en, tB=tB, bs=bs: en.tensor_tensor_scan(
                            out=bs[:, 0:n], data0=mask01[:, 0:n], data1=tB[:, 1:n + 1], initial=0.0, op0=ALU.mult,
                            op1=ALU.add), reads=[btB, bCONST], writes=[bbs])
                    else:
                        P.op("vector", lambda en, tB=tB, bs=bs: en.tensor_tensor_scan(
                            out=bs[:, 0:n], data0=tB[:, 0:n], data1=mask01[:, 0:n], initial=0.0, op0=ALU.add,
                            op1=ALU.mult), reads=[btB, bCONST], writes=[bbs])
                for (d, h) in grp:
                    tA, btA, tB, btB, tC, btC, tD, btD, bs, bbs = bufs[(d, h)]
                    sq_, sk_ = (1.0, -1.0) if d == 0 else (-1.0, 1.0)
                    P.op("scalar", lambda en, bs=bs, tC=tC, sq_=sq_: en.activation(out=tC[:, 0:n], in_=bs[:, 0:n],
                                                                                  func=AF.Exp, scale=sq_),
                         reads=[bbs], writes=[btC])
                    P.op("scalar", lambda en, bs=bs, tD=tD, sk_=sk_: en.activation(out=tD[:, 0:n], in_=bs[:, 0:n],
                                                                                  func=AF.Exp, scale=sk_),
                         reads=[bbs], writes=[btD])
                for (d, h) in grp:
                    tA, btA, tB, btB, tC, btC, tD, btD, bs, bbs = bufs[(d, h)]
                    if not isctx:
                        P.op("vector", lambda en, tC=tC, h=h, d=d: en.tensor_tensor(
                            out=qd[d][0][:, h, 0:n], in0=q_t[:, h, 0:n], in1=tC[:, 0:n], op=ALU.mult),
                            reads=[btC, bq], writes=[qd[d][1]])
                    P.op("vector", lambda en, tA=tA, tD=tD, h=h, d=d: en.scalar_tensor_tensor(
                        out=kx[d][0][:, h, 0:n], in0=tA[:, 0:n], scalar=OML[:, d * 8 + h:d * 8 + h + 1], in1=tD[:, 0:n],
                        op0=ALU.mult, op1=ALU.mult), reads=[btA, btD, bOML], writes=[kx[d][1]])
                    if d == 0:
                        P.op("vector", lambda en, tC=tC, h=h: en.tensor_copy(
                            out=Df[:, h, gc0 + 1:gc0 + 1 + nch], in_=tC[:, 63:n:64]), reads=[btC], writes=[bDf])
                    else:
                        ts_, bts = tsr.next()
                        P.op("vector", lambda en, bs=bs, tB=tB, ts_=ts_: en.tensor_tensor(
                            out=ts_[:, 0:nch], in0=bs[:, 63:n:64], in1=tB[:, 64:n + 1:64], op=ALU.add),
                            reads=[bbs, btB], writes=[bts])
                        P.op("scalar", lambda en, ts_=ts_, h=h: en.activation(
                            out=Db[:, h, gc0:gc0 + nch], in_=ts_[:, 0:nch], func=AF.Exp), reads=[bts], writes=[bDb])
            if isctx:
                run_chain([(c, gc0 + c) for c in range(n // 64)], kx[0][0], kx[0][1], None, None, vt_t, bvt, 0, Yf,
                          bYf, Df, bDf, rings, False)
                run_chain([(c, gc0 + c) for c in reversed(range(n // 64))], kx[1][0], kx[1][1], None, None,
                          vt_t, bvt, 1, Yb, bYb, Db, bDb, rings, False)
            else:
                of_t, bof = ofr.next()
                run_chain([(c, gc0 + c) for c in range(n // 64)], kx[0][0], kx[0][1], qd[0][0], qd[0][1], vt_t, bvt,
                          0, Yf, bYf, Df, bDf, rings, True, of_t, bof)
                t0 = tl["t0"]
                P.op("sync", lambda en, of_t=of_t, t0=t0, n=n: en.dma_start(out=fm(OF, t0, t0 + n), in_=of_t[:, :, 0:n]),
                     reads=[bof], writes=[bOF], dma=True)
                P.op("sync", lambda en, q=qd[1][0], t0=t0, n=n: en.dma_start(out=fm(QDB, t0, t0 + n), in_=q[:, :, 0:n]),
                     reads=[qd[1][1]], writes=[bQDB], dma=True)
                P.op("sync", lambda en, q=kx[1][0], t0=t0, n=n: en.dma_start(out=fm(KXB, t0, t0 + n), in_=q[:, :, 0:n]),
                     reads=[kx[1][1]], writes=[bKXB], dma=True)

        for i, tl in enumerate(tls):
            p3_tile(i, tl)
        P.barrier()
        P.emit()
        ph.close()

    def phase4():
        T = 256
        NC4 = T // 64
        ph = new_phase()
        Wo, bWo = wview(0, KC, D)
        wload(Wo, bWo, hg_w_out, KC, D)
        xr = Ring("dx", [128, KC, T], F32, 2, ph)
        qdr = Ring("dqd", [128, 8, T], BF16, 2, ph)
        kxr = Ring("dkx", [128, 8, T], BF16, 2, ph)
        sgr = Ring("dsg", [128, 8, T], BF16, 2, ph)
        vtr = Ring("dvt", [64, NC4, D], BF16, 2, ph)
        ofr = Ring("dof", [128, 8, T], F32, 2, ph)
        sqr = Ring("dsq", [128, 8, T], BF16, 1, ph)
        rsr = Ring("drs", [128, 8, T], F32, 1, ph)
        tmpr = Ring("dtmp", [128, T], F32, 3, ph)
        mr = Ring("dm", [128, 8, T], BF16, 1, ph)
        rings = (Ring("datt", [64, 512], BF16, 2, ph), Ring("dkxt", [64, 1024], BF16, 2, ph),
                 Ring("dxb", [128, 8, 128], BF16, 2, ph), Ring("dx32", [128, 8, 128], F32, 2, ph))
        tls = list(reversed(tiles(T, with_ctx=False)))
        st = {}

        def stageA(i):
            tl = tls[i]
            n, t0 = tl["n"], tl["t0"]
            x_t, bx = xr.next()
            qd, bqd = qdr.next()
            kx, bkx = kxr.next()
            sg, bsg = sgr.next()
            vt, bvt = vtr.next()
            of, bof = ofr.next()
            P.op("sync", lambda en: en.dma_start(out=qd[:, :, 0:n], in_=fm(QDB, t0, t0 + n)), reads=[bQDB], writes=[bqd],
                 dma=True)
            P.op("sync", lambda en: en.dma_start(out=kx[:, :, 0:n], in_=fm(KXB, t0, t0 + n)), reads=[bKXB], writes=[bkx],
                 dma=True)
            P.op("sync", lambda en: en.dma_start(
                out=vt[:, 0:n // 64, :], in_=VT.rearrange("(c s) f -> s c f", s=64)[:, t0 // 64:(t0 + n) // 64, :]),
                reads=[bVT], writes=[bvt], dma=True)
            P.op("sync", lambda en: en.dma_start(out=of[:, :, 0:n], in_=fm(OF, t0, t0 + n)), reads=[bOF], writes=[bof],
                 dma=True)
            P.op("sync", lambda en: en.dma_start(out=sg[:, :, 0:n], in_=fm(SG, t0, t0 + n)), reads=[bSG], writes=[bsg],
                 dma=True)
            P.op("sync", lambda en: en.dma_start(out=x_t[:, :, 0:n], in_=fm(XB, t0, t0 + n)), reads=[bXB], writes=[bx],
                 dma=True)
            st[i] = (x_t, bx, qd, bqd, kx, bkx, sg, bsg, vt, bvt, of, bof)

        def stageB(i):
            tl = tls[i]
            n, t0 = tl["n"], tl["t0"]
            x_t, bx, qd, bqd, kx, bkx, sg, bsg, vt, bvt, of, bof = st.pop(i)
            run_chain([(c, CTX // 64 + t0 // 64 + c) for c in reversed(range(n // 64))], kx, bkx, qd,
                      bqd, vt, bvt, 1, Yb, bYb, Db, bDb, rings, True, of, bof, add_prev=True)
            if i + 1 < len(tls):
                stageA(i + 1)
            sq_t, bsq = sqr.next()
            P.op("gpsimd", lambda en: en.tensor_tensor(out=sq_t[:, :, 0:n], in0=of[:, :, 0:n], in1=of[:, :, 0:n],
                                                       op=ALU.mult), reads=[bof], writes=[bsq])
            rs, brs = rsr.next()
            for h in range(8):
                pst, bpst = ps_next()
                P.op("tensor", lambda en, pst=pst, h=h: en.matmul(pst[:, 0:n], ones_bf[:], sq_t[:, h, 0:n], start=True,
                                                                 stop=True), reads=[bsq, bCONST], writes=[bpst])
                P.op("scalar", lambda en, pst=pst, h=h: en.activation(out=rs[:, h, 0:n], in_=pst[:, 0:n], func=AF.Ln,
                                                                     bias=EPS, scale=1.0 / 128), reads=[bpst],
                     writes=[brs])
            P.op("scalar", lambda en: en.activation(out=rs[:, :, 0:n], in_=rs[:, :, 0:n], func=AF.Exp, scale=-0.5),
                 reads=[brs], writes=[brs])
            m_t, bm = mr.next()
            for h in range(8):
                tmp, btmp = tmpr.next()
                P.op("vector", lambda en, h=h, tmp=tmp: en.scalar_tensor_tensor(
                    out=tmp[:, 0:n], in0=of[:, h, 0:n], scalar=vcol("hnw", h), in1=rs[:, h, 0:n], op0=ALU.mult,
                    op1=ALU.mult), reads=[bof, brs, bVv], writes=[btmp])
                P.op("gpsimd", lambda en, h=h, tmp=tmp: en.tensor_tensor(
                    out=m_t[:, h, 0:n], in0=tmp[:, 0:n], in1=sg[:, h, 0:n], op=ALU.mult), reads=[btmp, bsg],
                    writes=[bm])
            for oc in range(KC):
                py, bpy = std_mm(Wo, bWo, KC, oc * 128, lambda k: m_t[:, k, 0:n], n, [bm])
                g_ap = mvcol(1, 0, G_M, oc)
                P.op("vector", lambda en, py=py, oc=oc, g_ap=g_ap: en.scalar_tensor_tensor(
                    out=x_t[:, oc, 0:n], in0=py[:, 0:n], scalar=g_ap, in1=x_t[:, oc, 0:n], op0=ALU.mult, op1=ALU.add),
                    reads=[bpy, bMV, bx], writes=[bx])
            P.op("sync", lambda en: en.dma_start(out=fm(XC, t0, t0 + n), in_=x_t[:, :, 0:n]), reads=[bx], writes=[bXC],
                 dma=True)

        stageA(0)
        for i in range(len(tls)):
            stageB(i)
        P.barrier()
        P.emit()
        ph.close()

    phases = [("p1a", phase1a), ("p1b", phase1b), ("p2", lambda: ffn_phase(0, XA, bXA, True, XB, bXB, False)),
              ("p3", phase3), ("p4", phase4), ("p5", lambda: ffn_phase(1, XC, bXC, False, outT, bOUT, True))]
    for name, f in phases:
        f()
        if stop_after == name:
            break
    P.barrier()
    P.emit()
    es.close()
    return nc


def make_in_maps(inputs, S):
    f = lambda a: np.ascontiguousarray(np.asarray(a, dtype=np.float32))
    x = f(inputs["x"])
    B = x.shape[0]
    pos = pos_table(S)
    gm_w_s = f(inputs["gm_w_s"])[0]
    wsT = np.ascontiguousarray(gm_w_s.transpose(2, 0, 1).reshape(128, 1024))
    shared = {
        "posT": pos,
        "bvrow": f(inputs["gm_b_in"])[0, GMH:].reshape(1, GMH).copy(),
        "lnbrow": f(inputs["gm_ln_b"])[0].reshape(1, GMH).copy(),
        "bsrow": f(inputs["gm_b_s"])[0].reshape(1, 1024).copy(),
        "wsT": wsT,
        "ada_w": f(inputs["ada_w"]),
        "gm_w_in": f(inputs["gm_w_in"])[0],
        "gm_w_out": f(inputs["gm_w_out"])[0],
        "hg_w_in": f(inputs["hg_w_in"])[0],
        "hg_w_out": f(inputs["hg_w_out"])[0],
        "ffn_w_in": f(inputs["ffn_w_in"]),
        "ffn_w_out": f(inputs["ffn_w_out"]),
    }
    hg_lb = f(inputs["hg_lb"])
    maps = []
    for b in range(B):
        vec = np.zeros((128, NV), np.float32)

        def put(name, arr):
            o, w = VEC_LAYOUT[name]
            assert arr.shape == (128, w), (name, arr.shape)
            vec[:, o:o + w] = arr

        cc = np.stack([_col(inputs["c"][b]), _col(inputs["c_ctx"])], axis=-1).reshape(128, 16)
        put("cc", cc)
        for l in range(2):
            put("ada_b%d" % l, _col(inputs["ada_b"][l]))
            put("nmw%d" % l, _col(inputs["norm_mix_w"][l]))
            put("nfw%d" % l, _col(inputs["norm_ffn_w"][l]))
            put("lb%d" % l, np.concatenate([_col(hg_lb[l, 0]), _col(hg_lb[l, 1])], axis=1))
        put("bu", _col(f(inputs["gm_b_in"])[0, :GMH]))
        put("lng", _col(inputs["gm_ln_g"][0]))
        put("hnw", _col(inputs["hg_norm_w"][0]))
        put("fnw", _col(inputs["final_norm_w"]))
        m = dict(shared)
        m["xT"] = np.ascontiguousarray(x[b, :S].T)
        m["ctxT"] = np.ascontiguousarray(f(inputs["ctx"])[b].T)
        m["vecs"] = vec
        maps.append(m)
    return maps


def kernel(**inputs):
    S = inputs["x"].shape[1]
    nc = build(S)
    maps = make_in_maps(inputs, S)
    res = run_bass_kernel_spmd(nc, maps, core_ids=list(range(len(maps))))
    out = np.stack([np.ascontiguousarray(r["outT"].T) for r in res.results], axis=0)
    return out.astype(np.float32)
```
